# Optimizing a Trainium2 kernel written in Bass

```python
import math
import jax
import jax.numpy as jnp
from jax import lax
import numpy as np

D_MODEL = 1024
BATCH = 32
SEQ = 256
DEPTH = 2
DEC_BATCH = 4
DEC_SEQ = 2048
PAST_LEN = 512

GRID_W = 64
MIX_DIM = D_MODEL
V_DIM = 128
QK_DIM = V_DIM // 2
N_ATT_HEADS = (MIX_DIM // 2) // V_DIM
CONV_DIM = MIX_DIM // 4
N_CONV_GROUPS = 4
FOURIER_DIM = MIX_DIM // 4
N_FOURIER_GROUPS = 4
FOURIER_GROUP_DIM = FOURIER_DIM // N_FOURIER_GROUPS
ATT_QK = N_ATT_HEADS * 2 * QK_DIM
ATT_V = N_ATT_HEADS * V_DIM
IN_DIM = 2 * ATT_QK + ATT_V + 3 * CONV_DIM + FOURIER_DIM
SPLITS = (ATT_QK, 2 * ATT_QK, 2 * ATT_QK + ATT_V,
          2 * ATT_QK + ATT_V + CONV_DIM,
          2 * ATT_QK + ATT_V + 2 * CONV_DIM,
          2 * ATT_QK + ATT_V + 3 * CONV_DIM)
D_FF = ((8 * D_MODEL // 3 + 127) // 128) * 128
CONV_WIDTH = 3
N_MOD = 9
ROT_AXIS_DIM = QK_DIM // 2
ROT_FREQS = ROT_AXIS_DIM // 2
ROPE_THETA = 10000.0
Q_BLOCK = 128
FFN_RESIDUAL_WEIGHT = 0.5
EPS = 1e-6

kernel_name = "hybrid_diffattn_conv_fourier_dit_step"


def _lambda_init(layer_idx):
    return 0.8 - 0.6 * math.exp(-0.3 * layer_idx)


def _rmsnorm(x, g):
    xf = x.astype(jnp.float32)
    y = xf * lax.rsqrt(jnp.mean(xf * xf, axis=-1, keepdims=True) + EPS)
    return (y * g.astype(jnp.float32)).astype(x.dtype)


def _modulation(cvec, w_mod_l, b_mod_l):
    m = jax.nn.silu(cvec) @ w_mod_l + b_mod_l
    return m.reshape(-1, 1, N_MOD, D_MODEL)


def _axial_angles(n):
    rows = n // GRID_W
    t = jnp.arange(rows * GRID_W)
    row = (t // GRID_W).astype(jnp.float32)
    col = (t % GRID_W).astype(jnp.float32)
    inv = 1.0 / (ROPE_THETA ** (jnp.arange(ROT_FREQS, dtype=jnp.float32) / ROT_FREQS))
    return row[:, None] * inv, col[:, None] * inv


def _rot(x, ang):
    cos = jnp.cos(ang)[None, :, None, None, :].astype(x.dtype)
    sin = jnp.sin(ang)[None, :, None, None, :].astype(x.dtype)
    x1, x2 = x[..., :ROT_FREQS], x[..., ROT_FREQS:]
    return jnp.concatenate([x1 * cos - x2 * sin, x2 * cos + x1 * sin], axis=-1)


def _rope_2d(x, ang_r, ang_c):
    return jnp.concatenate([_rot(x[..., :ROT_AXIS_DIM], ang_r),
                            _rot(x[..., ROT_AXIS_DIM:], ang_c)], axis=-1)


def _diff_attention(q, k, v, lam, lambda_init, subln_g):
    b, lq, h, _, dk = q.shape
    nb = lq // Q_BLOCK
    qb = q.reshape(b, nb, Q_BLOCK, h, 2, dk).transpose(1, 0, 2, 3, 4, 5)
    scale = dk ** -0.5

    def one_block(qblk):
        s = jnp.einsum('bqhmd,bkhmd->bhmqk', qblk, k).astype(jnp.float32) * scale
        p = jax.nn.softmax(s, axis=-1)
        a = p[:, :, 0] - lam * p[:, :, 1]
        o = jnp.einsum('bhqk,bkhe->bqhe', a.astype(v.dtype), v)
        return _rmsnorm(o, subln_g) * (1.0 - lambda_init)

    ob = lax.map(one_block, qb)
    return ob.transpose(1, 0, 2, 3, 4).reshape(b, lq, h * v.shape[-1])


def _short_conv(u, w):
    up = jnp.pad(u, ((0, 0), (1, 1), (0, 0)))
    return up[:, :-2] * w[0] + up[:, 1:-1] * w[1] + up[:, 2:] * w[2]


def _fourier_mix(f):
    b, n, _ = f.shape
    ff = f.astype(jnp.float32).reshape(b, n, N_FOURIER_GROUPS, FOURIER_GROUP_DIM)
    out = jnp.fft.fft2(ff, axes=(1, 3), norm="ortho").real
    return out.reshape(b, n, FOURIER_DIM).astype(f.dtype)


def _ffn_sub(x, shift, scale, gate, g_pre, g_post, w_gu, w_down):
    h = _rmsnorm(x, g_pre) * (1 + scale) + shift
    g, u = jnp.split(h @ w_gu, 2, axis=-1)
    y = (jax.nn.silu(g) * u) @ w_down
    return x + FFN_RESIDUAL_WEIGHT * gate * _rmsnorm(y, g_post)


def _mixer_sub(x, shift, scale, gate, g_pre, g_post, w_in_l, conv_w_l, w_out_l,
               lam, lambda_init, subln_g_l, ctx_k, ctx_v):
    b, n, _ = x.shape
    h = _rmsnorm(x, g_pre) * (1 + scale) + shift
    p = h @ w_in_l
    q, k, v, gb, gc, hc, f = jnp.split(p, SPLITS, axis=-1)
    q = q.reshape(b, n, N_ATT_HEADS, 2, QK_DIM)
    k = k.reshape(b, n, N_ATT_HEADS, 2, QK_DIM)
    v = v.reshape(b, n, N_ATT_HEADS, V_DIM)
    if ctx_k is None:
        keys, vals = k, v
    else:
        ang_r, ang_c = _axial_angles(n)
        q = _rope_2d(q, ang_r, ang_c)
        keys = jnp.concatenate([_rope_2d(k, ang_r, ang_c), ctx_k.astype(k.dtype)], axis=1)
        vals = jnp.concatenate([v, ctx_v.astype(v.dtype)], axis=1)
    att = _diff_attention(q, keys, vals, lam, lambda_init, subln_g_l)
    conv = gb * _short_conv(gc * hc, conv_w_l)
    four = _fourier_mix(f)
    y = jnp.concatenate([att, conv, four], axis=-1) @ w_out_l
    return x + gate * _rmsnorm(y, g_post), k, v


def _layer(x, l, cvec, w_mod, b_mod, norm_g, w_ffn_gu, w_ffn_down, w_in, w_out,
           conv_w, lam_qk, subln_g, ctx_k, ctx_v):
    mod = _modulation(cvec, w_mod[l], b_mod[l])
    g = norm_g[l]
    lambda_init = _lambda_init(l)
    lq = lam_qk[l].astype(jnp.float32)
    lam = jnp.exp(jnp.sum(lq[0] * lq[1])) - jnp.exp(jnp.sum(lq[2] * lq[3])) + lambda_init
    x = _ffn_sub(x, mod[:, :, 0], mod[:, :, 1], mod[:, :, 2], g[0], g[1],
                 w_ffn_gu[l, 0], w_ffn_down[l, 0])
    x, k, v = _mixer_sub(x, mod[:, :, 3], mod[:, :, 4], mod[:, :, 5], g[2], g[3],
                         w_in[l], conv_w[l], w_out[l], lam, lambda_init, subln_g[l],
                         ctx_k, ctx_v)
    x = _ffn_sub(x, mod[:, :, 6], mod[:, :, 7], mod[:, :, 8], g[4], g[5],
                 w_ffn_gu[l, 1], w_ffn_down[l, 1])
    return x, k, v


def setup_inputs(seed: int = 0) -> dict:
    key = jax.random.key(seed)
    ks = jax.random.split(key, 16)
    f32 = jnp.float32
    x_prompt = jax.random.normal(ks[0], (BATCH, SEQ, D_MODEL), f32)
    x_sample = jax.random.normal(ks[1], (DEC_BATCH, DEC_SEQ, D_MODEL), f32)
    cache_k = jax.random.normal(ks[2], (DEC_BATCH, DEPTH, PAST_LEN, N_ATT_HEADS, 2, QK_DIM), f32)
    cache_v = jax.random.normal(ks[3], (DEC_BATCH, DEPTH, PAST_LEN, N_ATT_HEADS, V_DIM), f32)
    c = jax.random.normal(ks[4], (DEC_BATCH, D_MODEL), f32)
    c_ctx = jax.random.normal(ks[5], (D_MODEL,), f32)
    w_mod = jax.random.normal(ks[6], (DEPTH, D_MODEL, N_MOD * D_MODEL), f32) * (0.5 * D_MODEL ** -0.5)
    b_mod = jax.random.normal(ks[7], (DEPTH, N_MOD * D_MODEL), f32) * 0.02
    norm_g = 1.0 + 0.02 * jax.random.normal(ks[8], (DEPTH, 6, D_MODEL), f32)
    w_ffn_gu = jax.random.normal(ks[9], (DEPTH, 2, D_MODEL, 2 * D_FF), f32) * D_MODEL ** -0.5
    w_ffn_down = jax.random.normal(ks[10], (DEPTH, 2, D_FF, D_MODEL), f32) * D_FF ** -0.5
    w_in = jax.random.normal(ks[11], (DEPTH, D_MODEL, IN_DIM), f32) * D_MODEL ** -0.5
    w_out = jax.random.normal(ks[12], (DEPTH, MIX_DIM, D_MODEL), f32) * MIX_DIM ** -0.5
    conv_w = jax.random.normal(ks[13], (DEPTH, CONV_WIDTH, CONV_DIM), f32) * CONV_WIDTH ** -0.5
    lam_qk = jax.random.normal(ks[14], (DEPTH, 4, QK_DIM), f32) * 0.1
    subln_g = 1.0 + 0.02 * jax.random.normal(ks[15], (DEPTH, V_DIM), f32)
    return {"x_prompt": x_prompt, "x_sample": x_sample, "cache_k": cache_k, "cache_v": cache_v,
            "c": c, "c_ctx": c_ctx, "w_mod": w_mod, "b_mod": b_mod, "norm_g": norm_g,
            "w_ffn_gu": w_ffn_gu, "w_ffn_down": w_ffn_down, "w_in": w_in, "w_out": w_out,
            "conv_w": conv_w, "lam_qk": lam_qk, "subln_g": subln_g}


def reference(x_prompt, x_sample, cache_k, cache_v, c, c_ctx, w_mod, b_mod, norm_g,
              w_ffn_gu, w_ffn_down, w_in, w_out, conv_w, lam_qk, subln_g):
    y_prompt = x_prompt
    ks, vs = [], []
    for l in range(DEPTH):
        y_prompt, k, v = _layer(y_prompt, l, c_ctx, w_mod, b_mod, norm_g, w_ffn_gu, w_ffn_down,
                                w_in, w_out, conv_w, lam_qk, subln_g, None, None)
        ks.append(k)
        vs.append(v)
    new_cache_k = jnp.stack(ks, axis=1)
    new_cache_v = jnp.stack(vs, axis=1)
    y_sample = x_sample
    for l in range(DEPTH):
        y_sample, _, _ = _layer(y_sample, l, c, w_mod, b_mod, norm_g, w_ffn_gu, w_ffn_down,
                                w_in, w_out, conv_w, lam_qk, subln_g,
                                cache_k[:, l], cache_v[:, l])
    return (y_prompt, y_sample, new_cache_k, new_cache_v)
```

```python
import math
from contextlib import ExitStack

import numpy as np
import ml_dtypes

import concourse.bass as bass
import concourse.mybir as mybir
from concourse.bass_utils import run_bass_kernel_spmd

F32 = mybir.dt.float32
BF16 = mybir.dt.bfloat16
AF = mybir.ActivationFunctionType
ALU = mybir.AluOpType

D = 1024
T = 2048
NCH = 8
DFF = 2816
NF = 22
DEPTH = 2
PAST = 512
NKT = 20
EPS = 1e-6
NEG = -30000.0
ENGS = ("pe", "act", "dve", "pool", "sp")


class Op:
    __slots__ = ("eng", "fn", "deps", "signal", "val", "sem", "is_dma", "slot")

    def __init__(self, eng, fn, is_dma=False, slot=None):
        self.eng = eng
        self.fn = fn
        self.deps = []
        self.signal = False
        self.val = None
        self.sem = None
        self.is_dma = is_dma
        self.slot = slot


class Prog:
    def __init__(self):
        self.ops = {e: [] for e in ENGS}
        self.last_w = {}
        self.readers = {}
        self.all = []
        self.final = []
        self.pending_barrier = {e: None for e in ENGS}
        self.since_barrier_dma = []
        self.last_op = {e: None for e in ENGS}

    def barrier(self):
        deps = [o for o in self.last_op.values() if o is not None] + list(self.since_barrier_dma)
        self.since_barrier_dma = []
        for e in ENGS:
            prev = self.pending_barrier[e] or []
            self.pending_barrier[e] = prev + deps

    def add(self, eng, fn, reads=(), writes=(), dma_slot=None, final=False):
        op = Op(eng, fn, is_dma=dma_slot is not None, slot=dma_slot)
        deps = set()
        for r in reads:
            w = self.last_w.get(r)
            if w is not None:
                deps.add(w)
        for w_ in writes:
            w = self.last_w.get(w_)
            if w is not None:
                deps.add(w)
            for rd in self.readers.get(w_, ()):
                deps.add(rd)
        for r in reads:
            self.readers.setdefault(r, []).append(op)
        for w_ in writes:
            self.last_w[w_] = op
            self.readers[w_] = []
        pb = self.pending_barrier[eng]
        if pb:
            deps.update(pb)
            self.pending_barrier[eng] = None
        deps.discard(op)
        for d in deps:
            if d.eng == "pe" and eng == "pe" and not d.is_dma and not op.is_dma:
                continue
            op.deps.append(d)
            d.signal = True
        if final:
            op.signal = True
            self.final.append(op)
        self.all.append(op)
        self.ops[eng].append(op)
        if op.is_dma:
            self.since_barrier_dma.append(op)
        else:
            self.last_op[eng] = op
        return op

    def emit(self, nc, ctx):
        esem = {e: ctx.enter_context(nc.semaphore("c_" + e)) for e in ENGS}
        slot_sem = {}
        slot_cnt = {}
        cnt = {e: 0 for e in ENGS}
        for op in self.all:
            if op.is_dma:
                if op.slot not in slot_sem:
                    slot_sem[op.slot] = ctx.enter_context(nc.semaphore("d_%d" % len(slot_sem)))
                    slot_cnt[op.slot] = 0
                slot_cnt[op.slot] += 16
                op.sem = slot_sem[op.slot]
                op.val = slot_cnt[op.slot]
            elif op.signal:
                cnt[op.eng] += 1
                op.sem = esem[op.eng]
                op.val = cnt[op.eng]
        final = self.final
        ops = self.ops

        def run(eng_name, eng):
            waited = {}
            for op in ops[eng_name]:
                need = {}
                for d in op.deps:
                    k = id(d.sem)
                    if waited.get(k, 0) >= d.val:
                        continue
                    if k not in need or need[k][1] < d.val:
                        need[k] = (d.sem, d.val)
                for k, (s, v) in need.items():
                    eng.wait_ge(s, v)
                    waited[k] = v
                ins = op.fn(eng)
                if op.is_dma:
                    ins.then_inc(op.sem, 16)
                elif op.signal:
                    ins.then_inc(op.sem, 1)
            if eng_name == "sp":
                for f in final:
                    k = id(f.sem)
                    if waited.get(k, 0) >= f.val:
                        continue
                    eng.wait_ge(f.sem, f.val)
                    waited[k] = f.val

        with nc.Block() as block:
            @block.tensor
            def _(e):
                run("pe", e)

            @block.scalar
            def _(e):
                run("act", e)

            @block.vector
            def _(e):
                run("dve", e)

            @block.gpsimd
            def _(e):
                run("pool", e)

            @block.sync
            def _(e):
                run("sp", e)


SM_CV = 0
SM_L = 8
SM_LSZ = 127
SM_MASK = SM_L + DEPTH * SM_LSZ
SM_LAM = SM_MASK + 160
NSM = SM_LAM + 8

CB_M1024, CB_M128, CB_ONE, CB_PERM, CB_CC, CB_CS = range(6)


def _lambda_init(l):
    return 0.8 - 0.6 * math.exp(-0.3 * l)


def build_program():
    nc = bass.Bass("TRN2", target_bir_lowering=False)

    def din(name, shape, dt=F32):
        return nc.dram_tensor(name, list(shape), dt, kind="ExternalInput").ap()

    def dout(name, shape, dt=F32):
        return nc.dram_tensor(name, list(shape), dt, kind="ExternalOutput").ap()

    x_d = din("x", [T, D])
    sm_d = din("sm", [128, NSM])
    cb_d = din("cb", [128, 6 * 128], BF16)
    idf_d = din("idf", [128, 128])
    mq_d = din("mq", [2, 768], BF16)
    rope_d = din("rope", [128, 2 * T])
    cmask_d = din("cmask", [128, 2 * T], BF16)
    dft_d = din("dft", [2, T, T], BF16)
    ck_d = din("ck", [DEPTH, PAST, 512])
    cvv_d = din("cvv", [DEPTH, PAST, 512])
    wmod_d = din("w_mod", [DEPTH, D, 9 * D])
    wgu_d = din("w_ffn_gu", [DEPTH, 2, D, 2 * DFF])
    wdn_d = din("w_ffn_down", [DEPTH, 2, DFF, D])
    win_d = din("w_in", [DEPTH, D, 2560])
    wout_d = din("w_out", [DEPTH, D, D])
    y_d = dout("y", [T, D])
    nk_d = dout("nk", [DEPTH, T, 512])
    nv_d = dout("nv", [DEPTH, T, 512])

    P = Prog()
    with ExitStack() as ctx:
        sb = lambda name, shape, dt: ctx.enter_context(nc.sbuf_tensor("s_" + name, list(shape), dt))
        xT = sb("xT", [128, NCH, T], F32)
        wg = [sb("wg%d" % i, [128, 8, 512], BF16) for i in range(2)]
        sm = sb("sm", [128, NSM], F32)
        cb = sb("cb", [128, 6 * 128], BF16)
        idf = sb("idf", [128, 128], F32)
        mq = sb("mq", [2, 768], BF16)
        onesf = sb("onesf", [128, 128], F32)
        drv = sb("drv", [128, 160], F32)
        sT = sb("sT", [128, 8], BF16)
        epsc = sb("epsc", [128, 1], F32)
        ARENA_KB = 120
        arena = sb("arena", [128, ARENA_KB * 512], BF16)
        arena_f = arena.bitcast(F32)
        psall = ctx.enter_context(nc.psum_tensor("psall", [128, 8, 512], F32))
        ps = [psall[:, i, :] for i in range(8)]

        def abf(off_kb, shape):
            n = int(np.prod(shape[1:]))
            o = int(off_kb * 512)
            ap = arena[:, o:o + n]
            if len(shape) == 3:
                ap = ap.rearrange("p (a b) -> p a b", a=shape[1])
            return ap

        def af32(off_kb, shape):
            n = int(np.prod(shape[1:]))
            o = int(off_kb * 256)
            ap = arena_f[:, o:o + n]
            if len(shape) == 3:
                ap = ap.rearrange("p (a b) -> p a b", a=shape[1])
            return ap

        def cbm(i):
            return cb[:, i * 128:(i + 1) * 128]

        rot = {"i": 0}

        def nbank(lo=0, hi=6):
            b = lo + rot["i"] % (hi - lo)
            rot["i"] += 1
            return b

        def pk(b):
            return "ps%d" % b

        wgi = {"i": 0}

        def next_wg():
            b = wgi["i"] % 2
            wgi["i"] += 1
            return b

        P.add("sp", lambda e: e.dma_start(out=sm[:], in_=sm_d), writes=["sm"], dma_slot="sm")
        P.add("sp", lambda e: e.dma_start(out=cb[:], in_=cb_d), writes=["cb"], dma_slot="cb")
        P.add("sp", lambda e: e.dma_start(out=idf[:], in_=idf_d), writes=["idf"], dma_slot="idf")
        P.add("sp", lambda e: e.dma_start(out=mq[:], in_=mq_d), writes=["mq"], dma_slot="mq")
        P.add("dve", lambda e: e.memset(onesf[:], 1.0), writes=["onesf"])
        P.add("dve", lambda e: e.memset(epsc[:], EPS), writes=["epsc"])
        P.add("act", lambda e: e.activation(sT[:], sm[:, SM_CV:SM_CV + 8], AF.Silu), reads=["sm"], writes=["sT"])

        xst = [af32(0 + 4 * i, [128, D]) for i in range(2)]

        def xload_tile(tt):
            if tt >= 16:
                return
            st = xst[tt % 2]
            sk = "xst%d" % (tt % 2)
            P.add("sp", lambda e, st=st, tt=tt: e.dma_start(out=st, in_=x_d[tt * 128:(tt + 1) * 128, :]),
                  writes=[sk], dma_slot=sk)
            for hf in range(2):
                b = nbank()
                for c4 in range(4):
                    c = hf * 4 + c4
                    P.add("pe", lambda e, b=b, c=c, c4=c4, st=st: e.transpose(
                        ps[b][:, c4 * 128:(c4 + 1) * 128], st[:, c * 128:(c + 1) * 128], idf[:]),
                        reads=[sk, "idf"], writes=[pk(b)])
                eng = "act" if hf == 0 else "dve"
                outap = xT[:, hf * 4:hf * 4 + 4, tt * 128:(tt + 1) * 128]
                inap = ps[b][:].rearrange("p (a b) -> p a b", a=4)
                if eng == "act":
                    P.add("act", lambda e, o=outap, i=inap: e.activation(o, i, AF.Copy),
                          writes=[pk(b)] + ["xT%d_%d" % (c, tt // 4) for c in range(hf * 4, hf * 4 + 4)])
                else:
                    P.add("dve", lambda e, o=outap, i=inap: e.tensor_copy(o, i),
                          writes=[pk(b)] + ["xT%d_%d" % (c, tt // 4) for c in range(hf * 4, hf * 4 + 4)])

        drv2 = sb("drv2", [128, DEPTH * 48], F32)

        def mod_tile_evac(l, t_, pb):
            base = 80 * l
            wv = wmod_d[l].rearrange("(c p) f -> p c f", p=128)
            b = next_wg()
            P.add("pool", lambda e, b=b, t_=t_: e.dma_start(out=wg[b][:], in_=wv[:, :, t_ * 512:(t_ + 1) * 512]),
                  writes=["wg%d" % b], dma_slot="wg%d" % b)
            for fc in range(4):
                for c in range(8):
                    P.add("pe", lambda e, b=b, fc=fc, c=c: e.matmul(
                        ps[pb][:, fc:fc + 1], lhsT=wg[b][:, c, fc * 128:(fc + 1) * 128], rhs=sT[:, c:c + 1],
                        start=(c == 0), stop=(c == 7)),
                        reads=["wg%d" % b, "sT"], writes=[pk(pb)])
            bm = sm[:, SM_L + l * SM_LSZ + t_ * 4: SM_L + l * SM_LSZ + t_ * 4 + 4]
            P.add("dve", lambda e: e.tensor_tensor(drv[:, base + t_ * 4:base + t_ * 4 + 4], ps[pb][:, 0:4], bm, ALU.add),
                  reads=["sm"], writes=[pk(pb), "drvm%d" % l])

        def mod_tiles(l, ta, tb, hook=None):
            pb = 7
            wv = wmod_d[l].rearrange("(c p) f -> p c f", p=128)
            for t_ in range(ta, tb):
                if hook is not None:
                    hook(t_)
                b = next_wg()
                P.add("pool", lambda e, b=b, t_=t_: e.dma_start(out=wg[b][:], in_=wv[:, :, t_ * 512:(t_ + 1) * 512]),
                      writes=["wg%d" % b], dma_slot="wg%d" % b)
                for fc in range(4):
                    col = t_ * 4 + fc
                    for c in range(8):
                        P.add("pe", lambda e, b=b, fc=fc, c=c, col=col: e.matmul(
                            ps[pb][:, col:col + 1], lhsT=wg[b][:, c, fc * 128:(fc + 1) * 128], rhs=sT[:, c:c + 1],
                            start=(c == 0), stop=(c == 7)),
                            reads=["wg%d" % b, "sT"], writes=[pk(pb)])

        def mod_finish(l, evacuated=False):
            base = 80 * l
            bmod = sm[:, SM_L + l * SM_LSZ: SM_L + l * SM_LSZ + 72]
            ng = lambda i: sm[:, SM_L + l * SM_LSZ + 72 + i * 8: SM_L + l * SM_LSZ + 72 + i * 8 + 8]
            pb = 7
            if not evacuated:
                P.add("dve", lambda e: e.tensor_tensor(drv[:, base:base + 72], ps[pb][:, 0:72], bmod, ALU.add),
                      reads=["sm"], writes=[pk(pb), "drv"])
            else:
                P.add("dve", lambda e: e.memset(drv[:, base + 78:base + 79], 0.0),
                      reads=["drvm%d" % l], writes=["drv"])
            for s in range(3):
                sc = drv[:, base + (3 * s + 1) * 8: base + (3 * s + 1) * 8 + 8]
                gt = drv[:, base + (3 * s + 2) * 8: base + (3 * s + 2) * 8 + 8]
                gs = drv2[:, l * 48 + s * 8: l * 48 + s * 8 + 8]
                gg = drv2[:, l * 48 + 24 + s * 8: l * 48 + 24 + s * 8 + 8]
                wres = 1.0 if s == 1 else 0.5
                P.add("dve", lambda e, sc=sc, gs=gs, s=s: e.scalar_tensor_tensor(
                    gs, sc, 1.0, ng(2 * s), ALU.add, ALU.mult), reads=["drv", "sm"], writes=["drv2"])
                P.add("dve", lambda e, gt=gt, gg=gg, s=s, wres=wres: e.scalar_tensor_tensor(
                    gg, gt, wres, ng(2 * s + 1), ALU.mult, ALU.mult), reads=["drv", "sm"], writes=["drv2"])
            lc = SM_LAM + l * 4
            tmp = drv[0:64, base + 76:base + 78]
            P.add("dve", lambda e: e.tensor_tensor(drv[0:64, base + 76:base + 77], sm[0:64, lc:lc + 1],
                                                  sm[0:64, lc + 1:lc + 2], ALU.mult), reads=["sm"], writes=["drv"])
            P.add("dve", lambda e: e.tensor_tensor(drv[0:64, base + 77:base + 78], sm[0:64, lc + 2:lc + 3],
                                                  sm[0:64, lc + 3:lc + 4], ALU.mult), reads=["sm"], writes=["drv"])
            P.add("pe", lambda e: e.matmul(ps[pb][:, 128:130], lhsT=onesf[0:64, :], rhs=tmp, start=True, stop=True),
                  reads=["drv", "onesf"], writes=[pk(pb)])
            P.add("act", lambda e: e.activation(drv[:, base + 74:base + 76], ps[pb][:, 128:130], AF.Exp),
                  writes=[pk(pb), "drv"])
            P.add("dve", lambda e: e.tensor_tensor(drv[:, base + 72:base + 73], drv[:, base + 74:base + 75],
                                                  drv[:, base + 75:base + 76], ALU.subtract), writes=["drv"])
            li = _lambda_init(l)
            P.add("dve", lambda e: e.tensor_scalar(drv[:, base + 72:base + 73], drv[:, base + 72:base + 73],
                                                  li, -1.0, ALU.add, ALU.mult), writes=["drv"])
            sub = sm[:, SM_L + l * SM_LSZ + 126: SM_L + l * SM_LSZ + 127]
            P.add("dve", lambda e: e.tensor_scalar(drv[:, base + 73:base + 74], sub, 1.0 - li, None, ALU.mult),
                  reads=["sm"], writes=["drv"])

        def mod_part(l, names):
            base = 80 * l
            ng = lambda i: sm[:, SM_L + l * SM_LSZ + 72 + i * 8: SM_L + l * SM_LSZ + 72 + i * 8 + 8]
            P.add("dve", lambda e: e.memset(drv[:, base + 78:base + 79], 0.0),
                  reads=["drvm%d" % l], writes=["drv"])
            for kind, s in names:
                if kind == "gs":
                    sc = drv[:, base + (3 * s + 1) * 8: base + (3 * s + 1) * 8 + 8]
                    gs = drv2[:, l * 48 + s * 8: l * 48 + s * 8 + 8]
                    P.add("dve", lambda e, sc=sc, gs=gs, s=s: e.scalar_tensor_tensor(
                        gs, sc, 1.0, ng(2 * s), ALU.add, ALU.mult), reads=["drv", "sm"], writes=["drv2"])
                else:
                    gt = drv[:, base + (3 * s + 2) * 8: base + (3 * s + 2) * 8 + 8]
                    gg = drv2[:, l * 48 + 24 + s * 8: l * 48 + 24 + s * 8 + 8]
                    wres = 1.0 if s == 1 else 0.5
                    P.add("dve", lambda e, gt=gt, gg=gg, s=s, wres=wres: e.scalar_tensor_tensor(
                        gg, gt, wres, ng(2 * s + 1), ALU.mult, ALU.mult), reads=["drv", "sm"], writes=["drv2"])

        def mod_lambda(l):
            base = 80 * l
            pb = 7
            lc = SM_LAM + l * 4
            tmp = drv[0:64, base + 76:base + 78]
            P.add("dve", lambda e: e.tensor_tensor(drv[0:64, base + 76:base + 77], sm[0:64, lc:lc + 1],
                                                  sm[0:64, lc + 1:lc + 2], ALU.mult), reads=["sm"], writes=["drv"])
            P.add("dve", lambda e: e.tensor_tensor(drv[0:64, base + 77:base + 78], sm[0:64, lc + 2:lc + 3],
                                                  sm[0:64, lc + 3:lc + 4], ALU.mult), reads=["sm"], writes=["drv"])
            P.add("pe", lambda e: e.matmul(ps[pb][:, 128:130], lhsT=onesf[0:64, :], rhs=tmp, start=True, stop=True),
                  reads=["drv", "onesf"], writes=[pk(pb)])
            P.add("act", lambda e: e.activation(drv[:, base + 74:base + 76], ps[pb][:, 128:130], AF.Exp),
                  writes=[pk(pb), "drv"])
            P.add("dve", lambda e: e.tensor_tensor(drv[:, base + 72:base + 73], drv[:, base + 74:base + 75],
                                                  drv[:, base + 75:base + 76], ALU.subtract), writes=["drv"])
            li = _lambda_init(l)
            P.add("dve", lambda e: e.tensor_scalar(drv[:, base + 72:base + 73], drv[:, base + 72:base + 73],
                                                  li, -1.0, ALU.add, ALU.mult), writes=["drv"])
            sub = sm[:, SM_L + l * SM_LSZ + 126: SM_L + l * SM_LSZ + 127]
            P.add("dve", lambda e: e.tensor_scalar(drv[:, base + 73:base + 74], sub, 1.0 - li, None, ALU.mult),
                  reads=["sm"], writes=["drv"])

        for t_ in range(4):
            for k_ in range(4):
                xload_tile(4 * t_ + k_)
            mod_tile_evac(0, t_, 7)
        mod_part(0, [("gs", 0)])
        mod_lambda(0)
        P.barrier()

        def mod0_down_hook(half, dc):
            t_ = 4 + half * 8 + dc
            if t_ < 18:
                mod_tile_evac(0, t_, nbank(0, 6))
            if half == 0 and dc == 1:
                mod_part(0, [("gg", 0)])
            if half == 0 and dc == 7:
                mod_part(0, [("gs", 1), ("gg", 1)])
            if half == 1 and dc == 5:
                mod_part(0, [("gs", 2), ("gg", 2)])

        def mod_cols(l, j):
            return drv[:, 80 * l + j * 8: 80 * l + j * 8 + 8]

        hT = abf(0, [128, 8, 1024])
        NT_SQ = [abf(16 + i, [128, 512]) for i in range(2)]
        NT_R = af32(18, [128, 512])
        NT_L = af32(20, [128, 512])
        NT_T = [af32(22 + 2 * i, [128, 512]) for i in range(2)]

        NT_SQ2 = [abf(113 + i, [128, 512]) for i in range(4)]
        pending = []

        def drain(n=1):
            for _ in range(n):
                if pending:
                    pending.pop(0)()

        def flush():
            while pending:
                pending.pop(0)()

        def norm_site(l, s, half, alt=False, defer=None):
            gs = drv2[:, l * 48 + s * 8: l * 48 + s * 8 + 8]
            sh = mod_cols(l, 3 * s)
            sqb = NT_SQ2 if alt else NT_SQ
            sqn = "ntsqb%d" if alt else "ntsq%d"
            nbanks = [nbank(), nbank()]
            for bl in range(2):
                t0 = half * 1024 + bl * 512
                gb_ = t0 // 512
                b = nbanks[bl]
                for c in range(8):
                    sq = sqb[c % len(sqb)]
                    sqk = sqn % (c % len(sqb))
                    if c % 2 == 0:
                        P.add("act", lambda e, sq=sq, c=c, t0=t0: e.activation(sq, xT[:, c, t0:t0 + 512], AF.Square),
                              reads=["xT%d_%d" % (c, gb_)], writes=[sqk])
                    else:
                        P.add("dve", lambda e, sq=sq, c=c, t0=t0: e.tensor_tensor(
                            sq, xT[:, c, t0:t0 + 512], xT[:, c, t0:t0 + 512], ALU.mult),
                            reads=["xT%d_%d" % (c, gb_)], writes=[sqk])
                    P.add("pe", lambda e, b=b, sq=sq, c=c: e.matmul(ps[b][:], lhsT=cbm(CB_M1024), rhs=sq,
                                                                   start=(c == 0), stop=(c == 7)),
                          reads=[sqk, "cb"], writes=[pk(b)])
            pieces = []
            for bl in range(2):
                t0 = half * 1024 + bl * 512
                gb_ = t0 // 512
                b = nbanks[bl]

                rbuf = NT_R if bl == 0 else NT_L
                rkey = "ntr" if bl == 0 else "ntl"
                P.add("act", lambda e, b=b: e.activation(NT_L, ps[b][:], AF.Ln, bias=epsc[:, 0:1]),
                      reads=["epsc"], writes=[pk(b), "ntl"])
                P.add("act", lambda e, rbuf=rbuf: e.activation(rbuf, NT_L, AF.Exp, scale=-0.5),
                      reads=["ntl"], writes=[rkey])
                for c in range(8):
                    def body(c=c, bl=bl, t0=t0, gb_=gb_, rbuf=rbuf, rkey=rkey):
                        tt_ = NT_T[c % 2]
                        tk = "ntt%d" % (c % 2)
                        P.add("dve", lambda e, tt_=tt_, c=c, t0=t0: e.scalar_tensor_tensor(
                            tt_, xT[:, c, t0:t0 + 512], gs[:, c:c + 1], rbuf, ALU.mult, ALU.mult),
                            reads=["xT%d_%d" % (c, gb_), rkey, "drv2"], writes=[tk])
                        P.add("act", lambda e, tt_=tt_, c=c, bl=bl: e.activation(
                            hT[:, c, bl * 512:(bl + 1) * 512], tt_, AF.Identity, bias=sh[:, c:c + 1]),
                            reads=[tk, "drv"], writes=["hT%d_%d" % (c, bl)])
                    pieces.append(body)
            if defer is None:
                for f in pieces:
                    f()
            else:
                defer.extend(pieces)

        def post_closures(l, s, yT, nblk, tok0, ssb, ykeys):
            gg = drv2[:, l * 48 + 24 + s * 8: l * 48 + 24 + s * 8 + 8]
            out = []
            for bl in range(nblk):
                t0 = tok0 + bl * 512
                gb_ = t0 // 512
                b = ssb[bl]

                def head(b=b):
                    P.add("act", lambda e, b=b: e.activation(NT_L, ps[b][:], AF.Ln, bias=epsc[:, 0:1]),
                          reads=["epsc"], writes=[pk(b), "ntl"])
                    P.add("act", lambda e: e.activation(NT_R, NT_L, AF.Exp, scale=-0.5), reads=["ntl"], writes=["ntr"])
                out.append(head)
                for c in range(8):
                    def body(c=c, bl=bl, t0=t0, gb_=gb_):
                        tt_ = NT_T[c % 2]
                        tk = "ntt%d" % (c % 2)
                        P.add("dve", lambda e, tt_=tt_, c=c, bl=bl: e.scalar_tensor_tensor(
                            tt_, yT[:, c, bl * 512:(bl + 1) * 512], gg[:, c:c + 1], NT_R, ALU.mult, ALU.mult),
                            reads=[ykeys(c, bl), "ntr", "drv2"], writes=[tk])
                        P.add("dve", lambda e, tt_=tt_, c=c, t0=t0: e.tensor_tensor(
                            xT[:, c, t0:t0 + 512], xT[:, c, t0:t0 + 512], tt_, ALU.add),
                            reads=[tk], writes=["xT%d_%d" % (c, gb_)])
                    out.append(body)
            return out

        def post_site(l, s, yT, nblk, tok0, ssb, ykeys):
            for f in post_closures(l, s, yT, nblk, tok0, ssb, ykeys):
                f()

        def ffn_first_tile(l, i):
            wgu = wgu_d[l, i].rearrange("(c p) f -> p c f", p=128)
            b = next_wg()
            P.add("pool", lambda e, b=b: e.dma_start(out=wg[b][:, :, 0:256], in_=wgu[:, :, 0:256]),
                  writes=["wg%d" % b], dma_slot="wg%da" % b)
            P.add("pool", lambda e, b=b: e.dma_start(out=wg[b][:, :, 256:512], in_=wgu[:, :, DFF:DFF + 256]),
                  writes=["wg%d" % b], dma_slot="wg%db" % b)
            return b

        def ffn(l, i, first_norm_done=False, tail=None, down_hook=None, pre_tile=None):
            s = 0 if i == 0 else 2
            actT = abf(26, [128, NF, 1024])
            wd = [abf(70 + 5.5 * k, [128, NF, 128]) for k in range(2)]
            yT = af32(81, [128, 8, 1024])
            sg = [af32(113 + 2 * k, [128, 512]) for k in range(2)]
            wgu = wgu_d[l, i].rearrange("(c p) f -> p c f", p=128)
            wdn = wdn_d[l, i].rearrange("(f p) d -> p f d", p=128)
            if not first_norm_done:
                norm_site(l, s, 0)
            for half in range(2):
                for j in range(11):
                    if half == 0 and j == 0 and pre_tile is not None:
                        b = pre_tile
                    else:
                        b = next_wg()
                        P.add("pool", lambda e, b=b, j=j: e.dma_start(out=wg[b][:, :, 0:256], in_=wgu[:, :, j * 256:(j + 1) * 256]),
                              writes=["wg%d" % b], dma_slot="wg%da" % b)
                        P.add("pool", lambda e, b=b, j=j: e.dma_start(out=wg[b][:, :, 256:512],
                                                                     in_=wgu[:, :, DFF + j * 256:DFF + (j + 1) * 256]),
                              writes=["wg%d" % b], dma_slot="wg%db" % b)
                    for fcl in range(2):
                        fc = 2 * j + fcl
                        for bl in range(2):
                            bg = nbank(0, 6)
                            bu = nbank(0, 6)
                            for c in range(8):
                                P.add("pe", lambda e, b=b, bg=bg, c=c, fcl=fcl, bl=bl: e.matmul(
                                    ps[bg][:], lhsT=wg[b][:, c, fcl * 128:(fcl + 1) * 128],
                                    rhs=hT[:, c, bl * 512:(bl + 1) * 512], start=(c == 0), stop=(c == 7)),
                                    reads=["wg%d" % b, "hT%d_%d" % (c, bl)], writes=[pk(bg)])
                            for c in range(8):
                                P.add("pe", lambda e, b=b, bu=bu, c=c, fcl=fcl, bl=bl: e.matmul(
                                    ps[bu][:], lhsT=wg[b][:, c, 256 + fcl * 128:256 + (fcl + 1) * 128],
                                    rhs=hT[:, c, bl * 512:(bl + 1) * 512], start=(c == 0), stop=(c == 7)),
                                    reads=["wg%d" % b, "hT%d_%d" % (c, bl)], writes=[pk(bu)])
                            k = (fcl * 2 + bl) % 2
                            P.add("act", lambda e, k=k, bg=bg: e.activation(sg[k], ps[bg][:], AF.Silu),
                                  writes=[pk(bg), "sg%d" % k])
                            P.add("dve", lambda e, k=k, bu=bu, fc=fc, bl=bl: e.tensor_tensor(
                                actT[:, fc, bl * 512:(bl + 1) * 512], sg[k], ps[bu][:], ALU.mult),
                                reads=["sg%d" % k], writes=[pk(bu), "actT%d_%d" % (fc, bl)])
                            drain(1)
                assert not pending
                hoist = (lambda d: norm_site(l, s, 1, alt=True, defer=d)) if half == 0 else tail
                dpend = []
                ssb = [6, 7]
                pend = None

                def emit_ss(sq, sqk, dc, bl):
                    P.add("pe", lambda e: e.matmul(
                        ps[6 + bl][:], lhsT=cbm(CB_M1024), rhs=sq, start=(dc == 0), stop=(dc == 7)),
                        reads=[sqk, "cb"], writes=[pk(6 + bl)])

                for dc in range(8):
                    k = dc % 2
                    P.add("pool", lambda e, k=k, dc=dc: e.dma_start(out=wd[k], in_=wdn[:, :, dc * 128:(dc + 1) * 128]),
                          writes=["wd%d" % k], dma_slot="wd%d" % k)
                    for bl in range(2):
                        b = nbank(0, 6)
                        for f in range(NF):
                            P.add("pe", lambda e, k=k, b=b, f=f, bl=bl: e.matmul(
                                ps[b][:], lhsT=wd[k][:, f, :], rhs=actT[:, f, bl * 512:(bl + 1) * 512],
                                start=(f == 0), stop=(f == NF - 1)),
                                reads=["wd%d" % k, "actT%d_%d" % (f, bl)], writes=[pk(b)])
                        P.add("act", lambda e, b=b, dc=dc, bl=bl: e.activation(
                            yT[:, dc, bl * 512:(bl + 1) * 512], ps[b][:], AF.Copy),
                            writes=[pk(b), "yT%d_%d" % (dc, bl)])
                        sq = NT_SQ[(dc * 2 + bl) % 2]
                        sqk = "ntsq%d" % ((dc * 2 + bl) % 2)
                        P.add("dve", lambda e, sq=sq, dc=dc, bl=bl: e.tensor_tensor(
                            sq, yT[:, dc, bl * 512:(bl + 1) * 512], yT[:, dc, bl * 512:(bl + 1) * 512], ALU.mult),
                            reads=["yT%d_%d" % (dc, bl)], writes=[sqk])
                        if pend is not None:
                            emit_ss(*pend)
                        pend = (sq, sqk, dc, bl)
                    if dc == 0 and hoist is not None:
                        hoist(dpend)
                    for _ in range(3):
                        if dpend:
                            dpend.pop(0)()
                    if down_hook is not None:
                        down_hook(half, dc)
                emit_ss(*pend)
                while dpend:
                    dpend.pop(0)()
                pending.extend(post_closures(l, s, yT, 2, half * 1024, ssb, lambda c, bl: "yT%d_%d" % (c, bl)))

        def mixer(l, first_norm_done=False, pre_drain=False):
            s = 1
            winv = win_d[l].rearrange("(c p) f -> p c f", p=128)
            mixC = abf(26, [128, 4, T])
            gbT = abf(42, [128, 2, T])
            gcT = af32(50, [128, 2, T])
            uT = af32(66, [128, 2, T + 2])
            fT = abf(83, [128, 16, 256])
            aT = [[abf(114 + (cs * 2 + ch), [128, 512]) for ch in range(2)] for cs in range(2)]
            cmk = abf(95, [128, 2, T])
            cA = af32(103, [128, 512])
            cB = af32(105, [128, 512])

            def load_win(tile_idx):
                b = next_wg()
                P.add("pool", lambda e, b=b: e.dma_start(out=wg[b][:], in_=winv[:, :, tile_idx * 512:(tile_idx + 1) * 512]),
                      writes=["wg%d" % b], dma_slot="wg%d" % b)
                return b

            def proj_fm(b, col0, bl, bank):
                for c in range(8):
                    P.add("pe", lambda e, c=c: e.matmul(ps[bank][:], lhsT=wg[b][:, c, col0:col0 + 128],
                                                        rhs=hT[:, c, bl * 512:(bl + 1) * 512],
                                                        start=(c == 0), stop=(c == 7)),
                          reads=["wg%d" % b, "hT%d_%d" % (c, bl)], writes=[pk(bank)])

            def proj_tm(b, col0, ncol, ttl, bank):
                for c in range(8):
                    P.add("pe", lambda e, c=c: e.matmul(ps[bank][:, 0:ncol], lhsT=hT[:, c, ttl * 128:(ttl + 1) * 128],
                                                        rhs=wg[b][:, c, col0:col0 + ncol],
                                                        start=(c == 0), stop=(c == 7)),
                          reads=["wg%d" % b, "hT%d_%d" % (c, ttl // 4)], writes=[pk(bank)])

            def t3_proj(half, b3=None):
                if b3 is None:
                    b3 = load_win(3)
                for ch4 in range(4):
                    for bl in range(2):
                        t0 = half * 1024 + bl * 512
                        bank = nbank()
                        proj_fm(b3, ch4 * 128, bl, bank)
                        if ch4 < 2:
                            P.add("act", lambda e, bank=bank, ch4=ch4, t0=t0: e.activation(
                                gbT[:, ch4, t0:t0 + 512], ps[bank][:], AF.Copy),
                                writes=[pk(bank), "gbT%d_%d" % (ch4, t0 // 512)])
                        else:
                            P.add("act", lambda e, bank=bank, ch4=ch4, t0=t0: e.activation(
                                gcT[:, ch4 - 2, t0:t0 + 512], ps[bank][:], AF.Copy),
                                writes=[pk(bank), "gcT%d_%d" % (ch4 - 2, t0 // 512)])
                        drain(3)

            pre_b4 = None
            if pre_drain:
                pre_b3 = load_win(3)
                pre_b4 = load_win(4)
                P.barrier()
                t3_proj(0, pre_b3)
                flush()
                P.barrier()
            P.add("sp", lambda e: e.dma_start(out=cmk, in_=cmask_d.rearrange("p (a b) -> p a b", a=2)),
                  writes=["cmk"], dma_slot="cmk")
            for cc in range(2):
                P.add("dve", lambda e, cc=cc: e.memset(uT[:, cc, 0:1], 0.0), writes=["uTpad"])
                P.add("dve", lambda e, cc=cc: e.memset(uT[:, cc, T + 1:T + 2], 0.0), writes=["uTpad"])
            for half in range(2):
                if not (half == 0 and first_norm_done):
                    norm_site(l, s, half)
                if not (half == 0 and pre_drain):
                    t3_proj(half)
                b4 = pre_b4 if (half == 0 and pre_b4 is not None) else load_win(4)
                for cc in range(2):
                    for bl in range(2):
                        t0 = half * 1024 + bl * 512
                        bank = nbank()
                        proj_fm(b4, cc * 128, bl, bank)
                        P.add("dve", lambda e, bank=bank, cc=cc, t0=t0: e.tensor_tensor(
                            uT[:, cc, 1 + t0:1 + t0 + 512], ps[bank][:], gcT[:, cc, t0:t0 + 512], ALU.mult),
                            reads=["gcT%d_%d" % (cc, t0 // 512)], writes=[pk(bank), "uT%d_%d" % (cc, t0 // 512)])
                for ttl in range(8):
                    tt = half * 8 + ttl
                    bank = nbank()
                    proj_tm(b4, 256, 256, ttl, bank)
                    P.add("act", lambda e, bank=bank, tt=tt: e.activation(fT[:, tt, :], ps[bank][:, 0:256], AF.Copy),
                          writes=[pk(bank), "fT%d" % tt])
            cw = lambda cc, k: sm[:, SM_L + l * SM_LSZ + 120 + cc * 3 + k: SM_L + l * SM_LSZ + 120 + cc * 3 + k + 1]
            for cc in range(2):
                for blk in range(4):
                    t0 = blk * 512
                    ukeys = ["uT%d_%d" % (cc, g) for g in range(max(0, blk - 1), min(3, blk + 1) + 1)] + ["uTpad"]
                    P.add("dve", lambda e, cc=cc, t0=t0: e.tensor_scalar(
                        cA, uT[:, cc, 1 + t0:1 + t0 + 512], cw(cc, 1), None, ALU.mult),
                        reads=ukeys + ["sm"], writes=["cA"])
                    P.add("dve", lambda e, cc=cc, t0=t0: e.tensor_tensor(
                        cB, uT[:, cc, t0:t0 + 512], cmk[:, 0, t0:t0 + 512], ALU.mult),
                        reads=ukeys + ["cmk"], writes=["cB"])
                    P.add("dve", lambda e, cc=cc: e.scalar_tensor_tensor(
                        cA, cB, cw(cc, 0), cA, ALU.mult, ALU.add), reads=["cB", "sm"], writes=["cA"])
                    P.add("dve", lambda e, cc=cc, t0=t0: e.tensor_tensor(
                        cB, uT[:, cc, 2 + t0:2 + t0 + 512], cmk[:, 1, t0:t0 + 512], ALU.mult),
                        reads=ukeys + ["cmk"], writes=["cB"])
                    P.add("dve", lambda e, cc=cc: e.scalar_tensor_tensor(
                        cA, cB, cw(cc, 2), cA, ALU.mult, ALU.add), reads=["cB", "sm"], writes=["cA"])
                    P.add("dve", lambda e, cc=cc, t0=t0: e.tensor_tensor(
                        mixC[:, cc, t0:t0 + 512], cA, gbT[:, cc, t0:t0 + 512], ALU.mult),
                        reads=["cA", "gbT%d_%d" % (cc, blk)], writes=["mixC%d_%d" % (cc, blk)])
            kT = abf(42, [128, 4, T + PAST])
            V = abf(78, [128, NKT, 512])
            rope = af32(98, [128, 2, T])
            ckst = af32(62, [128, 4, 512])
            convdone = ["mixC0_3", "mixC1_3"]
            dv = [dft_d[cs].rearrange("(c p) n -> p c n", p=128) for cs in range(2)]
            for nb in range(4):
                if nb == 2:
                    norm_site(l, s, 0)
                for cs in range(2):
                    banks = [nbank(), nbank()]
                    for hf in range(2):
                        b = next_wg()
                        P.add("sp", lambda e, b=b, cs=cs, hf=hf, nb=nb: e.dma_start(
                            out=wg[b][:], in_=dv[cs][:, hf * 8:(hf + 1) * 8, nb * 512:(nb + 1) * 512]),
                            writes=["wg%d" % b], dma_slot="wgsp%d" % b)
                        for ch in range(2):
                            for c in range(8):
                                P.add("pe", lambda e, b=b, ch=ch, c=c, hf=hf, bk=banks[ch]: e.matmul(
                                    ps[bk][:], lhsT=fT[:, hf * 8 + c, ch * 128:(ch + 1) * 128], rhs=wg[b][:, c, :],
                                    start=(hf == 0 and c == 0), stop=(hf == 1 and c == 7)),
                                    reads=["wg%d" % b, "fT%d" % (hf * 8 + c)], writes=[pk(banks[ch])])
                    for ch in range(2):
                        P.add("act", lambda e, cs=cs, ch=ch, bk=banks[ch]: e.activation(aT[cs][ch], ps[bk][:], AF.Copy),
                              writes=[pk(banks[ch]), "aT%d_%d" % (cs, ch)])
                for ch in range(2):
                    bank = nbank()
                    P.add("pe", lambda e, ch=ch, bank=bank: e.matmul(ps[bank][:], lhsT=cbm(CB_CC), rhs=aT[0][ch],
                                                                     start=True, stop=False),
                          reads=["cb", "aT0_%d" % ch], writes=[pk(bank)])
                    P.add("pe", lambda e, ch=ch, bank=bank: e.matmul(ps[bank][:], lhsT=cbm(CB_CS), rhs=aT[1][ch],
                                                                     start=False, stop=True),
                          reads=["cb", "aT1_%d" % ch], writes=[pk(bank)])
                    P.add("act", lambda e, ch=ch, bank=bank, nb=nb: e.activation(
                        mixC[:, 2 + ch, nb * 512:(nb + 1) * 512], ps[bank][:], AF.Copy),
                        writes=[pk(bank), "mixC%d_%d" % (2 + ch, nb)])
            P.add("sp", lambda e: e.dma_start(out=rope, in_=rope_d.rearrange("p (a b) -> p a b", a=2)),
                  reads=convdone, writes=["rope"], dma_slot="rope")
            P.add("pool", lambda e: e.dma_start(out=V[:, 16:20, :], in_=cvv_d[l].rearrange("(t p) e -> p t e", p=128)),
                  reads=convdone, writes=["Vctx"], dma_slot="Vctx")
            P.add("sp", lambda e: e.dma_start(out=ckst, in_=ck_d[l].rearrange("(t p) e -> p t e", p=128)),
                  reads=convdone, writes=["ckst"], dma_slot="ckst")
            for h in range(4):
                bank = nbank()
                for pt in range(4):
                    P.add("pe", lambda e, h=h, pt=pt, bank=bank: e.transpose(
                        ps[bank][:, pt * 128:(pt + 1) * 128], ckst[:, pt, h * 128:(h + 1) * 128], idf[:]),
                        reads=["ckst", "idf"], writes=[pk(bank)])
                P.add("act", lambda e, h=h, bank=bank: e.activation(kT[:, h, T:T + PAST], ps[bank][:], AF.Copy),
                      reads=convdone, writes=[pk(bank), "kTctx%d" % h])
            P.barrier()

            qT = abf(62, [128, 4, T])
            qb16 = [abf(114 + k, [128, 512]) for k in range(2)]
            stg = [af32(116, [128, 512]), af32(118, [128, 512])]
            stgi = {"i": 0}
            for half in range(2):
                if half == 1:
                    norm_site(l, s, half)
                for which in range(2):
                    bq = load_win(which)
                    dst = qT if which == 0 else kT
                    prev = None
                    for h in range(4):
                        for bl in range(2):
                            t0 = half * 1024 + bl * 512
                            bank = nbank()
                            proj_fm(bq, h * 128, bl, bank)
                            qb = qb16[(h * 2 + bl) % 2]
                            qk_ = "qb16_%d" % ((h * 2 + bl) % 2)
                            P.add("act", lambda e, qb=qb, bank=bank: e.activation(qb, ps[bank][:], AF.Copy),
                                  writes=[pk(bank), qk_])

                            def rope_part(qb=qb, qk_=qk_, bank=bank, t0=t0, h=h, dst=dst, which=which):
                                bank2 = nbank()
                                P.add("pe", lambda e, qb=qb, bank2=bank2: e.matmul(ps[bank2][:], lhsT=cbm(CB_PERM), rhs=qb,
                                                                                   start=True, stop=True),
                                      reads=["cb", qk_], writes=[pk(bank2)])
                                t1 = NT_T[0]
                                t2 = NT_T[1]
                                P.add("dve", lambda e, bank=bank, t0=t0: e.tensor_tensor(
                                    t1, ps[bank][:], rope[:, 0, t0:t0 + 512], ALU.mult),
                                    reads=["rope"], writes=[pk(bank), "ntt0"])
                                P.add("dve", lambda e, bank2=bank2, t0=t0: e.tensor_tensor(
                                    t2, ps[bank2][:], rope[:, 1, t0:t0 + 512], ALU.mult),
                                    reads=["rope"], writes=[pk(bank2), "ntt1"])
                                P.add("dve", lambda e, dst=dst, h=h, t0=t0: e.tensor_tensor(
                                    dst[:, h, t0:t0 + 512], t1, t2, ALU.add),
                                    reads=["ntt0", "ntt1"],
                                    writes=["%s%d_%d" % ("qT" if which == 0 else "kT", h, t0 // 512)])

                            if prev is not None:
                                prev()
                            prev = rope_part
                    prev()
                    if which == 1:
                        for ttl in range(8):
                            tt = half * 8 + ttl
                            bank = nbank()
                            proj_tm(bq, 0, 512, ttl, bank)
                            si = stgi["i"] % 2
                            stgi["i"] += 1
                            P.add("act", lambda e, bank=bank, si=si: e.activation(stg[si], ps[bank][:], AF.Copy),
                                  writes=[pk(bank), "stg%d" % si])
                            P.add("sp", lambda e, tt=tt, si=si: e.dma_start(out=nk_d[l, tt * 128:(tt + 1) * 128, :], in_=stg[si]),
                                  reads=["stg%d" % si], dma_slot="ostg%d" % si, final=True)
                bv = load_win(2)
                for ttl in range(8):
                    tt = half * 8 + ttl
                    bank = nbank()
                    proj_tm(bv, 0, 512, ttl, bank)
                    si = stgi["i"] % 2
                    stgi["i"] += 1
                    P.add("act", lambda e, bank=bank, si=si: e.activation(stg[si], ps[bank][:], AF.Copy),
                          writes=[pk(bank), "stg%d" % si])
                    P.add("dve", lambda e, tt=tt, si=si: e.tensor_copy(V[:, tt, :], stg[si]),
                          reads=["stg%d" % si], writes=["V%d" % tt])
                    P.add("sp", lambda e, tt=tt, si=si: e.dma_start(out=nv_d[l, tt * 128:(tt + 1) * 128, :], in_=stg[si]),
                          reads=["stg%d" % si], dma_slot="ostg%d" % si, final=True)
            P.barrier()

            mixA = abf(0, [128, 4, T])
            NPT = 6
            pTall = abf(98, [128, NPT, 512])
            pT = [pTall[:, k, :] for k in range(NPT)]
            rcp = [af32(104 + 2 * k, [128, 512]) for k in range(2)]
            tO = [af32(108 + 2 * k, [128, 512]) for k in range(2)]
            oF = af32(112, [128, 512])
            osq = abf(114, [128, 512])
            accD = [af32(115, [128, 512]), af32(117, [128, 512])]
            negLam = drv[:, 80 * l + 72: 80 * l + 73]
            sgl = drv[:, 80 * l + 73: 80 * l + 74]
            items = [(h, qb_, kt) for h in range(4) for qb_ in range(4) for kt in range(NKT)]
            state = {}
            acc_o = [5, 6]
            SPARE = 4

            def stage_A(j):
                h, qb_, kt = items[j]
                q0 = qb_ * 512
                diag = (kt < 16 and kt // 4 == qb_)
                banks = [(2 * j) % 4, (2 * j + 1) % 4]
                kkey = ("kT%d_%d" % (h, kt // 4)) if kt < 16 else ("kTctx%d" % h)
                for m in range(2):
                    bank = banks[m]
                    P.add("pe", lambda e, bank=bank, h=h, kt=kt, m=m, q0=q0: e.matmul(
                        ps[bank][:], lhsT=kT[m * 64:(m + 1) * 64, h, kt * 128:(kt + 1) * 128],
                        rhs=qT[m * 64:(m + 1) * 64, h, q0:q0 + 512], start=True, stop=True),
                        reads=[kkey, "qT%d_%d" % (h, qb_)], writes=[pk(bank)])
                b0 = banks[0]
                pi0 = (2 * j) % NPT
                if diag:
                    for qh in range(2):
                        mcol = SM_MASK + kt * 8 + 4 + qh
                        P.add("act", lambda e, b0=b0, pi0=pi0, mcol=mcol, qh=qh: e.activation(
                            pTall[:, pi0:pi0 + 2, qh * 256:(qh + 1) * 256], psall[:, b0:b0 + 2, qh * 256:(qh + 1) * 256],
                            AF.Exp, bias=sm[:, mcol:mcol + 1], scale=0.125),
                            reads=["sm"], writes=[pk(b0), pk(b0 + 1), "pT%d" % pi0, "pT%d" % (pi0 + 1)])
                else:
                    mcol = SM_MASK + kt * 8 + qb_
                    P.add("act", lambda e, b0=b0, pi0=pi0, mcol=mcol: e.activation(
                        pTall[:, pi0:pi0 + 2, :], psall[:, b0:b0 + 2, :], AF.Exp, bias=sm[:, mcol:mcol + 1], scale=0.125),
                        reads=["sm"], writes=[pk(b0), pk(b0 + 1), "pT%d" % pi0, "pT%d" % (pi0 + 1)])
                state[j] = pi0

            def stage_C(j):
                h, qb_, kt = items[j]
                par = (h * 4 + qb_) % 2
                pi0 = state.pop(j)
                vkey = ("V%d" % kt) if kt < 16 else "Vctx"
                for m in range(2):
                    pi = pi0 + m
                    P.add("pe", lambda e, pi=pi, m=m, kt=kt, h=h: e.matmul(
                        ps[acc_o[m]][:], lhsT=V[:, kt, h * 128:(h + 1) * 128], rhs=pT[pi],
                        start=(kt == 0), stop=(kt == NKT - 1)),
                        reads=[vkey, "pT%d" % pi], writes=[pk(acc_o[m])])
                P.add("pe", lambda e, pi0=pi0, kt=kt: e.matmul(
                    ps[7][:], lhsT=cbm(CB_ONE), rhs=pT[pi0], start=(kt == 0), stop=(kt == NKT - 1)),
                    reads=["cb", "pT%d" % pi0], writes=[pk(7)])
                ad = accD[par]
                if kt == 0:
                    P.add("dve", lambda e, ad=ad, pi0=pi0: e.tensor_copy(ad, pT[pi0 + 1]),
                          reads=["pT%d" % (pi0 + 1)], writes=["accD%d" % par])
                else:
                    P.add("dve", lambda e, ad=ad, pi0=pi0: e.tensor_tensor(ad, ad, pT[pi0 + 1], ALU.add),
                          reads=["pT%d" % (pi0 + 1)], writes=["accD%d" % par])
                if kt == NKT - 1:
                    P.add("dve", lambda e: e.tensor_copy(rcp[0], ps[7][:]), writes=[pk(7), "rcp0"])
                    P.add("dve", lambda e: e.tensor_copy(tO[0], ps[acc_o[0]][:]), writes=[pk(acc_o[0]), "tO0"])
                    P.add("dve", lambda e: e.tensor_copy(tO[1], ps[acc_o[1]][:]), writes=[pk(acc_o[1]), "tO1"])
                    P.add("dve", lambda e: e.reciprocal(rcp[0], rcp[0]), writes=["rcp0"])
                    state["epi"] = (h, qb_, par)

            def epi_den1():
                h, qb_, par = state["epi"]
                P.add("pe", lambda e, par=par: e.matmul(ps[SPARE][:], lhsT=onesf[:], rhs=accD[par], start=True, stop=True),
                      reads=["onesf", "accD%d" % par], writes=[pk(SPARE)])
                P.add("act", lambda e: e.activation(rcp[1], ps[SPARE][:], AF.Ln), writes=[pk(SPARE), "rcp1"])
                P.add("act", lambda e: e.activation(rcp[1], rcp[1], AF.Exp, scale=-1.0), writes=["rcp1"])

            def epi_comb():
                for m in range(2):
                    P.add("dve", lambda e, m=m: e.tensor_tensor(tO[m], tO[m], rcp[m], ALU.mult),
                          reads=["rcp%d" % m], writes=["tO%d" % m])
                P.add("dve", lambda e: e.scalar_tensor_tensor(oF, tO[1], negLam, tO[0], ALU.mult, ALU.add),
                      reads=["tO0", "tO1", "drv"], writes=["oF"])
                P.add("dve", lambda e: e.tensor_tensor(osq, oF, oF, ALU.mult), reads=["oF"], writes=["osq"])

            def epi_final():
                h, qb_, par = state.pop("epi")
                q0 = qb_ * 512
                P.add("pe", lambda e: e.matmul(ps[SPARE][:], lhsT=cbm(CB_M128), rhs=osq, start=True, stop=True),
                      reads=["cb", "osq"], writes=[pk(SPARE)])
                P.add("act", lambda e: e.activation(NT_L, ps[SPARE][:], AF.Ln, bias=epsc[:, 0:1]),
                      reads=["epsc"], writes=[pk(SPARE), "ntl"])
                P.add("act", lambda e: e.activation(NT_R, NT_L, AF.Exp, scale=-0.5), reads=["ntl"], writes=["ntr"])
                P.add("dve", lambda e, h=h, q0=q0: e.scalar_tensor_tensor(
                    mixA[:, h, q0:q0 + 512], oF, sgl, NT_R, ALU.mult, ALU.mult),
                    reads=["oF", "ntr", "drv"], writes=["mixA%d_%d" % (h, qb_)])

            NI = len(items)
            LA = 2
            modt = {"i": 0}
            j0 = None
            sched = {2: epi_den1, 5: epi_comb, 10: epi_final}
            for j in range(NI + LA):
                if j < NI:
                    stage_A(j)
                if j - LA >= 0:
                    stage_C(j - LA)
                    if items[j - LA][2] == NKT - 1:
                        j0 = j
                if j0 is not None and (j - j0) in sched:
                    sched[j - j0]()
                    if j - j0 == 10:
                        j0 = None
                if l == 0 and j % 17 == 8 and modt["i"] < 18:
                    mod_tile_evac(1, modt["i"], SPARE)
                    modt["i"] += 1
            if l == 0:
                while modt["i"] < 18:
                    mod_tile_evac(1, modt["i"], SPARE)
                    modt["i"] += 1
            wv = wout_d[l].rearrange("(c p) d -> p c d", p=128)
            bo = []
            for hf in range(2):
                b = next_wg()
                bo.append(b)
                P.add("pool", lambda e, b=b, hf=hf: e.dma_start(out=wg[b][:], in_=wv[:, :, hf * 512:(hf + 1) * 512]),
                      writes=["wg%d" % b], dma_slot="wg%d" % b)
            if j0 is not None:
                for d in (2, 5, 10):
                    if d > (NI + LA - 1 - j0):
                        sched[d]()
            if l == 0:
                mod_finish(1, evacuated=True)
            P.barrier()

            yTs = [af32(42, [128, 8, 512]), af32(58, [128, 8, 512])]
            def emit_ssb(sq, sqk, dc, sbank):
                P.add("pe", lambda e: e.matmul(ps[sbank][:], lhsT=cbm(CB_M1024), rhs=sq,
                                               start=(dc == 0), stop=(dc == 7)),
                      reads=[sqk, "cb"], writes=[pk(sbank)])

            for blk in range(4):
                t0 = blk * 512
                par = blk % 2
                yT = yTs[par]
                sbank = 7 - par
                pend = None
                for dc in range(8):
                    b = bo[dc // 4]
                    bank = nbank(0, 6)
                    for cc in range(8):
                        if cc < 4:
                            rhs = mixA[:, cc, t0:t0 + 512]
                            rk = "mixA%d_%d" % (cc, blk)
                        else:
                            rhs = mixC[:, cc - 4, t0:t0 + 512]
                            rk = "mixC%d_%d" % (cc - 4, blk)
                        P.add("pe", lambda e, b=b, bank=bank, cc=cc, dc=dc, rhs=rhs: e.matmul(
                            ps[bank][:], lhsT=wg[b][:, cc, (dc % 4) * 128:(dc % 4 + 1) * 128], rhs=rhs,
                            start=(cc == 0), stop=(cc == 7)),
                            reads=["wg%d" % b, rk], writes=[pk(bank)])
                    P.add("act", lambda e, bank=bank, dc=dc, yT=yT: e.activation(yT[:, dc, :], ps[bank][:], AF.Copy),
                          writes=[pk(bank), "yT%d_%d" % (dc, par)])
                    sq = NT_SQ[dc % 2]
                    sqk = "ntsq%d" % (dc % 2)
                    P.add("dve", lambda e, sq=sq, dc=dc, yT=yT: e.tensor_tensor(sq, yT[:, dc, :], yT[:, dc, :], ALU.mult),
                          reads=["yT%d_%d" % (dc, par)], writes=[sqk])
                    if pend is not None:
                        emit_ssb(*pend)
                    pend = (sq, sqk, dc, sbank)
                    drain(2)
                emit_ssb(*pend)
                assert not pending
                pending.extend(post_closures(l, s, yT, 1, t0, [sbank], lambda c, bl, par=par: "yT%d_%d" % (c, par)))
            pre_ffn2["b"] = ffn_first_tile(l, 1)
            flush()
            P.barrier()

        pre_ffn2 = {"b": None}
        for l in range(DEPTH):
            ffn(l, 0, first_norm_done=(l > 0), tail=(lambda d, l=l: norm_site(l, 1, 0, alt=True, defer=d)),
                down_hook=(mod0_down_hook if l == 0 else None))
            mixer(l, first_norm_done=True, pre_drain=True)
            if l + 1 < DEPTH:
                ffn(l, 1, tail=(lambda d, l=l: norm_site(l + 1, 0, 0, alt=True, defer=d)), pre_tile=pre_ffn2["b"])
            else:
                ffn(l, 1, pre_tile=pre_ffn2["b"])

        ost = [af32(0 + 4 * i, [128, D]) for i in range(2)]
        for tt in range(16):
            if tt == 8:
                flush()
            st = ost[tt % 2]
            sk = "ost%d" % (tt % 2)
            for hf in range(2):
                b = nbank()
                for c4 in range(4):
                    c = hf * 4 + c4
                    P.add("pe", lambda e, b=b, c=c, c4=c4, tt=tt: e.transpose(
                        ps[b][:, c4 * 128:(c4 + 1) * 128], xT[:, c, tt * 128:(tt + 1) * 128], idf[:]),
                        reads=["xT%d_%d" % (c, tt // 4), "idf"], writes=[pk(b)])
                if hf == 0:
                    P.add("act", lambda e, b=b, st=st: e.activation(st[:, 0:512], ps[b][:], AF.Copy),
                          writes=[pk(b), sk + "a"])
                else:
                    P.add("dve", lambda e, b=b, st=st: e.tensor_copy(st[:, 512:1024], ps[b][:]),
                          writes=[pk(b), sk + "b"])
            P.add("sp", lambda e, st=st, tt=tt: e.dma_start(out=y_d[tt * 128:(tt + 1) * 128, :], in_=st),
                  reads=[sk + "a", sk + "b"], writes=[sk + "a", sk + "b"], dma_slot=sk, final=True)
            if tt < 8:
                drain(3)
        flush()
        P.emit(nc, ctx)
    return nc


def _const_tables():
    bf = ml_dtypes.bfloat16
    cbt = np.zeros((128, 6, 128), np.float32)
    cbt[:, CB_M1024] = 1.0 / 1024.0
    cbt[:, CB_M128] = 1.0 / 128.0
    cbt[:, CB_ONE] = 1.0
    perm = np.zeros((128, 128), np.float32)
    for dst in range(128):
        j = dst % 32
        if j < 16:
            perm[dst + 16, dst] = -1.0
        else:
            perm[dst - 16, dst] = 1.0
    cbt[:, CB_PERM] = perm
    c = np.arange(64)
    ang = 2.0 * np.pi * ((c[:, None] * c[None, :]) % 64) / 64.0
    cc = np.zeros((128, 128))
    cs = np.zeros((128, 128))
    for g in range(2):
        cc[g * 64:(g + 1) * 64, g * 64:(g + 1) * 64] = np.cos(ang) / 8.0
        cs[g * 64:(g + 1) * 64, g * 64:(g + 1) * 64] = -np.sin(ang) / 8.0
    cbt[:, CB_CC] = cc
    cbt[:, CB_CS] = cs
    cb = cbt.reshape(128, 6 * 128).astype(bf)

    def dft(n_seq):
        n = np.arange(T)
        pos = n % n_seq
        blk = n // n_seq
        prod = (pos[:, None].astype(np.int64) * pos[None, :].astype(np.int64)) % n_seq
        a = 2.0 * np.pi * prod / n_seq
        same = (blk[:, None] == blk[None, :])
        sc = 1.0 / math.sqrt(n_seq)
        out = np.stack([np.where(same, np.cos(a) * sc, 0.0), np.where(same, np.sin(a) * sc, 0.0)])
        return out.astype(bf)

    p = np.arange(128)
    j = p % 64
    axis = j // 32
    fi = j % 16
    inv = 1.0 / (10000.0 ** (fi.astype(np.float64) / 16.0))
    t = np.arange(T)
    row = (t // 64).astype(np.float64)
    col = (t % 64).astype(np.float64)
    posn = np.where(axis[:, None] == 0, row[None, :], col[None, :])
    ang = posn * inv[:, None]
    rope_s = np.concatenate([np.cos(ang), np.sin(ang)], axis=1).astype(np.float32)
    rope_p = np.concatenate([np.ones((128, T)), np.zeros((128, T))], axis=1).astype(np.float32)

    def cmask(L):
        mL = (t % L != 0).astype(np.float32)
        mR = (t % L != L - 1).astype(np.float32)
        return np.broadcast_to(np.concatenate([mL, mR])[None, :], (128, 2 * T)).astype(bf)

    amask_s = np.zeros((NKT, 8), np.float32)
    amask_p = np.full((NKT, 8), NEG, np.float32)
    for kt in range(16):
        amask_p[kt, kt // 4] = 0.0
        amask_p[kt, 4 + (kt // 2) % 2] = 0.0
    amask_s[:, 4:6] = 0.0
    NEGQ = -240000.0
    mq_s = np.zeros((2, 768), np.float32)
    mq_s[0, 512:640] = 1.0
    mq_s[1, 640:768] = 1.0
    mq_p = mq_s.copy()
    mq_p[0, 256:512] = NEGQ
    mq_p[1, 0:256] = NEGQ
    mq_s = mq_s.astype(bf)
    mq_p = mq_p.astype(bf)
    return dict(cb=cb, dft_s=dft(T), dft_p=dft(256), rope_s=rope_s, rope_p=rope_p,
                cmask_s=cmask(T), cmask_p=cmask(256), amask_s=amask_s, amask_p=amask_p, mq_s=mq_s, mq_p=mq_p)


_CACHE = {}


def kernel(x_prompt, x_sample, cache_k, cache_v, c, c_ctx, w_mod, b_mod, norm_g,
           w_ffn_gu, w_ffn_down, w_in, w_out, conv_w, lam_qk, subln_g):
    f32 = np.float32
    x_prompt = np.asarray(x_prompt, f32)
    x_sample = np.asarray(x_sample, f32)
    cache_k = np.asarray(cache_k, f32)
    cache_v = np.asarray(cache_v, f32)
    c = np.asarray(c, f32)
    c_ctx = np.asarray(c_ctx, f32)
    w_mod = np.ascontiguousarray(np.asarray(w_mod, f32))
    b_mod = np.asarray(b_mod, f32)
    norm_g = np.asarray(norm_g, f32)
    w_ffn_gu = np.ascontiguousarray(np.asarray(w_ffn_gu, f32))
    w_ffn_down = np.ascontiguousarray(np.asarray(w_ffn_down, f32))
    w_in = np.ascontiguousarray(np.asarray(w_in, f32))
    w_out = np.ascontiguousarray(np.asarray(w_out, f32))
    conv_w = np.asarray(conv_w, f32)
    lam_qk = np.asarray(lam_qk, f32)
    subln_g = np.asarray(subln_g, f32)

    if "nc" not in _CACHE:
        _CACHE["nc"] = build_program()
        _CACHE["tab"] = _const_tables()
    nc = _CACHE["nc"]
    tab = _CACHE["tab"]
    idf = np.eye(128, dtype=f32)

    def fm(v):
        return np.ascontiguousarray(v.reshape(8, 128).T)

    in_maps = []
    for core in range(8):
        is_s = core < 4
        sm = np.zeros((128, NSM), f32)
        cvec = c[core] if is_s else c_ctx
        sm[:, SM_CV:SM_CV + 8] = fm(cvec)
        for l in range(DEPTH):
            o = SM_L + l * SM_LSZ
            sm[:, o:o + 72] = b_mod[l].reshape(72, 128).T
            sm[:, o + 72:o + 120] = norm_g[l].reshape(48, 128).T
            sm[:, o + 120:o + 126] = conv_w[l].reshape(3, 2, 128).transpose(2, 1, 0).reshape(128, 6)
            sm[:, o + 126] = subln_g[l]
            sm[0:64, SM_LAM + l * 4:SM_LAM + l * 4 + 4] = lam_qk[l].T
        am = tab["amask_s"] if is_s else tab["amask_p"]
        sm[:, SM_MASK:SM_MASK + 160] = am.reshape(1, 160)
        if is_s:
            xx = x_sample[core]
            ck = cache_k[core].reshape(DEPTH, PAST, 512)
            cvv = cache_v[core].reshape(DEPTH, PAST, 512)
        else:
            xx = x_prompt[(core - 4) * 8:(core - 4) * 8 + 8].reshape(T, D)
            ck = np.zeros((DEPTH, PAST, 512), f32)
            cvv = np.zeros((DEPTH, PAST, 512), f32)
        in_maps.append({
            "x": np.ascontiguousarray(xx), "sm": sm, "cb": tab["cb"], "idf": idf,
            "rope": tab["rope_s"] if is_s else tab["rope_p"],
            "mq": tab["mq_s"] if is_s else tab["mq_p"],
            "cmask": tab["cmask_s"] if is_s else tab["cmask_p"],
            "dft": tab["dft_s"] if is_s else tab["dft_p"],
            "ck": np.ascontiguousarray(ck), "cvv": np.ascontiguousarray(cvv),
            "w_mod": w_mod, "w_ffn_gu": w_ffn_gu, "w_ffn_down": w_ffn_down, "w_in": w_in, "w_out": w_out,
        })
    res = run_bass_kernel_spmd(nc, in_maps, core_ids=list(range(8)))
    r = res.results
    y_sample = np.stack([r[b]["y"] for b in range(4)]).astype(f32)
    y_prompt = np.concatenate([r[4 + i]["y"].reshape(8, 256, D) for i in range(4)]).astype(f32)
    nk = np.concatenate([r[4 + i]["nk"].reshape(DEPTH, 8, 256, 4, 2, 64).transpose(1, 0, 2, 3, 4, 5)
                         for i in range(4)]).astype(f32)
    nv = np.concatenate([r[4 + i]["nv"].reshape(DEPTH, 8, 256, 4, 128).transpose(1, 0, 2, 3, 4)
                         for i in range(4)]).astype(f32)
    return (y_prompt, y_sample, nk, nv)
```

```python
import math
from contextlib import ExitStack

import numpy as np
import ml_dtypes

import concourse.bass as bass
import concourse.mybir as mybir
from concourse.bass_utils import run_bass_kernel_spmd

F32 = mybir.dt.float32
BF16 = mybir.dt.bfloat16
AF = mybir.ActivationFunctionType
ALU = mybir.AluOpType

D = 1024
T = 2048
NCH = 8
DFF = 2816
NF = 22
DEPTH = 2
PAST = 512
NKT = 20
EPS = 1e-6
NEG = -30000.0
ENGS = ("pe", "act", "dve", "pool", "sp")


class Op:
    __slots__ = ("eng", "fn", "deps", "signal", "val", "sem", "is_dma", "slot")

    def __init__(self, eng, fn, is_dma=False, slot=None):
        self.eng = eng
        self.fn = fn
        self.deps = []
        self.signal = False
        self.val = None
        self.sem = None
        self.is_dma = is_dma
        self.slot = slot


class Prog:
    def __init__(self):
        self.ops = {e: [] for e in ENGS}
        self.last_w = {}
        self.readers = {}
        self.all = []
        self.final = []
        self.pending_barrier = {e: None for e in ENGS}
        self.since_barrier_dma = []
        self.last_op = {e: None for e in ENGS}

    def barrier(self):
        deps = [o for o in self.last_op.values() if o is not None] + list(self.since_barrier_dma)
        self.since_barrier_dma = []
        for e in ENGS:
            prev = self.pending_barrier[e] or []
            self.pending_barrier[e] = prev + deps

    def add(self, eng, fn, reads=(), writes=(), dma_slot=None, final=False):
        op = Op(eng, fn, is_dma=dma_slot is not None, slot=dma_slot)
        deps = set()
        for r in reads:
            w = self.last_w.get(r)
            if w is not None:
                deps.add(w)
        for w_ in writes:
            w = self.last_w.get(w_)
            if w is not None:
                deps.add(w)
            for rd in self.readers.get(w_, ()):
                deps.add(rd)
        for r in reads:
            self.readers.setdefault(r, []).append(op)
        for w_ in writes:
            self.last_w[w_] = op
            self.readers[w_] = []
        pb = self.pending_barrier[eng]
        if pb:
            deps.update(pb)
            self.pending_barrier[eng] = None
        deps.discard(op)
        for d in deps:
            if d.eng == "pe" and eng == "pe" and not d.is_dma and not op.is_dma:
                continue
            op.deps.append(d)
            d.signal = True
        if final:
            op.signal = True
            self.final.append(op)
        self.all.append(op)
        self.ops[eng].append(op)
        if op.is_dma:
            self.since_barrier_dma.append(op)
        else:
            self.last_op[eng] = op
        return op

    def emit(self, nc, ctx):
        esem = {e: ctx.enter_context(nc.semaphore("c_" + e)) for e in ENGS}
        slot_sem = {}
        slot_cnt = {}
        cnt = {e: 0 for e in ENGS}
        for op in self.all:
            if op.is_dma:
                if op.slot not in slot_sem:
                    slot_sem[op.slot] = ctx.enter_context(nc.semaphore("d_%d" % len(slot_sem)))
                    slot_cnt[op.slot] = 0
                slot_cnt[op.slot] += 16
                op.sem = slot_sem[op.slot]
                op.val = slot_cnt[op.slot]
            elif op.signal:
                cnt[op.eng] += 1
                op.sem = esem[op.eng]
                op.val = cnt[op.eng]
        final = self.final
        ops = self.ops

        def run(eng_name, eng):
            waited = {}
            for op in ops[eng_name]:
                need = {}
                for d in op.deps:
                    k = id(d.sem)
                    if waited.get(k, 0) >= d.val:
                        continue
                    if k not in need or need[k][1] < d.val:
                        need[k] = (d.sem, d.val)
                for k, (s, v) in need.items():
                    eng.wait_ge(s, v)
                    waited[k] = v
                ins = op.fn(eng)
                if op.is_dma:
                    ins.then_inc(op.sem, 16)
                elif op.signal:
                    ins.then_inc(op.sem, 1)
            if eng_name == "sp":
                for f in final:
                    k = id(f.sem)
                    if waited.get(k, 0) >= f.val:
                        continue
                    eng.wait_ge(f.sem, f.val)
                    waited[k] = f.val

        with nc.Block() as block:
            @block.tensor
            def _(e):
                run("pe", e)

            @block.scalar
            def _(e):
                run("act", e)

            @block.vector
            def _(e):
                run("dve", e)

            @block.gpsimd
            def _(e):
                run("pool", e)

            @block.sync
            def _(e):
                run("sp", e)


SM_CV = 0
SM_L = 8
SM_LSZ = 127
SM_MASK = SM_L + DEPTH * SM_LSZ
SM_LAM = SM_MASK + 160
NSM = SM_LAM + 8

CB_M1024, CB_M128, CB_ONE, CB_PERM, CB_CC, CB_CS = range(6)


def _lambda_init(l):
    return 0.8 - 0.6 * math.exp(-0.3 * l)


def build_program():
    nc = bass.Bass("TRN2", target_bir_lowering=False)

    def din(name, shape, dt=F32):
        return nc.dram_tensor(name, list(shape), dt, kind="ExternalInput").ap()

    def dout(name, shape, dt=F32):
        return nc.dram_tensor(name, list(shape), dt, kind="ExternalOutput").ap()

    x_d = din("x", [T, D])
    sm_d = din("sm", [128, NSM])
    cb_d = din("cb", [128, 6 * 128], BF16)
    idf_d = din("idf", [128, 128])
    mq_d = din("mq", [2, 768], BF16)
    rope_d = din("rope", [128, 2 * T])
    cmask_d = din("cmask", [128, 2 * T], BF16)
    dft_d = din("dft", [2, T, T], BF16)
    ck_d = din("ck", [DEPTH, PAST, 512])
    cvv_d = din("cvv", [DEPTH, PAST, 512])
    wmod_d = din("w_mod", [DEPTH, D, 9 * D])
    wgu_d = din("w_ffn_gu", [DEPTH, 2, D, 2 * DFF])
    wdn_d = din("w_ffn_down", [DEPTH, 2, DFF, D])
    win_d = din("w_in", [DEPTH, D, 2560])
    wout_d = din("w_out", [DEPTH, D, D])
    y_d = dout("y", [T, D])
    nk_d = dout("nk", [DEPTH, T, 512])
    nv_d = dout("nv", [DEPTH, T, 512])

    P = Prog()
    with ExitStack() as ctx:
        sb = lambda name, shape, dt: ctx.enter_context(nc.sbuf_tensor("s_" + name, list(shape), dt))
        xT = sb("xT", [128, NCH, T], F32)
        wg = [sb("wg%d" % i, [128, 8, 512], BF16) for i in range(2)]
        sm = sb("sm", [128, NSM], F32)
        cb = sb("cb", [128, 6 * 128], BF16)
        idf = sb("idf", [128, 128], F32)
        mq = sb("mq", [2, 768], BF16)
        onesf = sb("onesf", [128, 128], F32)
        drv = sb("drv", [128, 160], F32)
        sT = sb("sT", [128, 8], BF16)
        epsc = sb("epsc", [128, 1], F32)
        ARENA_KB = 120
        arena = sb("arena", [128, ARENA_KB * 512], BF16)
        arena_f = arena.bitcast(F32)
        psall = ctx.enter_context(nc.psum_tensor("psall", [128, 8, 512], F32))
        ps = [psall[:, i, :] for i in range(8)]

        def abf(off_kb, shape):
            n = int(np.prod(shape[1:]))
            o = int(off_kb * 512)
            ap = arena[:, o:o + n]
            if len(shape) == 3:
                ap = ap.rearrange("p (a b) -> p a b", a=shape[1])
            return ap

        def af32(off_kb, shape):
            n = int(np.prod(shape[1:]))
            o = int(off_kb * 256)
            ap = arena_f[:, o:o + n]
            if len(shape) == 3:
                ap = ap.rearrange("p (a b) -> p a b", a=shape[1])
            return ap

        def cbm(i):
            return cb[:, i * 128:(i + 1) * 128]

        rot = {"i": 0}

        def nbank(lo=0, hi=6):
            b = lo + rot["i"] % (hi - lo)
            rot["i"] += 1
            return b

        def pk(b):
            return "ps%d" % b

        wgi = {"i": 0}

        def next_wg():
            b = wgi["i"] % 2
            wgi["i"] += 1
            return b

        P.add("sp", lambda e: e.dma_start(out=sm[:], in_=sm_d), writes=["sm"], dma_slot="sm")
        P.add("sp", lambda e: e.dma_start(out=cb[:], in_=cb_d), writes=["cb"], dma_slot="cb")
        P.add("sp", lambda e: e.dma_start(out=idf[:], in_=idf_d), writes=["idf"], dma_slot="idf")
        P.add("sp", lambda e: e.dma_start(out=mq[:], in_=mq_d), writes=["mq"], dma_slot="mq")
        P.add("dve", lambda e: e.memset(onesf[:], 1.0), writes=["onesf"])
        P.add("dve", lambda e: e.memset(epsc[:], EPS), writes=["epsc"])
        P.add("act", lambda e: e.activation(sT[:], sm[:, SM_CV:SM_CV + 8], AF.Silu), reads=["sm"], writes=["sT"])

        xst = [af32(0 + 4 * i, [128, D]) for i in range(2)]

        def xload_tile(tt):
            if tt >= 16:
                return
            st = xst[tt % 2]
            sk = "xst%d" % (tt % 2)
            P.add("sp", lambda e, st=st, tt=tt: e.dma_start(out=st, in_=x_d[tt * 128:(tt + 1) * 128, :]),
                  writes=[sk], dma_slot=sk)
            for hf in range(2):
                b = nbank()
                for c4 in range(4):
                    c = hf * 4 + c4
                    P.add("pe", lambda e, b=b, c=c, c4=c4, st=st: e.transpose(
                        ps[b][:, c4 * 128:(c4 + 1) * 128], st[:, c * 128:(c + 1) * 128], idf[:]),
                        reads=[sk, "idf"], writes=[pk(b)])
                eng = "act" if hf == 0 else "dve"
                outap = xT[:, hf * 4:hf * 4 + 4, tt * 128:(tt + 1) * 128]
                inap = ps[b][:].rearrange("p (a b) -> p a b", a=4)
                if eng == "act":
                    P.add("act", lambda e, o=outap, i=inap: e.activation(o, i, AF.Copy),
                          writes=[pk(b)] + ["xT%d_%d" % (c, tt // 4) for c in range(hf * 4, hf * 4 + 4)])
                else:
                    P.add("dve", lambda e, o=outap, i=inap: e.tensor_copy(o, i),
                          writes=[pk(b)] + ["xT%d_%d" % (c, tt // 4) for c in range(hf * 4, hf * 4 + 4)])

        drv2 = sb("drv2", [128, DEPTH * 48], F32)

        def mod_tile_evac(l, t_, pb):
            base = 80 * l
            wv = wmod_d[l].rearrange("(c p) f -> p c f", p=128)
            b = next_wg()
            P.add("pool", lambda e, b=b, t_=t_: e.dma_start(out=wg[b][:], in_=wv[:, :, t_ * 512:(t_ + 1) * 512]),
                  writes=["wg%d" % b], dma_slot="wg%d" % b)
            for fc in range(4):
                for c in range(8):
                    P.add("pe", lambda e, b=b, fc=fc, c=c: e.matmul(
                        ps[pb][:, fc:fc + 1], lhsT=wg[b][:, c, fc * 128:(fc + 1) * 128], rhs=sT[:, c:c + 1],
                        start=(c == 0), stop=(c == 7)),
                        reads=["wg%d" % b, "sT"], writes=[pk(pb)])
            bm = sm[:, SM_L + l * SM_LSZ + t_ * 4: SM_L + l * SM_LSZ + t_ * 4 + 4]
            P.add("dve", lambda e: e.tensor_tensor(drv[:, base + t_ * 4:base + t_ * 4 + 4], ps[pb][:, 0:4], bm, ALU.add),
                  reads=["sm"], writes=[pk(pb), "drvm%d" % l])

        def mod_tiles(l, ta, tb, hook=None):
            pb = 7
            wv = wmod_d[l].rearrange("(c p) f -> p c f", p=128)
            for t_ in range(ta, tb):
                if hook is not None:
                    hook(t_)
                b = next_wg()
                P.add("pool", lambda e, b=b, t_=t_: e.dma_start(out=wg[b][:], in_=wv[:, :, t_ * 512:(t_ + 1) * 512]),
                      writes=["wg%d" % b], dma_slot="wg%d" % b)
                for fc in range(4):
                    col = t_ * 4 + fc
                    for c in range(8):
                        P.add("pe", lambda e, b=b, fc=fc, c=c, col=col: e.matmul(
                            ps[pb][:, col:col + 1], lhsT=wg[b][:, c, fc * 128:(fc + 1) * 128], rhs=sT[:, c:c + 1],
                            start=(c == 0), stop=(c == 7)),
                            reads=["wg%d" % b, "sT"], writes=[pk(pb)])

        def mod_finish(l, evacuated=False):
            base = 80 * l
            bmod = sm[:, SM_L + l * SM_LSZ: SM_L + l * SM_LSZ + 72]
            ng = lambda i: sm[:, SM_L + l * SM_LSZ + 72 + i * 8: SM_L + l * SM_LSZ + 72 + i * 8 + 8]
            pb = 7
            if not evacuated:
                P.add("dve", lambda e: e.tensor_tensor(drv[:, base:base + 72], ps[pb][:, 0:72], bmod, ALU.add),
                      reads=["sm"], writes=[pk(pb), "drv"])
            else:
                P.add("dve", lambda e: e.memset(drv[:, base + 78:base + 79], 0.0),
                      reads=["drvm%d" % l], writes=["drv"])
            for s in range(3):
                sc = drv[:, base + (3 * s + 1) * 8: base + (3 * s + 1) * 8 + 8]
                gt = drv[:, base + (3 * s + 2) * 8: base + (3 * s + 2) * 8 + 8]
                gs = drv2[:, l * 48 + s * 8: l * 48 + s * 8 + 8]
                gg = drv2[:, l * 48 + 24 + s * 8: l * 48 + 24 + s * 8 + 8]
                wres = 1.0 if s == 1 else 0.5
                P.add("dve", lambda e, sc=sc, gs=gs, s=s: e.scalar_tensor_tensor(
                    gs, sc, 1.0, ng(2 * s), ALU.add, ALU.mult), reads=["drv", "sm"], writes=["drv2"])
                P.add("dve", lambda e, gt=gt, gg=gg, s=s, wres=wres: e.scalar_tensor_tensor(
                    gg, gt, wres, ng(2 * s + 1), ALU.mult, ALU.mult), reads=["drv", "sm"], writes=["drv2"])
            lc = SM_LAM + l * 4
            tmp = drv[0:64, base + 76:base + 78]
            P.add("dve", lambda e: e.tensor_tensor(drv[0:64, base + 76:base + 77], sm[0:64, lc:lc + 1],
                                                  sm[0:64, lc + 1:lc + 2], ALU.mult), reads=["sm"], writes=["drv"])
            P.add("dve", lambda e: e.tensor_tensor(drv[0:64, base + 77:base + 78], sm[0:64, lc + 2:lc + 3],
                                                  sm[0:64, lc + 3:lc + 4], ALU.mult), reads=["sm"], writes=["drv"])
            P.add("pe", lambda e: e.matmul(ps[pb][:, 128:130], lhsT=onesf[0:64, :], rhs=tmp, start=True, stop=True),
                  reads=["drv", "onesf"], writes=[pk(pb)])
            P.add("act", lambda e: e.activation(drv[:, base + 74:base + 76], ps[pb][:, 128:130], AF.Exp),
                  writes=[pk(pb), "drv"])
            P.add("dve", lambda e: e.tensor_tensor(drv[:, base + 72:base + 73], drv[:, base + 74:base + 75],
                                                  drv[:, base + 75:base + 76], ALU.subtract), writes=["drv"])
            li = _lambda_init(l)
            P.add("dve", lambda e: e.tensor_scalar(drv[:, base + 72:base + 73], drv[:, base + 72:base + 73],
                                                  li, -1.0, ALU.add, ALU.mult), writes=["drv"])
            sub = sm[:, SM_L + l * SM_LSZ + 126: SM_L + l * SM_LSZ + 127]
            P.add("dve", lambda e: e.tensor_scalar(drv[:, base + 73:base + 74], sub, 1.0 - li, None, ALU.mult),
                  reads=["sm"], writes=["drv"])

        def mod_part(l, names):
            base = 80 * l
            ng = lambda i: sm[:, SM_L + l * SM_LSZ + 72 + i * 8: SM_L + l * SM_LSZ + 72 + i * 8 + 8]
            P.add("dve", lambda e: e.memset(drv[:, base + 78:base + 79], 0.0),
                  reads=["drvm%d" % l], writes=["drv"])
            for kind, s in names:
                if kind == "gs":
                    sc = drv[:, base + (3 * s + 1) * 8: base + (3 * s + 1) * 8 + 8]
                    gs = drv2[:, l * 48 + s * 8: l * 48 + s * 8 + 8]
                    P.add("dve", lambda e, sc=sc, gs=gs, s=s: e.scalar_tensor_tensor(
                        gs, sc, 1.0, ng(2 * s), ALU.add, ALU.mult), reads=["drv", "sm"], writes=["drv2"])
                else:
                    gt = drv[:, base + (3 * s + 2) * 8: base + (3 * s + 2) * 8 + 8]
                    gg = drv2[:, l * 48 + 24 + s * 8: l * 48 + 24 + s * 8 + 8]
                    wres = 1.0 if s == 1 else 0.5
                    P.add("dve", lambda e, gt=gt, gg=gg, s=s, wres=wres: e.scalar_tensor_tensor(
                        gg, gt, wres, ng(2 * s + 1), ALU.mult, ALU.mult), reads=["drv", "sm"], writes=["drv2"])

        def mod_lambda(l):
            base = 80 * l
            pb = 7
            lc = SM_LAM + l * 4
            tmp = drv[0:64, base + 76:base + 78]
            P.add("dve", lambda e: e.tensor_tensor(drv[0:64, base + 76:base + 77], sm[0:64, lc:lc + 1],
                                                  sm[0:64, lc + 1:lc + 2], ALU.mult), reads=["sm"], writes=["drv"])
            P.add("dve", lambda e: e.tensor_tensor(drv[0:64, base + 77:base + 78], sm[0:64, lc + 2:lc + 3],
                                                  sm[0:64, lc + 3:lc + 4], ALU.mult), reads=["sm"], writes=["drv"])
            P.add("pe", lambda e: e.matmul(ps[pb][:, 128:130], lhsT=onesf[0:64, :], rhs=tmp, start=True, stop=True),
                  reads=["drv", "onesf"], writes=[pk(pb)])
            P.add("act", lambda e: e.activation(drv[:, base + 74:base + 76], ps[pb][:, 128:130], AF.Exp),
                  writes=[pk(pb), "drv"])
            P.add("dve", lambda e: e.tensor_tensor(drv[:, base + 72:base + 73], drv[:, base + 74:base + 75],
                                                  drv[:, base + 75:base + 76], ALU.subtract), writes=["drv"])
            li = _lambda_init(l)
            P.add("dve", lambda e: e.tensor_scalar(drv[:, base + 72:base + 73], drv[:, base + 72:base + 73],
                                                  li, -1.0, ALU.add, ALU.mult), writes=["drv"])
            sub = sm[:, SM_L + l * SM_LSZ + 126: SM_L + l * SM_LSZ + 127]
            P.add("dve", lambda e: e.tensor_scalar(drv[:, base + 73:base + 74], sub, 1.0 - li, None, ALU.mult),
                  reads=["sm"], writes=["drv"])

        for t_ in range(4):
            for k_ in range(4):
                xload_tile(4 * t_ + k_)
            mod_tile_evac(0, t_, 7)
        mod_part(0, [("gs", 0)])
        mod_lambda(0)
        P.barrier()

        def mod0_down_hook(half, dc):
            t_ = 4 + half * 8 + dc
            if t_ < 18:
                mod_tile_evac(0, t_, nbank(0, 6))
            if half == 0 and dc == 1:
                mod_part(0, [("gg", 0)])
            if half == 0 and dc == 7:
                mod_part(0, [("gs", 1), ("gg", 1)])
            if half == 1 and dc == 5:
                mod_part(0, [("gs", 2), ("gg", 2)])

        def mod_cols(l, j):
            return drv[:, 80 * l + j * 8: 80 * l + j * 8 + 8]

        hT = abf(0, [128, 8, 1024])
        NT_SQ = [abf(16 + i, [128, 512]) for i in range(2)]
        NT_R = af32(18, [128, 512])
        NT_L = af32(20, [128, 512])
        NT_T = [af32(22 + 2 * i, [128, 512]) for i in range(2)]

        NT_SQ2 = [abf(113 + i, [128, 512]) for i in range(4)]
        pending = []

        def drain(n=1):
            for _ in range(n):
                if pending:
                    pending.pop(0)()

        def flush():
            while pending:
                pending.pop(0)()

        def norm_site(l, s, half, alt=False, defer=None):
            gs = drv2[:, l * 48 + s * 8: l * 48 + s * 8 + 8]
            sh = mod_cols(l, 3 * s)
            sqb = NT_SQ2 if alt else NT_SQ
            sqn = "ntsqb%d" if alt else "ntsq%d"
            nbanks = [nbank(), nbank()]
            for bl in range(2):
                t0 = half * 1024 + bl * 512
                gb_ = t0 // 512
                b = nbanks[bl]
                for c in range(8):
                    sq = sqb[c % len(sqb)]
                    sqk = sqn % (c % len(sqb))
                    if c % 2 == 0:
                        P.add("act", lambda e, sq=sq, c=c, t0=t0: e.activation(sq, xT[:, c, t0:t0 + 512], AF.Square),
                              reads=["xT%d_%d" % (c, gb_)], writes=[sqk])
                    else:
                        P.add("dve", lambda e, sq=sq, c=c, t0=t0: e.tensor_tensor(
                            sq, xT[:, c, t0:t0 + 512], xT[:, c, t0:t0 + 512], ALU.mult),
                            reads=["xT%d_%d" % (c, gb_)], writes=[sqk])
                    P.add("pe", lambda e, b=b, sq=sq, c=c: e.matmul(ps[b][:], lhsT=cbm(CB_M1024), rhs=sq,
                                                                   start=(c == 0), stop=(c == 7)),
                          reads=[sqk, "cb"], writes=[pk(b)])
            pieces = []
            for bl in range(2):
                t0 = half * 1024 + bl * 512
                gb_ = t0 // 512
                b = nbanks[bl]

                rbuf = NT_R if bl == 0 else NT_L
                rkey = "ntr" if bl == 0 else "ntl"
                P.add("act", lambda e, b=b: e.activation(NT_L, ps[b][:], AF.Ln, bias=epsc[:, 0:1]),
                      reads=["epsc"], writes=[pk(b), "ntl"])
                P.add("act", lambda e, rbuf=rbuf: e.activation(rbuf, NT_L, AF.Exp, scale=-0.5),
                      reads=["ntl"], writes=[rkey])
                for c in range(8):
                    def body(c=c, bl=bl, t0=t0, gb_=gb_, rbuf=rbuf, rkey=rkey):
                        tt_ = NT_T[c % 2]
                        tk = "ntt%d" % (c % 2)
                        P.add("dve", lambda e, tt_=tt_, c=c, t0=t0: e.scalar_tensor_tensor(
                            tt_, xT[:, c, t0:t0 + 512], gs[:, c:c + 1], rbuf, ALU.mult, ALU.mult),
                            reads=["xT%d_%d" % (c, gb_), rkey, "drv2"], writes=[tk])
                        P.add("act", lambda e, tt_=tt_, c=c, bl=bl: e.activation(
                            hT[:, c, bl * 512:(bl + 1) * 512], tt_, AF.Identity, bias=sh[:, c:c + 1]),
                            reads=[tk, "drv"], writes=["hT%d_%d" % (c, bl)])
                    pieces.append(body)
            if defer is None:
                for f in pieces:
                    f()
            else:
                defer.extend(pieces)

        def post_closures(l, s, yT, nblk, tok0, ssb, ykeys):
            gg = drv2[:, l * 48 + 24 + s * 8: l * 48 + 24 + s * 8 + 8]
            out = []
            for bl in range(nblk):
                t0 = tok0 + bl * 512
                gb_ = t0 // 512
                b = ssb[bl]

                def head(b=b):
                    P.add("act", lambda e, b=b: e.activation(NT_L, ps[b][:], AF.Ln, bias=epsc[:, 0:1]),
                          reads=["epsc"], writes=[pk(b), "ntl"])
                    P.add("act", lambda e: e.activation(NT_R, NT_L, AF.Exp, scale=-0.5), reads=["ntl"], writes=["ntr"])
                out.append(head)
                for c in range(8):
                    def body(c=c, bl=bl, t0=t0, gb_=gb_):
                        tt_ = NT_T[c % 2]
                        tk = "ntt%d" % (c % 2)
                        P.add("dve", lambda e, tt_=tt_, c=c, bl=bl: e.scalar_tensor_tensor(
                            tt_, yT[:, c, bl * 512:(bl + 1) * 512], gg[:, c:c + 1], NT_R, ALU.mult, ALU.mult),
                            reads=[ykeys(c, bl), "ntr", "drv2"], writes=[tk])
                        P.add("dve", lambda e, tt_=tt_, c=c, t0=t0: e.tensor_tensor(
                            xT[:, c, t0:t0 + 512], xT[:, c, t0:t0 + 512], tt_, ALU.add),
                            reads=[tk], writes=["xT%d_%d" % (c, gb_)])
                    out.append(body)
            return out

        def post_site(l, s, yT, nblk, tok0, ssb, ykeys):
            for f in post_closures(l, s, yT, nblk, tok0, ssb, ykeys):
                f()

        def ffn_first_tile(l, i):
            wgu = wgu_d[l, i].rearrange("(c p) f -> p c f", p=128)
            b = next_wg()
            P.add("pool", lambda e, b=b: e.dma_start(out=wg[b][:, :, 0:256], in_=wgu[:, :, 0:256]),
                  writes=["wg%d" % b], dma_slot="wg%da" % b)
            P.add("pool", lambda e, b=b: e.dma_start(out=wg[b][:, :, 256:512], in_=wgu[:, :, DFF:DFF + 256]),
                  writes=["wg%d" % b], dma_slot="wg%db" % b)
            return b

        def ffn(l, i, first_norm_done=False, tail=None, down_hook=None, pre_tile=None):
            s = 0 if i == 0 else 2
            actT = abf(26, [128, NF, 1024])
            wd = [abf(70 + 5.5 * k, [128, NF, 128]) for k in range(2)]
            yT = af32(81, [128, 8, 1024])
            sg = [af32(113 + 2 * k, [128, 512]) for k in range(2)]
            wgu = wgu_d[l, i].rearrange("(c p) f -> p c f", p=128)
            wdn = wdn_d[l, i].rearrange("(f p) d -> p f d", p=128)
            if not first_norm_done:
                norm_site(l, s, 0)
            for half in range(2):
                for j in range(11):
                    if half == 0 and j == 0 and pre_tile is not None:
                        b = pre_tile
                    else:
                        b = next_wg()
                        P.add("pool", lambda e, b=b, j=j: e.dma_start(out=wg[b][:, :, 0:256], in_=wgu[:, :, j * 256:(j + 1) * 256]),
                              writes=["wg%d" % b], dma_slot="wg%da" % b)
                        P.add("pool", lambda e, b=b, j=j: e.dma_start(out=wg[b][:, :, 256:512],
                                                                     in_=wgu[:, :, DFF + j * 256:DFF + (j + 1) * 256]),
                              writes=["wg%d" % b], dma_slot="wg%db" % b)
                    for fcl in range(2):
                        fc = 2 * j + fcl
                        for bl in range(2):
                            bg = nbank(0, 6)
                            bu = nbank(0, 6)
                            for c in range(8):
                                P.add("pe", lambda e, b=b, bg=bg, c=c, fcl=fcl, bl=bl: e.matmul(
                                    ps[bg][:], lhsT=wg[b][:, c, fcl * 128:(fcl + 1) * 128],
                                    rhs=hT[:, c, bl * 512:(bl + 1) * 512], start=(c == 0), stop=(c == 7)),
                                    reads=["wg%d" % b, "hT%d_%d" % (c, bl)], writes=[pk(bg)])
                            for c in range(8):
                                P.add("pe", lambda e, b=b, bu=bu, c=c, fcl=fcl, bl=bl: e.matmul(
                                    ps[bu][:], lhsT=wg[b][:, c, 256 + fcl * 128:256 + (fcl + 1) * 128],
                                    rhs=hT[:, c, bl * 512:(bl + 1) * 512], start=(c == 0), stop=(c == 7)),
                                    reads=["wg%d" % b, "hT%d_%d" % (c, bl)], writes=[pk(bu)])
                            k = (fcl * 2 + bl) % 2
                            P.add("act", lambda e, k=k, bg=bg: e.activation(sg[k], ps[bg][:], AF.Silu),
                                  writes=[pk(bg), "sg%d" % k])
                            P.add("dve", lambda e, k=k, bu=bu, fc=fc, bl=bl: e.tensor_tensor(
                                actT[:, fc, bl * 512:(bl + 1) * 512], sg[k], ps[bu][:], ALU.mult),
                                reads=["sg%d" % k], writes=[pk(bu), "actT%d_%d" % (fc, bl)])
                            drain(1)
                assert not pending
                hoist = (lambda d: norm_site(l, s, 1, alt=True, defer=d)) if half == 0 else tail
                dpend = []
                ssb = [6, 7]
                pend = None

                def emit_ss(sq, sqk, dc, bl):
                    P.add("pe", lambda e: e.matmul(
                        ps[6 + bl][:], lhsT=cbm(CB_M1024), rhs=sq, start=(dc == 0), stop=(dc == 7)),
                        reads=[sqk, "cb"], writes=[pk(6 + bl)])

                for dc in range(8):
                    k = dc % 2
                    P.add("pool", lambda e, k=k, dc=dc: e.dma_start(out=wd[k], in_=wdn[:, :, dc * 128:(dc + 1) * 128]),
                          writes=["wd%d" % k], dma_slot="wd%d" % k)
                    for bl in range(2):
                        b = nbank(0, 6)
                        for f in range(NF):
                            P.add("pe", lambda e, k=k, b=b, f=f, bl=bl: e.matmul(
                                ps[b][:], lhsT=wd[k][:, f, :], rhs=actT[:, f, bl * 512:(bl + 1) * 512],
                                start=(f == 0), stop=(f == NF - 1)),
                                reads=["wd%d" % k, "actT%d_%d" % (f, bl)], writes=[pk(b)])
                        P.add("act", lambda e, b=b, dc=dc, bl=bl: e.activation(
                            yT[:, dc, bl * 512:(bl + 1) * 512], ps[b][:], AF.Copy),
                            writes=[pk(b), "yT%d_%d" % (dc, bl)])
                        sq = NT_SQ[(dc * 2 + bl) % 2]
                        sqk = "ntsq%d" % ((dc * 2 + bl) % 2)
                        P.add("dve", lambda e, sq=sq, dc=dc, bl=bl: e.tensor_tensor(
                            sq, yT[:, dc, bl * 512:(bl + 1) * 512], yT[:, dc, bl * 512:(bl + 1) * 512], ALU.mult),
                            reads=["yT%d_%d" % (dc, bl)], writes=[sqk])
                        if pend is not None:
                            emit_ss(*pend)
                        pend = (sq, sqk, dc, bl)
                    if dc == 0 and hoist is not None:
                        hoist(dpend)
                    for _ in range(3):
                        if dpend:
                            dpend.pop(0)()
                    if down_hook is not None:
                        down_hook(half, dc)
                emit_ss(*pend)
                while dpend:
                    dpend.pop(0)()
                pending.extend(post_closures(l, s, yT, 2, half * 1024, ssb, lambda c, bl: "yT%d_%d" % (c, bl)))

        def mixer(l, first_norm_done=False, pre_drain=False):
            s = 1
            winv = win_d[l].rearrange("(c p) f -> p c f", p=128)
            mixC = abf(26, [128, 4, T])
            gbT = abf(42, [128, 2, T])
            gcT = af32(50, [128, 2, T])
            uT = af32(66, [128, 2, T + 2])
            fT = abf(83, [128, 16, 256])
            aT = [[abf(114 + (cs * 2 + ch), [128, 512]) for ch in range(2)] for cs in range(2)]
            cmk = abf(95, [128, 2, T])
            cA = af32(103, [128, 512])
            cB = af32(105, [128, 512])

            def load_win(tile_idx):
                b = next_wg()
                P.add("pool", lambda e, b=b: e.dma_start(out=wg[b][:], in_=winv[:, :, tile_idx * 512:(tile_idx + 1) * 512]),
                      writes=["wg%d" % b], dma_slot="wg%d" % b)
                return b

            def proj_fm(b, col0, bl, bank):
                for c in range(8):
                    P.add("pe", lambda e, c=c: e.matmul(ps[bank][:], lhsT=wg[b][:, c, col0:col0 + 128],
                                                        rhs=hT[:, c, bl * 512:(bl + 1) * 512],
                                                        start=(c == 0), stop=(c == 7)),
                          reads=["wg%d" % b, "hT%d_%d" % (c, bl)], writes=[pk(bank)])

            def proj_tm(b, col0, ncol, ttl, bank):
                for c in range(8):
                    P.add("pe", lambda e, c=c: e.matmul(ps[bank][:, 0:ncol], lhsT=hT[:, c, ttl * 128:(ttl + 1) * 128],
                                                        rhs=wg[b][:, c, col0:col0 + ncol],
                                                        start=(c == 0), stop=(c == 7)),
                          reads=["wg%d" % b, "hT%d_%d" % (c, ttl // 4)], writes=[pk(bank)])

            def t3_proj(half, b3=None):
                if b3 is None:
                    b3 = load_win(3)
                for ch4 in range(4):
                    for bl in range(2):
                        t0 = half * 1024 + bl * 512
                        bank = nbank()
                        proj_fm(b3, ch4 * 128, bl, bank)
                        if ch4 < 2:
                            P.add("act", lambda e, bank=bank, ch4=ch4, t0=t0: e.activation(
                                gbT[:, ch4, t0:t0 + 512], ps[bank][:], AF.Copy),
                                writes=[pk(bank), "gbT%d_%d" % (ch4, t0 // 512)])
                        else:
                            P.add("act", lambda e, bank=bank, ch4=ch4, t0=t0: e.activation(
                                gcT[:, ch4 - 2, t0:t0 + 512], ps[bank][:], AF.Copy),
                                writes=[pk(bank), "gcT%d_%d" % (ch4 - 2, t0 // 512)])
                        drain(3)

            pre_b4 = None
            if pre_drain:
                pre_b3 = load_win(3)
                pre_b4 = load_win(4)
                P.barrier()
                t3_proj(0, pre_b3)
                flush()
                P.barrier()
            P.add("sp", lambda e: e.dma_start(out=cmk, in_=cmask_d.rearrange("p (a b) -> p a b", a=2)),
                  writes=["cmk"], dma_slot="cmk")
            for cc in range(2):
                P.add("dve", lambda e, cc=cc: e.memset(uT[:, cc, 0:1], 0.0), writes=["uTpad"])
                P.add("dve", lambda e, cc=cc: e.memset(uT[:, cc, T + 1:T + 2], 0.0), writes=["uTpad"])
            for half in range(2):
                if not (half == 0 and first_norm_done):
                    norm_site(l, s, half)
                if not (half == 0 and pre_drain):
                    t3_proj(half)
                b4 = pre_b4 if (half == 0 and pre_b4 is not None) else load_win(4)
                for cc in range(2):
                    for bl in range(2):
                        t0 = half * 1024 + bl * 512
                        bank = nbank()
                        proj_fm(b4, cc * 128, bl, bank)
                        P.add("dve", lambda e, bank=bank, cc=cc, t0=t0: e.tensor_tensor(
                            uT[:, cc, 1 + t0:1 + t0 + 512], ps[bank][:], gcT[:, cc, t0:t0 + 512], ALU.mult),
                            reads=["gcT%d_%d" % (cc, t0 // 512)], writes=[pk(bank), "uT%d_%d" % (cc, t0 // 512)])
                for ttl in range(8):
                    tt = half * 8 + ttl
                    bank = nbank()
                    proj_tm(b4, 256, 256, ttl, bank)
                    P.add("act", lambda e, bank=bank, tt=tt: e.activation(fT[:, tt, :], ps[bank][:, 0:256], AF.Copy),
                          writes=[pk(bank), "fT%d" % tt])
            cw = lambda cc, k: sm[:, SM_L + l * SM_LSZ + 120 + cc * 3 + k: SM_L + l * SM_LSZ + 120 + cc * 3 + k + 1]
            for cc in range(2):
                for blk in range(4):
                    t0 = blk * 512
                    ukeys = ["uT%d_%d" % (cc, g) for g in range(max(0, blk - 1), min(3, blk + 1) + 1)] + ["uTpad"]
                    P.add("dve", lambda e, cc=cc, t0=t0: e.tensor_scalar(
                        cA, uT[:, cc, 1 + t0:1 + t0 + 512], cw(cc, 1), None, ALU.mult),
                        reads=ukeys + ["sm"], writes=["cA"])
                    P.add("dve", lambda e, cc=cc, t0=t0: e.tensor_tensor(
                        cB, uT[:, cc, t0:t0 + 512], cmk[:, 0, t0:t0 + 512], ALU.mult),
                        reads=ukeys + ["cmk"], writes=["cB"])
                    P.add("dve", lambda e, cc=cc: e.scalar_tensor_tensor(
                        cA, cB, cw(cc, 0), cA, ALU.mult, ALU.add), reads=["cB", "sm"], writes=["cA"])
                    P.add("dve", lambda e, cc=cc, t0=t0: e.tensor_tensor(
                        cB, uT[:, cc, 2 + t0:2 + t0 + 512], cmk[:, 1, t0:t0 + 512], ALU.mult),
                        reads=ukeys + ["cmk"], writes=["cB"])
                    P.add("dve", lambda e, cc=cc: e.scalar_tensor_tensor(
                        cA, cB, cw(cc, 2), cA, ALU.mult, ALU.add), reads=["cB", "sm"], writes=["cA"])
                    P.add("dve", lambda e, cc=cc, t0=t0: e.tensor_tensor(
                        mixC[:, cc, t0:t0 + 512], cA, gbT[:, cc, t0:t0 + 512], ALU.mult),
                        reads=["cA", "gbT%d_%d" % (cc, blk)], writes=["mixC%d_%d" % (cc, blk)])
            kT = abf(42, [128, 4, T + PAST])
            V = abf(78, [128, NKT, 512])
            rope = af32(98, [128, 2, T])
            ckst = af32(62, [128, 4, 512])
            convdone = ["mixC0_3", "mixC1_3"]
            dv = [dft_d[cs].rearrange("(c p) n -> p c n", p=128) for cs in range(2)]
            for nb in range(4):
                if nb == 2:
                    norm_site(l, s, 0)
                for cs in range(2):
                    banks = [nbank(), nbank()]
                    for hf in range(2):
                        b = next_wg()
                        P.add("sp", lambda e, b=b, cs=cs, hf=hf, nb=nb: e.dma_start(
                            out=wg[b][:], in_=dv[cs][:, hf * 8:(hf + 1) * 8, nb * 512:(nb + 1) * 512]),
                            writes=["wg%d" % b], dma_slot="wgsp%d" % b)
                        for ch in range(2):
                            for c in range(8):
                                P.add("pe", lambda e, b=b, ch=ch, c=c, hf=hf, bk=banks[ch]: e.matmul(
                                    ps[bk][:], lhsT=fT[:, hf * 8 + c, ch * 128:(ch + 1) * 128], rhs=wg[b][:, c, :],
                                    start=(hf == 0 and c == 0), stop=(hf == 1 and c == 7)),
                                    reads=["wg%d" % b, "fT%d" % (hf * 8 + c)], writes=[pk(banks[ch])])
                    for ch in range(2):
                        P.add("act", lambda e, cs=cs, ch=ch, bk=banks[ch]: e.activation(aT[cs][ch], ps[bk][:], AF.Copy),
                              writes=[pk(banks[ch]), "aT%d_%d" % (cs, ch)])
                for ch in range(2):
                    bank = nbank()
                    P.add("pe", lambda e, ch=ch, bank=bank: e.matmul(ps[bank][:], lhsT=cbm(CB_CC), rhs=aT[0][ch],
                                                                     start=True, stop=False),
                          reads=["cb", "aT0_%d" % ch], writes=[pk(bank)])
                    P.add("pe", lambda e, ch=ch, bank=bank: e.matmul(ps[bank][:], lhsT=cbm(CB_CS), rhs=aT[1][ch],
                                                                     start=False, stop=True),
                          reads=["cb", "aT1_%d" % ch], writes=[pk(bank)])
                    P.add("act", lambda e, ch=ch, bank=bank, nb=nb: e.activation(
                        mixC[:, 2 + ch, nb * 512:(nb + 1) * 512], ps[bank][:], AF.Copy),
                        writes=[pk(bank), "mixC%d_%d" % (2 + ch, nb)])
            P.add("sp", lambda e: e.dma_start(out=rope, in_=rope_d.rearrange("p (a b) -> p a b", a=2)),
                  reads=convdone, writes=["rope"], dma_slot="rope")
            P.add("pool", lambda e: e.dma_start(out=V[:, 16:20, :], in_=cvv_d[l].rearrange("(t p) e -> p t e", p=128)),
                  reads=convdone, writes=["Vctx"], dma_slot="Vctx")
            P.add("sp", lambda e: e.dma_start(out=ckst, in_=ck_d[l].rearrange("(t p) e -> p t e", p=128)),
                  reads=convdone, writes=["ckst"], dma_slot="ckst")
            pre_qk = [load_win(0), load_win(1)]
            for h in range(4):
                bank = nbank()
                for pt in range(4):
                    P.add("pe", lambda e, h=h, pt=pt, bank=bank: e.transpose(
                        ps[bank][:, pt * 128:(pt + 1) * 128], ckst[:, pt, h * 128:(h + 1) * 128], idf[:]),
                        reads=["ckst", "idf"], writes=[pk(bank)])
                P.add("act", lambda e, h=h, bank=bank: e.activation(kT[:, h, T:T + PAST], ps[bank][:], AF.Copy),
                      reads=convdone, writes=[pk(bank), "kTctx%d" % h])
            P.barrier()

            qT = abf(62, [128, 4, T])
            qb16 = [abf(114 + k, [128, 512]) for k in range(2)]
            stg = [af32(116, [128, 512]), af32(118, [128, 512])]
            stgi = {"i": 0}
            for half in range(2):
                if half == 1:
                    norm_site(l, s, half)
                for which in range(2):
                    bq = pre_qk[which] if half == 0 else load_win(which)
                    dst = qT if which == 0 else kT
                    prev = None
                    for h in range(4):
                        for bl in range(2):
                            t0 = half * 1024 + bl * 512
                            bank = nbank()
                            proj_fm(bq, h * 128, bl, bank)
                            qb = qb16[(h * 2 + bl) % 2]
                            qk_ = "qb16_%d" % ((h * 2 + bl) % 2)
                            P.add("act", lambda e, qb=qb, bank=bank: e.activation(qb, ps[bank][:], AF.Copy),
                                  writes=[pk(bank), qk_])

                            def rope_part(qb=qb, qk_=qk_, bank=bank, t0=t0, h=h, dst=dst, which=which):
                                bank2 = nbank()
                                P.add("pe", lambda e, qb=qb, bank2=bank2: e.matmul(ps[bank2][:], lhsT=cbm(CB_PERM), rhs=qb,
                                                                                   start=True, stop=True),
                                      reads=["cb", qk_], writes=[pk(bank2)])
                                t1 = NT_T[0]
                                t2 = NT_T[1]
                                P.add("dve", lambda e, bank=bank, t0=t0: e.tensor_tensor(
                                    t1, ps[bank][:], rope[:, 0, t0:t0 + 512], ALU.mult),
                                    reads=["rope"], writes=[pk(bank), "ntt0"])
                                P.add("dve", lambda e, bank2=bank2, t0=t0: e.tensor_tensor(
                                    t2, ps[bank2][:], rope[:, 1, t0:t0 + 512], ALU.mult),
                                    reads=["rope"], writes=[pk(bank2), "ntt1"])
                                P.add("dve", lambda e, dst=dst, h=h, t0=t0: e.tensor_tensor(
                                    dst[:, h, t0:t0 + 512], t1, t2, ALU.add),
                                    reads=["ntt0", "ntt1"],
                                    writes=["%s%d_%d" % ("qT" if which == 0 else "kT", h, t0 // 512)])

                            if prev is not None:
                                prev()
                            prev = rope_part
                    prev()
                    if which == 1:
                        for ttl in range(8):
                            tt = half * 8 + ttl
                            bank = nbank()
                            proj_tm(bq, 0, 512, ttl, bank)
                            si = stgi["i"] % 2
                            stgi["i"] += 1
                            P.add("act", lambda e, bank=bank, si=si: e.activation(stg[si], ps[bank][:], AF.Copy),
                                  writes=[pk(bank), "stg%d" % si])
                            P.add("sp", lambda e, tt=tt, si=si: e.dma_start(out=nk_d[l, tt * 128:(tt + 1) * 128, :], in_=stg[si]),
                                  reads=["stg%d" % si], dma_slot="ostg%d" % si, final=True)
                bv = load_win(2)
                for ttl in range(8):
                    tt = half * 8 + ttl
                    bank = nbank()
                    proj_tm(bv, 0, 512, ttl, bank)
                    si = stgi["i"] % 2
                    stgi["i"] += 1
                    P.add("act", lambda e, bank=bank, si=si: e.activation(stg[si], ps[bank][:], AF.Copy),
                          writes=[pk(bank), "stg%d" % si])
                    P.add("dve", lambda e, tt=tt, si=si: e.tensor_copy(V[:, tt, :], stg[si]),
                          reads=["stg%d" % si], writes=["V%d" % tt])
                    P.add("sp", lambda e, tt=tt, si=si: e.dma_start(out=nv_d[l, tt * 128:(tt + 1) * 128, :], in_=stg[si]),
                          reads=["stg%d" % si], dma_slot="ostg%d" % si, final=True)
            P.barrier()

            mixA = abf(0, [128, 4, T])
            NPT = 6
            pTall = abf(98, [128, NPT, 512])
            pT = [pTall[:, k, :] for k in range(NPT)]
            rcp = [af32(104 + 2 * k, [128, 512]) for k in range(2)]
            tO = [af32(108 + 2 * k, [128, 512]) for k in range(2)]
            oF = af32(112, [128, 512])
            osq = abf(114, [128, 512])
            accD = [af32(115, [128, 512]), af32(117, [128, 512])]
            negLam = drv[:, 80 * l + 72: 80 * l + 73]
            sgl = drv[:, 80 * l + 73: 80 * l + 74]
            items = [(h, qb_, kt) for h in range(4) for qb_ in range(4) for kt in range(NKT)]
            state = {}
            acc_o = [5, 6]
            SPARE = 4

            def stage_A(j):
                h, qb_, kt = items[j]
                q0 = qb_ * 512
                diag = (kt < 16 and kt // 4 == qb_)
                banks = [(2 * j) % 4, (2 * j + 1) % 4]
                kkey = ("kT%d_%d" % (h, kt // 4)) if kt < 16 else ("kTctx%d" % h)
                for m in range(2):
                    bank = banks[m]
                    P.add("pe", lambda e, bank=bank, h=h, kt=kt, m=m, q0=q0: e.matmul(
                        ps[bank][:], lhsT=kT[m * 64:(m + 1) * 64, h, kt * 128:(kt + 1) * 128],
                        rhs=qT[m * 64:(m + 1) * 64, h, q0:q0 + 512], start=True, stop=True),
                        reads=[kkey, "qT%d_%d" % (h, qb_)], writes=[pk(bank)])
                b0 = banks[0]
                pi0 = (2 * j) % NPT
                if diag:
                    for qh in range(2):
                        mcol = SM_MASK + kt * 8 + 4 + qh
                        P.add("act", lambda e, b0=b0, pi0=pi0, mcol=mcol, qh=qh: e.activation(
                            pTall[:, pi0:pi0 + 2, qh * 256:(qh + 1) * 256], psall[:, b0:b0 + 2, qh * 256:(qh + 1) * 256],
                            AF.Exp, bias=sm[:, mcol:mcol + 1], scale=0.125),
                            reads=["sm"], writes=[pk(b0), pk(b0 + 1), "pT%d" % pi0, "pT%d" % (pi0 + 1)])
                else:
                    mcol = SM_MASK + kt * 8 + qb_
                    P.add("act", lambda e, b0=b0, pi0=pi0, mcol=mcol: e.activation(
                        pTall[:, pi0:pi0 + 2, :], psall[:, b0:b0 + 2, :], AF.Exp, bias=sm[:, mcol:mcol + 1], scale=0.125),
                        reads=["sm"], writes=[pk(b0), pk(b0 + 1), "pT%d" % pi0, "pT%d" % (pi0 + 1)])
                state[j] = pi0

            def stage_C(j):
                h, qb_, kt = items[j]
                par = (h * 4 + qb_) % 2
                pi0 = state.pop(j)
                vkey = ("V%d" % kt) if kt < 16 else "Vctx"
                for m in range(2):
                    pi = pi0 + m
                    P.add("pe", lambda e, pi=pi, m=m, kt=kt, h=h: e.matmul(
                        ps[acc_o[m]][:], lhsT=V[:, kt, h * 128:(h + 1) * 128], rhs=pT[pi],
                        start=(kt == 0), stop=(kt == NKT - 1)),
                        reads=[vkey, "pT%d" % pi], writes=[pk(acc_o[m])])
                P.add("pe", lambda e, pi0=pi0, kt=kt: e.matmul(
                    ps[7][:], lhsT=cbm(CB_ONE), rhs=pT[pi0], start=(kt == 0), stop=(kt == NKT - 1)),
                    reads=["cb", "pT%d" % pi0], writes=[pk(7)])
                ad = accD[par]
                if kt == 0:
                    P.add("dve", lambda e, ad=ad, pi0=pi0: e.tensor_copy(ad, pT[pi0 + 1]),
                          reads=["pT%d" % (pi0 + 1)], writes=["accD%d" % par])
                else:
                    P.add("dve", lambda e, ad=ad, pi0=pi0: e.tensor_tensor(ad, ad, pT[pi0 + 1], ALU.add),
                          reads=["pT%d" % (pi0 + 1)], writes=["accD%d" % par])
                if kt == NKT - 1:
                    P.add("dve", lambda e: e.tensor_copy(rcp[0], ps[7][:]), writes=[pk(7), "rcp0"])
                    P.add("dve", lambda e: e.tensor_copy(tO[0], ps[acc_o[0]][:]), writes=[pk(acc_o[0]), "tO0"])
                    P.add("dve", lambda e: e.tensor_copy(tO[1], ps[acc_o[1]][:]), writes=[pk(acc_o[1]), "tO1"])
                    P.add("dve", lambda e: e.reciprocal(rcp[0], rcp[0]), writes=["rcp0"])
                    state["epi"] = (h, qb_, par)

            def epi_den1():
                h, qb_, par = state["epi"]
                P.add("pe", lambda e, par=par: e.matmul(ps[SPARE][:], lhsT=onesf[:], rhs=accD[par], start=True, stop=True),
                      reads=["onesf", "accD%d" % par], writes=[pk(SPARE)])
                P.add("act", lambda e: e.activation(rcp[1], ps[SPARE][:], AF.Ln), writes=[pk(SPARE), "rcp1"])
                P.add("act", lambda e: e.activation(rcp[1], rcp[1], AF.Exp, scale=-1.0), writes=["rcp1"])

            def epi_comb():
                for m in range(2):
                    P.add("dve", lambda e, m=m: e.tensor_tensor(tO[m], tO[m], rcp[m], ALU.mult),
                          reads=["rcp%d" % m], writes=["tO%d" % m])
                P.add("dve", lambda e: e.scalar_tensor_tensor(oF, tO[1], negLam, tO[0], ALU.mult, ALU.add),
                      reads=["tO0", "tO1", "drv"], writes=["oF"])
                P.add("dve", lambda e: e.tensor_tensor(osq, oF, oF, ALU.mult), reads=["oF"], writes=["osq"])

            def epi_final():
                h, qb_, par = state.pop("epi")
                q0 = qb_ * 512
                P.add("pe", lambda e: e.matmul(ps[SPARE][:], lhsT=cbm(CB_M128), rhs=osq, start=True, stop=True),
                      reads=["cb", "osq"], writes=[pk(SPARE)])
                P.add("act", lambda e: e.activation(NT_L, ps[SPARE][:], AF.Ln, bias=epsc[:, 0:1]),
                      reads=["epsc"], writes=[pk(SPARE), "ntl"])
                P.add("act", lambda e: e.activation(NT_R, NT_L, AF.Exp, scale=-0.5), reads=["ntl"], writes=["ntr"])
                P.add("dve", lambda e, h=h, q0=q0: e.scalar_tensor_tensor(
                    mixA[:, h, q0:q0 + 512], oF, sgl, NT_R, ALU.mult, ALU.mult),
                    reads=["oF", "ntr", "drv"], writes=["mixA%d_%d" % (h, qb_)])

            NI = len(items)
            LA = 2
            modt = {"i": 0}
            j0 = None
            sched = {2: epi_den1, 5: epi_comb, 10: epi_final}
            for j in range(NI + LA):
                if j < NI:
                    stage_A(j)
                if j - LA >= 0:
                    stage_C(j - LA)
                    if items[j - LA][2] == NKT - 1:
                        j0 = j
                if j0 is not None and (j - j0) in sched:
                    sched[j - j0]()
                    if j - j0 == 10:
                        j0 = None
                if l == 0 and j % 17 == 8 and modt["i"] < 18:
                    mod_tile_evac(1, modt["i"], SPARE)
                    modt["i"] += 1
            if l == 0:
                while modt["i"] < 18:
                    mod_tile_evac(1, modt["i"], SPARE)
                    modt["i"] += 1
            wv = wout_d[l].rearrange("(c p) d -> p c d", p=128)
            bo = []
            for hf in range(2):
                b = next_wg()
                bo.append(b)
                P.add("pool", lambda e, b=b, hf=hf: e.dma_start(out=wg[b][:], in_=wv[:, :, hf * 512:(hf + 1) * 512]),
                      writes=["wg%d" % b], dma_slot="wg%d" % b)
            if j0 is not None:
                for d in (2, 5, 10):
                    if d > (NI + LA - 1 - j0):
                        sched[d]()
            if l == 0:
                mod_finish(1, evacuated=True)
            P.barrier()

            yTs = [af32(42, [128, 8, 512]), af32(58, [128, 8, 512])]
            def emit_ssb(sq, sqk, dc, sbank):
                P.add("pe", lambda e: e.matmul(ps[sbank][:], lhsT=cbm(CB_M1024), rhs=sq,
                                               start=(dc == 0), stop=(dc == 7)),
                      reads=[sqk, "cb"], writes=[pk(sbank)])

            for blk in range(4):
                t0 = blk * 512
                par = blk % 2
                yT = yTs[par]
                sbank = 7 - par
                pend = None
                for dc in range(8):
                    b = bo[dc // 4]
                    bank = nbank(0, 6)
                    for cc in range(8):
                        if cc < 4:
                            rhs = mixA[:, cc, t0:t0 + 512]
                            rk = "mixA%d_%d" % (cc, blk)
                        else:
                            rhs = mixC[:, cc - 4, t0:t0 + 512]
                            rk = "mixC%d_%d" % (cc - 4, blk)
                        P.add("pe", lambda e, b=b, bank=bank, cc=cc, dc=dc, rhs=rhs: e.matmul(
                            ps[bank][:], lhsT=wg[b][:, cc, (dc % 4) * 128:(dc % 4 + 1) * 128], rhs=rhs,
                            start=(cc == 0), stop=(cc == 7)),
                            reads=["wg%d" % b, rk], writes=[pk(bank)])
                    P.add("act", lambda e, bank=bank, dc=dc, yT=yT: e.activation(yT[:, dc, :], ps[bank][:], AF.Copy),
                          writes=[pk(bank), "yT%d_%d" % (dc, par)])
                    sq = NT_SQ[dc % 2]
                    sqk = "ntsq%d" % (dc % 2)
                    P.add("dve", lambda e, sq=sq, dc=dc, yT=yT: e.tensor_tensor(sq, yT[:, dc, :], yT[:, dc, :], ALU.mult),
                          reads=["yT%d_%d" % (dc, par)], writes=[sqk])
                    if pend is not None:
                        emit_ssb(*pend)
                    pend = (sq, sqk, dc, sbank)
                    drain(2)
                emit_ssb(*pend)
                assert not pending
                pending.extend(post_closures(l, s, yT, 1, t0, [sbank], lambda c, bl, par=par: "yT%d_%d" % (c, par)))
            pre_ffn2["b"] = ffn_first_tile(l, 1)
            flush()
            P.barrier()

        pre_ffn2 = {"b": None}
        for l in range(DEPTH):
            ffn(l, 0, first_norm_done=(l > 0), tail=(lambda d, l=l: norm_site(l, 1, 0, alt=True, defer=d)),
                down_hook=(mod0_down_hook if l == 0 else None))
            mixer(l, first_norm_done=True, pre_drain=True)
            if l + 1 < DEPTH:
                ffn(l, 1, tail=(lambda d, l=l: norm_site(l + 1, 0, 0, alt=True, defer=d)), pre_tile=pre_ffn2["b"])
            else:
                ffn(l, 1, pre_tile=pre_ffn2["b"])
                flush()
                P.barrier()

        ost = [af32(0 + 4 * i, [128, D]) for i in range(2)]
        for tt in range(16):
            st = ost[tt % 2]
            sk = "ost%d" % (tt % 2)
            for hf in range(2):
                b = nbank()
                for c4 in range(4):
                    c = hf * 4 + c4
                    P.add("pe", lambda e, b=b, c=c, c4=c4, tt=tt: e.transpose(
                        ps[b][:, c4 * 128:(c4 + 1) * 128], xT[:, c, tt * 128:(tt + 1) * 128], idf[:]),
                        reads=["xT%d_%d" % (c, tt // 4), "idf"], writes=[pk(b)])
                if hf == 0:
                    P.add("act", lambda e, b=b, st=st: e.activation(st[:, 0:512], ps[b][:], AF.Copy),
                          writes=[pk(b), sk + "a"])
                else:
                    P.add("dve", lambda e, b=b, st=st: e.tensor_copy(st[:, 512:1024], ps[b][:]),
                          writes=[pk(b), sk + "b"])
            P.add("sp", lambda e, st=st, tt=tt: e.dma_start(out=y_d[tt * 128:(tt + 1) * 128, :], in_=st),
                  reads=[sk + "a", sk + "b"], writes=[sk + "a", sk + "b"], dma_slot=sk, final=True)
        P.emit(nc, ctx)
    return nc


def _const_tables():
    bf = ml_dtypes.bfloat16
    cbt = np.zeros((128, 6, 128), np.float32)
    cbt[:, CB_M1024] = 1.0 / 1024.0
    cbt[:, CB_M128] = 1.0 / 128.0
    cbt[:, CB_ONE] = 1.0
    perm = np.zeros((128, 128), np.float32)
    for dst in range(128):
        j = dst % 32
        if j < 16:
            perm[dst + 16, dst] = -1.0
        else:
            perm[dst - 16, dst] = 1.0
    cbt[:, CB_PERM] = perm
    c = np.arange(64)
    ang = 2.0 * np.pi * ((c[:, None] * c[None, :]) % 64) / 64.0
    cc = np.zeros((128, 128))
    cs = np.zeros((128, 128))
    for g in range(2):
        cc[g * 64:(g + 1) * 64, g * 64:(g + 1) * 64] = np.cos(ang) / 8.0
        cs[g * 64:(g + 1) * 64, g * 64:(g + 1) * 64] = -np.sin(ang) / 8.0
    cbt[:, CB_CC] = cc
    cbt[:, CB_CS] = cs
    cb = cbt.reshape(128, 6 * 128).astype(bf)

    def dft(n_seq):
        n = np.arange(T)
        pos = n % n_seq
        blk = n // n_seq
        prod = (pos[:, None].astype(np.int64) * pos[None, :].astype(np.int64)) % n_seq
        a = 2.0 * np.pi * prod / n_seq
        same = (blk[:, None] == blk[None, :])
        sc = 1.0 / math.sqrt(n_seq)
        out = np.stack([np.where(same, np.cos(a) * sc, 0.0), np.where(same, np.sin(a) * sc, 0.0)])
        return out.astype(bf)

    p = np.arange(128)
    j = p % 64
    axis = j // 32
    fi = j % 16
    inv = 1.0 / (10000.0 ** (fi.astype(np.float64) / 16.0))
    t = np.arange(T)
    row = (t // 64).astype(np.float64)
    col = (t % 64).astype(np.float64)
    posn = np.where(axis[:, None] == 0, row[None, :], col[None, :])
    ang = posn * inv[:, None]
    rope_s = np.concatenate([np.cos(ang), np.sin(ang)], axis=1).astype(np.float32)
    rope_p = np.concatenate([np.ones((128, T)), np.zeros((128, T))], axis=1).astype(np.float32)

    def cmask(L):
        mL = (t % L != 0).astype(np.float32)
        mR = (t % L != L - 1).astype(np.float32)
        return np.broadcast_to(np.concatenate([mL, mR])[None, :], (128, 2 * T)).astype(bf)

    amask_s = np.zeros((NKT, 8), np.float32)
    amask_p = np.full((NKT, 8), NEG, np.float32)
    for kt in range(16):
        amask_p[kt, kt // 4] = 0.0
        amask_p[kt, 4 + (kt // 2) % 2] = 0.0
    amask_s[:, 4:6] = 0.0
    NEGQ = -240000.0
    mq_s = np.zeros((2, 768), np.float32)
    mq_s[0, 512:640] = 1.0
    mq_s[1, 640:768] = 1.0
    mq_p = mq_s.copy()
    mq_p[0, 256:512] = NEGQ
    mq_p[1, 0:256] = NEGQ
    mq_s = mq_s.astype(bf)
    mq_p = mq_p.astype(bf)
    return dict(cb=cb, dft_s=dft(T), dft_p=dft(256), rope_s=rope_s, rope_p=rope_p,
                cmask_s=cmask(T), cmask_p=cmask(256), amask_s=amask_s, amask_p=amask_p, mq_s=mq_s, mq_p=mq_p)


_CACHE = {}


def kernel(x_prompt, x_sample, cache_k, cache_v, c, c_ctx, w_mod, b_mod, norm_g,
           w_ffn_gu, w_ffn_down, w_in, w_out, conv_w, lam_qk, subln_g):
    f32 = np.float32
    x_prompt = np.asarray(x_prompt, f32)
    x_sample = np.asarray(x_sample, f32)
    cache_k = np.asarray(cache_k, f32)
    cache_v = np.asarray(cache_v, f32)
    c = np.asarray(c, f32)
    c_ctx = np.asarray(c_ctx, f32)
    w_mod = np.ascontiguousarray(np.asarray(w_mod, f32))
    b_mod = np.asarray(b_mod, f32)
    norm_g = np.asarray(norm_g, f32)
    w_ffn_gu = np.ascontiguousarray(np.asarray(w_ffn_gu, f32))
    w_ffn_down = np.ascontiguousarray(np.asarray(w_ffn_down, f32))
    w_in = np.ascontiguousarray(np.asarray(w_in, f32))
    w_out = np.ascontiguousarray(np.asarray(w_out, f32))
    conv_w = np.asarray(conv_w, f32)
    lam_qk = np.asarray(lam_qk, f32)
    subln_g = np.asarray(subln_g, f32)

    if "nc" not in _CACHE:
        _CACHE["nc"] = build_program()
        _CACHE["tab"] = _const_tables()
    nc = _CACHE["nc"]
    tab = _CACHE["tab"]
    idf = np.eye(128, dtype=f32)

    def fm(v):
        return np.ascontiguousarray(v.reshape(8, 128).T)

    in_maps = []
    for core in range(8):
        is_s = core < 4
        sm = np.zeros((128, NSM), f32)
        cvec = c[core] if is_s else c_ctx
        sm[:, SM_CV:SM_CV + 8] = fm(cvec)
        for l in range(DEPTH):
            o = SM_L + l * SM_LSZ
            sm[:, o:o + 72] = b_mod[l].reshape(72, 128).T
            sm[:, o + 72:o + 120] = norm_g[l].reshape(48, 128).T
            sm[:, o + 120:o + 126] = conv_w[l].reshape(3, 2, 128).transpose(2, 1, 0).reshape(128, 6)
            sm[:, o + 126] = subln_g[l]
            sm[0:64, SM_LAM + l * 4:SM_LAM + l * 4 + 4] = lam_qk[l].T
        am = tab["amask_s"] if is_s else tab["amask_p"]
        sm[:, SM_MASK:SM_MASK + 160] = am.reshape(1, 160)
        if is_s:
            xx = x_sample[core]
            ck = cache_k[core].reshape(DEPTH, PAST, 512)
            cvv = cache_v[core].reshape(DEPTH, PAST, 512)
        else:
            xx = x_prompt[(core - 4) * 8:(core - 4) * 8 + 8].reshape(T, D)
            ck = np.zeros((DEPTH, PAST, 512), f32)
            cvv = np.zeros((DEPTH, PAST, 512), f32)
        in_maps.append({
            "x": np.ascontiguousarray(xx), "sm": sm, "cb": tab["cb"], "idf": idf,
            "rope": tab["rope_s"] if is_s else tab["rope_p"],
            "mq": tab["mq_s"] if is_s else tab["mq_p"],
            "cmask": tab["cmask_s"] if is_s else tab["cmask_p"],
            "dft": tab["dft_s"] if is_s else tab["dft_p"],
            "ck": np.ascontiguousarray(ck), "cvv": np.ascontiguousarray(cvv),
            "w_mod": w_mod, "w_ffn_gu": w_ffn_gu, "w_ffn_down": w_ffn_down, "w_in": w_in, "w_out": w_out,
        })
    res = run_bass_kernel_spmd(nc, in_maps, core_ids=list(range(8)))
    r = res.results
    y_sample = np.stack([r[b]["y"] for b in range(4)]).astype(f32)
    y_prompt = np.concatenate([r[4 + i]["y"].reshape(8, 256, D) for i in range(4)]).astype(f32)
    nk = np.concatenate([r[4 + i]["nk"].reshape(DEPTH, 8, 256, 4, 2, 64).transpose(1, 0, 2, 3, 4, 5)
                         for i in range(4)]).astype(f32)
    nv = np.concatenate([r[4 + i]["nv"].reshape(DEPTH, 8, 256, 4, 128).transpose(1, 0, 2, 3, 4)
                         for i in range(4)]).astype(f32)
    return (y_prompt, y_sample, nk, nv)
```

```python
import math
from contextlib import ExitStack

import numpy as np
import ml_dtypes

import concourse.bass as bass
import concourse.mybir as mybir
from concourse.bass_utils import run_bass_kernel_spmd

F32 = mybir.dt.float32
BF16 = mybir.dt.bfloat16
AF = mybir.ActivationFunctionType
ALU = mybir.AluOpType

D = 1024
T = 2048
NCH = 8
DFF = 2816
NF = 22
DEPTH = 2
PAST = 512
NKT = 20
EPS = 1e-6
NEG = -30000.0
ENGS = ("pe", "act", "dve", "pool", "sp")


class Op:
    __slots__ = ("eng", "fn", "deps", "signal", "val", "sem", "is_dma", "slot")

    def __init__(self, eng, fn, is_dma=False, slot=None):
        self.eng = eng
        self.fn = fn
        self.deps = []
        self.signal = False
        self.val = None
        self.sem = None
        self.is_dma = is_dma
        self.slot = slot


class Prog:
    def __init__(self):
        self.ops = {e: [] for e in ENGS}
        self.last_w = {}
        self.readers = {}
        self.all = []
        self.final = []
        self.pending_barrier = {e: None for e in ENGS}
        self.since_barrier_dma = []
        self.last_op = {e: None for e in ENGS}

    def barrier(self):
        deps = [o for o in self.last_op.values() if o is not None] + list(self.since_barrier_dma)
        self.since_barrier_dma = []
        for e in ENGS:
            prev = self.pending_barrier[e] or []
            self.pending_barrier[e] = prev + deps

    def add(self, eng, fn, reads=(), writes=(), dma_slot=None, final=False):
        op = Op(eng, fn, is_dma=dma_slot is not None, slot=dma_slot)
        deps = set()
        for r in reads:
            w = self.last_w.get(r)
            if w is not None:
                deps.add(w)
        for w_ in writes:
            w = self.last_w.get(w_)
            if w is not None:
                deps.add(w)
            for rd in self.readers.get(w_, ()):
                deps.add(rd)
        for r in reads:
            self.readers.setdefault(r, []).append(op)
        for w_ in writes:
            self.last_w[w_] = op
            self.readers[w_] = []
        pb = self.pending_barrier[eng]
        if pb:
            deps.update(pb)
            self.pending_barrier[eng] = None
        deps.discard(op)
        for d in deps:
            if d.eng == "pe" and eng == "pe" and not d.is_dma and not op.is_dma:
                continue
            op.deps.append(d)
            d.signal = True
        if final:
            op.signal = True
            self.final.append(op)
        self.all.append(op)
        self.ops[eng].append(op)
        if op.is_dma:
            self.since_barrier_dma.append(op)
        else:
            self.last_op[eng] = op
        return op

    def emit(self, nc, ctx):
        esem = {e: ctx.enter_context(nc.semaphore("c_" + e)) for e in ENGS}
        slot_sem = {}
        slot_cnt = {}
        cnt = {e: 0 for e in ENGS}
        for op in self.all:
            if op.is_dma:
                if op.slot not in slot_sem:
                    slot_sem[op.slot] = ctx.enter_context(nc.semaphore("d_%d" % len(slot_sem)))
                    slot_cnt[op.slot] = 0
                slot_cnt[op.slot] += 16
                op.sem = slot_sem[op.slot]
                op.val = slot_cnt[op.slot]
            elif op.signal:
                cnt[op.eng] += 1
                op.sem = esem[op.eng]
                op.val = cnt[op.eng]
        final = self.final
        ops = self.ops

        def run(eng_name, eng):
            waited = {}
            for op in ops[eng_name]:
                need = {}
                for d in op.deps:
                    k = id(d.sem)
                    if waited.get(k, 0) >= d.val:
                        continue
                    if k not in need or need[k][1] < d.val:
                        need[k] = (d.sem, d.val)
                for k, (s, v) in need.items():
                    eng.wait_ge(s, v)
                    waited[k] = v
                ins = op.fn(eng)
                if op.is_dma:
                    ins.then_inc(op.sem, 16)
                elif op.signal:
                    ins.then_inc(op.sem, 1)
            if eng_name == "sp":
                for f in final:
                    k = id(f.sem)
                    if waited.get(k, 0) >= f.val:
                        continue
                    eng.wait_ge(f.sem, f.val)
                    waited[k] = f.val

        with nc.Block() as block:
            @block.tensor
            def _(e):
                run("pe", e)

            @block.scalar
            def _(e):
                run("act", e)

            @block.vector
            def _(e):
                run("dve", e)

            @block.gpsimd
            def _(e):
                run("pool", e)

            @block.sync
            def _(e):
                run("sp", e)


SM_CV = 0
SM_L = 8
SM_LSZ = 127
SM_MASK = SM_L + DEPTH * SM_LSZ
SM_LAM = SM_MASK + 160
NSM = SM_LAM + 8

CB_M1024, CB_M128, CB_ONE, CB_PERM, CB_CC, CB_CS = range(6)


def _lambda_init(l):
    return 0.8 - 0.6 * math.exp(-0.3 * l)


def build_program():
    nc = bass.Bass("TRN2", target_bir_lowering=False)

    def din(name, shape, dt=F32):
        return nc.dram_tensor(name, list(shape), dt, kind="ExternalInput").ap()

    def dout(name, shape, dt=F32):
        return nc.dram_tensor(name, list(shape), dt, kind="ExternalOutput").ap()

    x_d = din("x", [T, D])
    sm_d = din("sm", [128, NSM])
    cb_d = din("cb", [128, 6 * 128], BF16)
    idf_d = din("idf", [128, 128])
    mq_d = din("mq", [2, 768], BF16)
    rope_d = din("rope", [128, 2 * T])
    cmask_d = din("cmask", [128, 2 * T], BF16)
    dft_d = din("dft", [2, T, T], BF16)
    ck_d = din("ck", [DEPTH, PAST, 512])
    cvv_d = din("cvv", [DEPTH, PAST, 512])
    wmod_d = din("w_mod", [DEPTH, D, 9 * D])
    wgu_d = din("w_ffn_gu", [DEPTH, 2, D, 2 * DFF])
    wdn_d = din("w_ffn_down", [DEPTH, 2, DFF, D])
    win_d = din("w_in", [DEPTH, D, 2560])
    wout_d = din("w_out", [DEPTH, D, D])
    y_d = dout("y", [T, D])
    nk_d = dout("nk", [DEPTH, T, 512])
    nv_d = dout("nv", [DEPTH, T, 512])

    P = Prog()
    with ExitStack() as ctx:
        sb = lambda name, shape, dt: ctx.enter_context(nc.sbuf_tensor("s_" + name, list(shape), dt))
        xT = sb("xT", [128, NCH, T], F32)
        wg = [sb("wg%d" % i, [128, 8, 512], BF16) for i in range(2)]
        sm = sb("sm", [128, NSM], F32)
        cb = sb("cb", [128, 6 * 128], BF16)
        idf = sb("idf", [128, 128], F32)
        mq = sb("mq", [2, 768], BF16)
        onesf = sb("onesf", [128, 128], F32)
        drv = sb("drv", [128, 160], F32)
        sT = sb("sT", [128, 8], BF16)
        epsc = sb("epsc", [128, 1], F32)
        ARENA_KB = 120
        arena = sb("arena", [128, ARENA_KB * 512], BF16)
        arena_f = arena.bitcast(F32)
        psall = ctx.enter_context(nc.psum_tensor("psall", [128, 8, 512], F32))
        ps = [psall[:, i, :] for i in range(8)]

        def abf(off_kb, shape):
            n = int(np.prod(shape[1:]))
            o = int(off_kb * 512)
            ap = arena[:, o:o + n]
            if len(shape) == 3:
                ap = ap.rearrange("p (a b) -> p a b", a=shape[1])
            return ap

        def af32(off_kb, shape):
            n = int(np.prod(shape[1:]))
            o = int(off_kb * 256)
            ap = arena_f[:, o:o + n]
            if len(shape) == 3:
                ap = ap.rearrange("p (a b) -> p a b", a=shape[1])
            return ap

        def cbm(i):
            return cb[:, i * 128:(i + 1) * 128]

        rot = {"i": 0}

        def nbank(lo=0, hi=6):
            b = lo + rot["i"] % (hi - lo)
            rot["i"] += 1
            return b

        def pk(b):
            return "ps%d" % b

        wgi = {"i": 0}

        def next_wg():
            b = wgi["i"] % 2
            wgi["i"] += 1
            return b

        P.add("sp", lambda e: e.dma_start(out=sm[:], in_=sm_d), writes=["sm"], dma_slot="sm")
        P.add("sp", lambda e: e.dma_start(out=cb[:], in_=cb_d), writes=["cb"], dma_slot="cb")
        P.add("sp", lambda e: e.dma_start(out=idf[:], in_=idf_d), writes=["idf"], dma_slot="idf")
        P.add("sp", lambda e: e.dma_start(out=mq[:], in_=mq_d), writes=["mq"], dma_slot="mq")
        P.add("dve", lambda e: e.memset(onesf[:], 1.0), writes=["onesf"])
        P.add("dve", lambda e: e.memset(epsc[:], EPS), writes=["epsc"])
        P.add("act", lambda e: e.activation(sT[:], sm[:, SM_CV:SM_CV + 8], AF.Silu), reads=["sm"], writes=["sT"])

        xst = [af32(0 + 4 * i, [128, D]) for i in range(2)]

        def xload_tile(tt):
            if tt >= 16:
                return
            st = xst[tt % 2]
            sk = "xst%d" % (tt % 2)
            P.add("sp", lambda e, st=st, tt=tt: e.dma_start(out=st, in_=x_d[tt * 128:(tt + 1) * 128, :]),
                  writes=[sk], dma_slot=sk)
            for hf in range(2):
                b = nbank()
                for c4 in range(4):
                    c = hf * 4 + c4
                    P.add("pe", lambda e, b=b, c=c, c4=c4, st=st: e.transpose(
                        ps[b][:, c4 * 128:(c4 + 1) * 128], st[:, c * 128:(c + 1) * 128], idf[:]),
                        reads=[sk, "idf"], writes=[pk(b)])
                eng = "act" if hf == 0 else "dve"
                outap = xT[:, hf * 4:hf * 4 + 4, tt * 128:(tt + 1) * 128]
                inap = ps[b][:].rearrange("p (a b) -> p a b", a=4)
                if eng == "act":
                    P.add("act", lambda e, o=outap, i=inap: e.activation(o, i, AF.Copy),
                          writes=[pk(b)] + ["xT%d_%d" % (c, tt // 4) for c in range(hf * 4, hf * 4 + 4)])
                else:
                    P.add("dve", lambda e, o=outap, i=inap: e.tensor_copy(o, i),
                          writes=[pk(b)] + ["xT%d_%d" % (c, tt // 4) for c in range(hf * 4, hf * 4 + 4)])

        drv2 = sb("drv2", [128, DEPTH * 48], F32)

        def mod_tile_evac(l, t_, pb):
            base = 80 * l
            wv = wmod_d[l].rearrange("(c p) f -> p c f", p=128)
            b = next_wg()
            P.add("pool", lambda e, b=b, t_=t_: e.dma_start(out=wg[b][:], in_=wv[:, :, t_ * 512:(t_ + 1) * 512]),
                  writes=["wg%d" % b], dma_slot="wg%d" % b)
            for fc in range(4):
                for c in range(8):
                    P.add("pe", lambda e, b=b, fc=fc, c=c: e.matmul(
                        ps[pb][:, fc:fc + 1], lhsT=wg[b][:, c, fc * 128:(fc + 1) * 128], rhs=sT[:, c:c + 1],
                        start=(c == 0), stop=(c == 7)),
                        reads=["wg%d" % b, "sT"], writes=[pk(pb)])
            bm = sm[:, SM_L + l * SM_LSZ + t_ * 4: SM_L + l * SM_LSZ + t_ * 4 + 4]
            P.add("dve", lambda e: e.tensor_tensor(drv[:, base + t_ * 4:base + t_ * 4 + 4], ps[pb][:, 0:4], bm, ALU.add),
                  reads=["sm"], writes=[pk(pb), "drvm%d" % l])

        def mod_tiles(l, ta, tb, hook=None):
            pb = 7
            wv = wmod_d[l].rearrange("(c p) f -> p c f", p=128)
            for t_ in range(ta, tb):
                if hook is not None:
                    hook(t_)
                b = next_wg()
                P.add("pool", lambda e, b=b, t_=t_: e.dma_start(out=wg[b][:], in_=wv[:, :, t_ * 512:(t_ + 1) * 512]),
                      writes=["wg%d" % b], dma_slot="wg%d" % b)
                for fc in range(4):
                    col = t_ * 4 + fc
                    for c in range(8):
                        P.add("pe", lambda e, b=b, fc=fc, c=c, col=col: e.matmul(
                            ps[pb][:, col:col + 1], lhsT=wg[b][:, c, fc * 128:(fc + 1) * 128], rhs=sT[:, c:c + 1],
                            start=(c == 0), stop=(c == 7)),
                            reads=["wg%d" % b, "sT"], writes=[pk(pb)])

        def mod_finish(l, evacuated=False):
            base = 80 * l
            bmod = sm[:, SM_L + l * SM_LSZ: SM_L + l * SM_LSZ + 72]
            ng = lambda i: sm[:, SM_L + l * SM_LSZ + 72 + i * 8: SM_L + l * SM_LSZ + 72 + i * 8 + 8]
            pb = 7
            if not evacuated:
                P.add("dve", lambda e: e.tensor_tensor(drv[:, base:base + 72], ps[pb][:, 0:72], bmod, ALU.add),
                      reads=["sm"], writes=[pk(pb), "drv"])
            else:
                P.add("dve", lambda e: e.memset(drv[:, base + 78:base + 79], 0.0),
                      reads=["drvm%d" % l], writes=["drv"])
            for s in range(3):
                sc = drv[:, base + (3 * s + 1) * 8: base + (3 * s + 1) * 8 + 8]
                gt = drv[:, base + (3 * s + 2) * 8: base + (3 * s + 2) * 8 + 8]
                gs = drv2[:, l * 48 + s * 8: l * 48 + s * 8 + 8]
                gg = drv2[:, l * 48 + 24 + s * 8: l * 48 + 24 + s * 8 + 8]
                wres = 1.0 if s == 1 else 0.5
                P.add("dve", lambda e, sc=sc, gs=gs, s=s: e.scalar_tensor_tensor(
                    gs, sc, 1.0, ng(2 * s), ALU.add, ALU.mult), reads=["drv", "sm"], writes=["drv2"])
                P.add("dve", lambda e, gt=gt, gg=gg, s=s, wres=wres: e.scalar_tensor_tensor(
                    gg, gt, wres, ng(2 * s + 1), ALU.mult, ALU.mult), reads=["drv", "sm"], writes=["drv2"])
            lc = SM_LAM + l * 4
            tmp = drv[0:64, base + 76:base + 78]
            P.add("dve", lambda e: e.tensor_tensor(drv[0:64, base + 76:base + 77], sm[0:64, lc:lc + 1],
                                                  sm[0:64, lc + 1:lc + 2], ALU.mult), reads=["sm"], writes=["drv"])
            P.add("dve", lambda e: e.tensor_tensor(drv[0:64, base + 77:base + 78], sm[0:64, lc + 2:lc + 3],
                                                  sm[0:64, lc + 3:lc + 4], ALU.mult), reads=["sm"], writes=["drv"])
            P.add("pe", lambda e: e.matmul(ps[pb][:, 128:130], lhsT=onesf[0:64, :], rhs=tmp, start=True, stop=True),
                  reads=["drv", "onesf"], writes=[pk(pb)])
            P.add("act", lambda e: e.activation(drv[:, base + 74:base + 76], ps[pb][:, 128:130], AF.Exp),
                  writes=[pk(pb), "drv"])
            P.add("dve", lambda e: e.tensor_tensor(drv[:, base + 72:base + 73], drv[:, base + 74:base + 75],
                                                  drv[:, base + 75:base + 76], ALU.subtract), writes=["drv"])
            li = _lambda_init(l)
            P.add("dve", lambda e: e.tensor_scalar(drv[:, base + 72:base + 73], drv[:, base + 72:base + 73],
                                                  li, -1.0, ALU.add, ALU.mult), writes=["drv"])
            sub = sm[:, SM_L + l * SM_LSZ + 126: SM_L + l * SM_LSZ + 127]
            P.add("dve", lambda e: e.tensor_scalar(drv[:, base + 73:base + 74], sub, 1.0 - li, None, ALU.mult),
                  reads=["sm"], writes=["drv"])

        def mod_part(l, names):
            base = 80 * l
            ng = lambda i: sm[:, SM_L + l * SM_LSZ + 72 + i * 8: SM_L + l * SM_LSZ + 72 + i * 8 + 8]
            P.add("dve", lambda e: e.memset(drv[:, base + 78:base + 79], 0.0),
                  reads=["drvm%d" % l], writes=["drv"])
            for kind, s in names:
                if kind == "gs":
                    sc = drv[:, base + (3 * s + 1) * 8: base + (3 * s + 1) * 8 + 8]
                    gs = drv2[:, l * 48 + s * 8: l * 48 + s * 8 + 8]
                    P.add("dve", lambda e, sc=sc, gs=gs, s=s: e.scalar_tensor_tensor(
                        gs, sc, 1.0, ng(2 * s), ALU.add, ALU.mult), reads=["drv", "sm"], writes=["drv2"])
                else:
                    gt = drv[:, base + (3 * s + 2) * 8: base + (3 * s + 2) * 8 + 8]
                    gg = drv2[:, l * 48 + 24 + s * 8: l * 48 + 24 + s * 8 + 8]
                    wres = 1.0 if s == 1 else 0.5
                    P.add("dve", lambda e, gt=gt, gg=gg, s=s, wres=wres: e.scalar_tensor_tensor(
                        gg, gt, wres, ng(2 * s + 1), ALU.mult, ALU.mult), reads=["drv", "sm"], writes=["drv2"])

        def mod_lambda(l):
            base = 80 * l
            pb = 7
            lc = SM_LAM + l * 4
            tmp = drv[0:64, base + 76:base + 78]
            P.add("dve", lambda e: e.tensor_tensor(drv[0:64, base + 76:base + 77], sm[0:64, lc:lc + 1],
                                                  sm[0:64, lc + 1:lc + 2], ALU.mult), reads=["sm"], writes=["drv"])
            P.add("dve", lambda e: e.tensor_tensor(drv[0:64, base + 77:base + 78], sm[0:64, lc + 2:lc + 3],
                                                  sm[0:64, lc + 3:lc + 4], ALU.mult), reads=["sm"], writes=["drv"])
            P.add("pe", lambda e: e.matmul(ps[pb][:, 128:130], lhsT=onesf[0:64, :], rhs=tmp, start=True, stop=True),
                  reads=["drv", "onesf"], writes=[pk(pb)])
            P.add("act", lambda e: e.activation(drv[:, base + 74:base + 76], ps[pb][:, 128:130], AF.Exp),
                  writes=[pk(pb), "drv"])
            P.add("dve", lambda e: e.tensor_tensor(drv[:, base + 72:base + 73], drv[:, base + 74:base + 75],
                                                  drv[:, base + 75:base + 76], ALU.subtract), writes=["drv"])
            li = _lambda_init(l)
            P.add("dve", lambda e: e.tensor_scalar(drv[:, base + 72:base + 73], drv[:, base + 72:base + 73],
                                                  li, -1.0, ALU.add, ALU.mult), writes=["drv"])
            sub = sm[:, SM_L + l * SM_LSZ + 126: SM_L + l * SM_LSZ + 127]
            P.add("dve", lambda e: e.tensor_scalar(drv[:, base + 73:base + 74], sub, 1.0 - li, None, ALU.mult),
                  reads=["sm"], writes=["drv"])

        for t_ in range(4):
            for k_ in range(4):
                xload_tile(4 * t_ + k_)
            mod_tile_evac(0, t_, 7)
        mod_part(0, [("gs", 0)])
        mod_lambda(0)
        _wgu00 = wgu_d[0, 0].rearrange("(c p) f -> p c f", p=128)
        pre_ffn1 = next_wg()
        P.add("pool", lambda e: e.dma_start(out=wg[pre_ffn1][:, :, 0:256], in_=_wgu00[:, :, 0:256]),
              writes=["wg%d" % pre_ffn1], dma_slot="wg%da" % pre_ffn1)
        P.add("pool", lambda e: e.dma_start(out=wg[pre_ffn1][:, :, 256:512], in_=_wgu00[:, :, DFF:DFF + 256]),
              writes=["wg%d" % pre_ffn1], dma_slot="wg%db" % pre_ffn1)
        P.barrier()

        def mod0_down_hook(half, dc):
            t_ = 4 + half * 8 + dc
            if t_ < 18:
                mod_tile_evac(0, t_, nbank(0, 6))
            if half == 0 and dc == 1:
                mod_part(0, [("gg", 0)])
            if half == 0 and dc == 7:
                mod_part(0, [("gs", 1), ("gg", 1)])
            if half == 1 and dc == 5:
                mod_part(0, [("gs", 2), ("gg", 2)])

        def mod_cols(l, j):
            return drv[:, 80 * l + j * 8: 80 * l + j * 8 + 8]

        hT = abf(0, [128, 8, 1024])
        NT_SQ = [abf(16 + i, [128, 512]) for i in range(2)]
        NT_R = af32(18, [128, 512])
        NT_L = af32(20, [128, 512])
        NT_T = [af32(22 + 2 * i, [128, 512]) for i in range(2)]

        NT_SQ2 = [abf(113 + i, [128, 512]) for i in range(4)]
        pending = []

        def drain(n=1):
            for _ in range(n):
                if pending:
                    pending.pop(0)()

        def flush():
            while pending:
                pending.pop(0)()

        def norm_site(l, s, half, alt=False, defer=None):
            gs = drv2[:, l * 48 + s * 8: l * 48 + s * 8 + 8]
            sh = mod_cols(l, 3 * s)
            sqb = NT_SQ2 if alt else NT_SQ
            sqn = "ntsqb%d" if alt else "ntsq%d"
            nbanks = [nbank(), nbank()]
            for bl in range(2):
                t0 = half * 1024 + bl * 512
                gb_ = t0 // 512
                b = nbanks[bl]
                for c in range(8):
                    sq = sqb[c % len(sqb)]
                    sqk = sqn % (c % len(sqb))
                    if c % 2 == 0:
                        P.add("act", lambda e, sq=sq, c=c, t0=t0: e.activation(sq, xT[:, c, t0:t0 + 512], AF.Square),
                              reads=["xT%d_%d" % (c, gb_)], writes=[sqk])
                    else:
                        P.add("dve", lambda e, sq=sq, c=c, t0=t0: e.tensor_tensor(
                            sq, xT[:, c, t0:t0 + 512], xT[:, c, t0:t0 + 512], ALU.mult),
                            reads=["xT%d_%d" % (c, gb_)], writes=[sqk])
                    P.add("pe", lambda e, b=b, sq=sq, c=c: e.matmul(ps[b][:], lhsT=cbm(CB_M1024), rhs=sq,
                                                                   start=(c == 0), stop=(c == 7)),
                          reads=[sqk, "cb"], writes=[pk(b)])
            pieces = []
            for bl in range(2):
                t0 = half * 1024 + bl * 512
                gb_ = t0 // 512
                b = nbanks[bl]

                rbuf = NT_R if bl == 0 else NT_L
                rkey = "ntr" if bl == 0 else "ntl"
                P.add("act", lambda e, b=b: e.activation(NT_L, ps[b][:], AF.Ln, bias=epsc[:, 0:1]),
                      reads=["epsc"], writes=[pk(b), "ntl"])
                P.add("act", lambda e, rbuf=rbuf: e.activation(rbuf, NT_L, AF.Exp, scale=-0.5),
                      reads=["ntl"], writes=[rkey])
                for c in range(8):
                    def body(c=c, bl=bl, t0=t0, gb_=gb_, rbuf=rbuf, rkey=rkey):
                        tt_ = NT_T[c % 2]
                        tk = "ntt%d" % (c % 2)
                        P.add("dve", lambda e, tt_=tt_, c=c, t0=t0: e.scalar_tensor_tensor(
                            tt_, xT[:, c, t0:t0 + 512], gs[:, c:c + 1], rbuf, ALU.mult, ALU.mult),
                            reads=["xT%d_%d" % (c, gb_), rkey, "drv2"], writes=[tk])
                        P.add("act", lambda e, tt_=tt_, c=c, bl=bl: e.activation(
                            hT[:, c, bl * 512:(bl + 1) * 512], tt_, AF.Identity, bias=sh[:, c:c + 1]),
                            reads=[tk, "drv"], writes=["hT%d_%d" % (c, bl)])
                    pieces.append(body)
            if defer is None:
                for f in pieces:
                    f()
            else:
                defer.extend(pieces)

        def post_closures(l, s, yT, nblk, tok0, ssb, ykeys):
            gg = drv2[:, l * 48 + 24 + s * 8: l * 48 + 24 + s * 8 + 8]
            out = []
            for bl in range(nblk):
                t0 = tok0 + bl * 512
                gb_ = t0 // 512
                b = ssb[bl]

                def head(b=b):
                    P.add("act", lambda e, b=b: e.activation(NT_L, ps[b][:], AF.Ln, bias=epsc[:, 0:1]),
                          reads=["epsc"], writes=[pk(b), "ntl"])
                    P.add("act", lambda e: e.activation(NT_R, NT_L, AF.Exp, scale=-0.5), reads=["ntl"], writes=["ntr"])
                out.append(head)
                for c in range(8):
                    def body(c=c, bl=bl, t0=t0, gb_=gb_):
                        tt_ = NT_T[c % 2]
                        tk = "ntt%d" % (c % 2)
                        P.add("dve", lambda e, tt_=tt_, c=c, bl=bl: e.scalar_tensor_tensor(
                            tt_, yT[:, c, bl * 512:(bl + 1) * 512], gg[:, c:c + 1], NT_R, ALU.mult, ALU.mult),
                            reads=[ykeys(c, bl), "ntr", "drv2"], writes=[tk])
                        P.add("dve", lambda e, tt_=tt_, c=c, t0=t0: e.tensor_tensor(
                            xT[:, c, t0:t0 + 512], xT[:, c, t0:t0 + 512], tt_, ALU.add),
                            reads=[tk], writes=["xT%d_%d" % (c, gb_)])
                    out.append(body)
            return out

        def post_site(l, s, yT, nblk, tok0, ssb, ykeys):
            for f in post_closures(l, s, yT, nblk, tok0, ssb, ykeys):
                f()

        def ffn_first_tile(l, i):
            wgu = wgu_d[l, i].rearrange("(c p) f -> p c f", p=128)
            b = next_wg()
            P.add("pool", lambda e, b=b: e.dma_start(out=wg[b][:, :, 0:256], in_=wgu[:, :, 0:256]),
                  writes=["wg%d" % b], dma_slot="wg%da" % b)
            P.add("pool", lambda e, b=b: e.dma_start(out=wg[b][:, :, 256:512], in_=wgu[:, :, DFF:DFF + 256]),
                  writes=["wg%d" % b], dma_slot="wg%db" % b)
            return b

        def ffn(l, i, first_norm_done=False, tail=None, down_hook=None, pre_tile=None):
            s = 0 if i == 0 else 2
            actT = abf(26, [128, NF, 1024])
            wd = [abf(70 + 5.5 * k, [128, NF, 128]) for k in range(2)]
            yT = af32(81, [128, 8, 1024])
            sg = [af32(113 + 2 * k, [128, 512]) for k in range(2)]
            wgu = wgu_d[l, i].rearrange("(c p) f -> p c f", p=128)
            wdn = wdn_d[l, i].rearrange("(f p) d -> p f d", p=128)
            if not first_norm_done:
                norm_site(l, s, 0)
            for half in range(2):
                for j in range(11):
                    if half == 0 and j == 0 and pre_tile is not None:
                        b = pre_tile
                    else:
                        b = next_wg()
                        P.add("pool", lambda e, b=b, j=j: e.dma_start(out=wg[b][:, :, 0:256], in_=wgu[:, :, j * 256:(j + 1) * 256]),
                              writes=["wg%d" % b], dma_slot="wg%da" % b)
                        P.add("pool", lambda e, b=b, j=j: e.dma_start(out=wg[b][:, :, 256:512],
                                                                     in_=wgu[:, :, DFF + j * 256:DFF + (j + 1) * 256]),
                              writes=["wg%d" % b], dma_slot="wg%db" % b)
                    for fcl in range(2):
                        fc = 2 * j + fcl
                        for bl in range(2):
                            bg = nbank(0, 6)
                            bu = nbank(0, 6)
                            for c in range(8):
                                P.add("pe", lambda e, b=b, bg=bg, c=c, fcl=fcl, bl=bl: e.matmul(
                                    ps[bg][:], lhsT=wg[b][:, c, fcl * 128:(fcl + 1) * 128],
                                    rhs=hT[:, c, bl * 512:(bl + 1) * 512], start=(c == 0), stop=(c == 7)),
                                    reads=["wg%d" % b, "hT%d_%d" % (c, bl)], writes=[pk(bg)])
                            for c in range(8):
                                P.add("pe", lambda e, b=b, bu=bu, c=c, fcl=fcl, bl=bl: e.matmul(
                                    ps[bu][:], lhsT=wg[b][:, c, 256 + fcl * 128:256 + (fcl + 1) * 128],
                                    rhs=hT[:, c, bl * 512:(bl + 1) * 512], start=(c == 0), stop=(c == 7)),
                                    reads=["wg%d" % b, "hT%d_%d" % (c, bl)], writes=[pk(bu)])
                            k = (fcl * 2 + bl) % 2
                            P.add("act", lambda e, k=k, bg=bg: e.activation(sg[k], ps[bg][:], AF.Silu),
                                  writes=[pk(bg), "sg%d" % k])
                            P.add("dve", lambda e, k=k, bu=bu, fc=fc, bl=bl: e.tensor_tensor(
                                actT[:, fc, bl * 512:(bl + 1) * 512], sg[k], ps[bu][:], ALU.mult),
                                reads=["sg%d" % k], writes=[pk(bu), "actT%d_%d" % (fc, bl)])
                            drain(1)
                assert not pending
                hoist = (lambda d: norm_site(l, s, 1, alt=True, defer=d)) if half == 0 else tail
                dpend = []
                ssb = [6, 7]
                pend = None

                def emit_ss(sq, sqk, dc, bl):
                    P.add("pe", lambda e: e.matmul(
                        ps[6 + bl][:], lhsT=cbm(CB_M1024), rhs=sq, start=(dc == 0), stop=(dc == 7)),
                        reads=[sqk, "cb"], writes=[pk(6 + bl)])

                for dc in range(8):
                    k = dc % 2
                    P.add("pool", lambda e, k=k, dc=dc: e.dma_start(out=wd[k], in_=wdn[:, :, dc * 128:(dc + 1) * 128]),
                          writes=["wd%d" % k], dma_slot="wd%d" % k)
                    for bl in range(2):
                        b = nbank(0, 6)
                        for f in range(NF):
                            P.add("pe", lambda e, k=k, b=b, f=f, bl=bl: e.matmul(
                                ps[b][:], lhsT=wd[k][:, f, :], rhs=actT[:, f, bl * 512:(bl + 1) * 512],
                                start=(f == 0), stop=(f == NF - 1)),
                                reads=["wd%d" % k, "actT%d_%d" % (f, bl)], writes=[pk(b)])
                        P.add("act", lambda e, b=b, dc=dc, bl=bl: e.activation(
                            yT[:, dc, bl * 512:(bl + 1) * 512], ps[b][:], AF.Copy),
                            writes=[pk(b), "yT%d_%d" % (dc, bl)])
                        sq = NT_SQ[(dc * 2 + bl) % 2]
                        sqk = "ntsq%d" % ((dc * 2 + bl) % 2)
                        P.add("dve", lambda e, sq=sq, dc=dc, bl=bl: e.tensor_tensor(
                            sq, yT[:, dc, bl * 512:(bl + 1) * 512], yT[:, dc, bl * 512:(bl + 1) * 512], ALU.mult),
                            reads=["yT%d_%d" % (dc, bl)], writes=[sqk])
                        if pend is not None:
                            emit_ss(*pend)
                        pend = (sq, sqk, dc, bl)
                    if dc == 0 and hoist is not None:
                        hoist(dpend)
                    for _ in range(3):
                        if dpend:
                            dpend.pop(0)()
                    if down_hook is not None:
                        down_hook(half, dc)
                emit_ss(*pend)
                while dpend:
                    dpend.pop(0)()
                pending.extend(post_closures(l, s, yT, 2, half * 1024, ssb, lambda c, bl: "yT%d_%d" % (c, bl)))

        def mixer(l, first_norm_done=False, pre_drain=False):
            s = 1
            winv = win_d[l].rearrange("(c p) f -> p c f", p=128)
            mixC = abf(26, [128, 4, T])
            gbT = abf(42, [128, 2, T])
            gcT = af32(50, [128, 2, T])
            uT = af32(66, [128, 2, T + 2])
            fT = abf(83, [128, 16, 256])
            aT = [[abf(114 + (cs * 2 + ch), [128, 512]) for ch in range(2)] for cs in range(2)]
            cmk = abf(95, [128, 2, T])
            cA = af32(103, [128, 512])
            cB = af32(105, [128, 512])

            def load_win(tile_idx):
                b = next_wg()
                P.add("pool", lambda e, b=b: e.dma_start(out=wg[b][:], in_=winv[:, :, tile_idx * 512:(tile_idx + 1) * 512]),
                      writes=["wg%d" % b], dma_slot="wg%d" % b)
                return b

            def proj_fm(b, col0, bl, bank):
                for c in range(8):
                    P.add("pe", lambda e, c=c: e.matmul(ps[bank][:], lhsT=wg[b][:, c, col0:col0 + 128],
                                                        rhs=hT[:, c, bl * 512:(bl + 1) * 512],
                                                        start=(c == 0), stop=(c == 7)),
                          reads=["wg%d" % b, "hT%d_%d" % (c, bl)], writes=[pk(bank)])

            def proj_tm(b, col0, ncol, ttl, bank):
                for c in range(8):
                    P.add("pe", lambda e, c=c: e.matmul(ps[bank][:, 0:ncol], lhsT=hT[:, c, ttl * 128:(ttl + 1) * 128],
                                                        rhs=wg[b][:, c, col0:col0 + ncol],
                                                        start=(c == 0), stop=(c == 7)),
                          reads=["wg%d" % b, "hT%d_%d" % (c, ttl // 4)], writes=[pk(bank)])

            def t3_proj(half, b3=None):
                if b3 is None:
                    b3 = load_win(3)
                for ch4 in range(4):
                    for bl in range(2):
                        t0 = half * 1024 + bl * 512
                        bank = nbank()
                        proj_fm(b3, ch4 * 128, bl, bank)
                        if ch4 < 2:
                            P.add("act", lambda e, bank=bank, ch4=ch4, t0=t0: e.activation(
                                gbT[:, ch4, t0:t0 + 512], ps[bank][:], AF.Copy),
                                writes=[pk(bank), "gbT%d_%d" % (ch4, t0 // 512)])
                        else:
                            P.add("act", lambda e, bank=bank, ch4=ch4, t0=t0: e.activation(
                                gcT[:, ch4 - 2, t0:t0 + 512], ps[bank][:], AF.Copy),
                                writes=[pk(bank), "gcT%d_%d" % (ch4 - 2, t0 // 512)])
                        drain(3)

            pre_b4 = None
            if pre_drain:
                pre_b3 = load_win(3)
                pre_b4 = load_win(4)
                P.barrier()
                t3_proj(0, pre_b3)
                flush()
                P.barrier()
            P.add("sp", lambda e: e.dma_start(out=cmk, in_=cmask_d.rearrange("p (a b) -> p a b", a=2)),
                  writes=["cmk"], dma_slot="cmk")
            for cc in range(2):
                P.add("dve", lambda e, cc=cc: e.memset(uT[:, cc, 0:1], 0.0), writes=["uTpad"])
                P.add("dve", lambda e, cc=cc: e.memset(uT[:, cc, T + 1:T + 2], 0.0), writes=["uTpad"])
            for half in range(2):
                if not (half == 0 and first_norm_done):
                    norm_site(l, s, half)
                if not (half == 0 and pre_drain):
                    t3_proj(half)
                b4 = pre_b4 if (half == 0 and pre_b4 is not None) else load_win(4)
                for cc in range(2):
                    for bl in range(2):
                        t0 = half * 1024 + bl * 512
                        bank = nbank()
                        proj_fm(b4, cc * 128, bl, bank)
                        P.add("dve", lambda e, bank=bank, cc=cc, t0=t0: e.tensor_tensor(
                            uT[:, cc, 1 + t0:1 + t0 + 512], ps[bank][:], gcT[:, cc, t0:t0 + 512], ALU.mult),
                            reads=["gcT%d_%d" % (cc, t0 // 512)], writes=[pk(bank), "uT%d_%d" % (cc, t0 // 512)])
                for ttl in range(8):
                    tt = half * 8 + ttl
                    bank = nbank()
                    proj_tm(b4, 256, 256, ttl, bank)
                    P.add("act", lambda e, bank=bank, tt=tt: e.activation(fT[:, tt, :], ps[bank][:, 0:256], AF.Copy),
                          writes=[pk(bank), "fT%d" % tt])
            cw = lambda cc, k: sm[:, SM_L + l * SM_LSZ + 120 + cc * 3 + k: SM_L + l * SM_LSZ + 120 + cc * 3 + k + 1]
            for cc in range(2):
                for blk in range(4):
                    t0 = blk * 512
                    ukeys = ["uT%d_%d" % (cc, g) for g in range(max(0, blk - 1), min(3, blk + 1) + 1)] + ["uTpad"]
                    P.add("dve", lambda e, cc=cc, t0=t0: e.tensor_scalar(
                        cA, uT[:, cc, 1 + t0:1 + t0 + 512], cw(cc, 1), None, ALU.mult),
                        reads=ukeys + ["sm"], writes=["cA"])
                    P.add("dve", lambda e, cc=cc, t0=t0: e.tensor_tensor(
                        cB, uT[:, cc, t0:t0 + 512], cmk[:, 0, t0:t0 + 512], ALU.mult),
                        reads=ukeys + ["cmk"], writes=["cB"])
                    P.add("dve", lambda e, cc=cc: e.scalar_tensor_tensor(
                        cA, cB, cw(cc, 0), cA, ALU.mult, ALU.add), reads=["cB", "sm"], writes=["cA"])
                    P.add("dve", lambda e, cc=cc, t0=t0: e.tensor_tensor(
                        cB, uT[:, cc, 2 + t0:2 + t0 + 512], cmk[:, 1, t0:t0 + 512], ALU.mult),
                        reads=ukeys + ["cmk"], writes=["cB"])
                    P.add("dve", lambda e, cc=cc: e.scalar_tensor_tensor(
                        cA, cB, cw(cc, 2), cA, ALU.mult, ALU.add), reads=["cB", "sm"], writes=["cA"])
                    P.add("dve", lambda e, cc=cc, t0=t0: e.tensor_tensor(
                        mixC[:, cc, t0:t0 + 512], cA, gbT[:, cc, t0:t0 + 512], ALU.mult),
                        reads=["cA", "gbT%d_%d" % (cc, blk)], writes=["mixC%d_%d" % (cc, blk)])
            kT = abf(42, [128, 4, T + PAST])
            V = abf(78, [128, NKT, 512])
            rope = af32(98, [128, 2, T])
            ckst = af32(62, [128, 4, 512])
            convdone = ["mixC0_3", "mixC1_3"]
            dv = [dft_d[cs].rearrange("(c p) n -> p c n", p=128) for cs in range(2)]
            for nb in range(4):
                if nb == 2:
                    norm_site(l, s, 0)
                for cs in range(2):
                    banks = [nbank(), nbank()]
                    for hf in range(2):
                        b = next_wg()
                        P.add("sp", lambda e, b=b, cs=cs, hf=hf, nb=nb: e.dma_start(
                            out=wg[b][:], in_=dv[cs][:, hf * 8:(hf + 1) * 8, nb * 512:(nb + 1) * 512]),
                            writes=["wg%d" % b], dma_slot="wgsp%d" % b)
                        for ch in range(2):
                            for c in range(8):
                                P.add("pe", lambda e, b=b, ch=ch, c=c, hf=hf, bk=banks[ch]: e.matmul(
                                    ps[bk][:], lhsT=fT[:, hf * 8 + c, ch * 128:(ch + 1) * 128], rhs=wg[b][:, c, :],
                                    start=(hf == 0 and c == 0), stop=(hf == 1 and c == 7)),
                                    reads=["wg%d" % b, "fT%d" % (hf * 8 + c)], writes=[pk(banks[ch])])
                    for ch in range(2):
                        P.add("act", lambda e, cs=cs, ch=ch, bk=banks[ch]: e.activation(aT[cs][ch], ps[bk][:], AF.Copy),
                              writes=[pk(banks[ch]), "aT%d_%d" % (cs, ch)])
                for ch in range(2):
                    bank = nbank()
                    P.add("pe", lambda e, ch=ch, bank=bank: e.matmul(ps[bank][:], lhsT=cbm(CB_CC), rhs=aT[0][ch],
                                                                     start=True, stop=False),
                          reads=["cb", "aT0_%d" % ch], writes=[pk(bank)])
                    P.add("pe", lambda e, ch=ch, bank=bank: e.matmul(ps[bank][:], lhsT=cbm(CB_CS), rhs=aT[1][ch],
                                                                     start=False, stop=True),
                          reads=["cb", "aT1_%d" % ch], writes=[pk(bank)])
                    P.add("act", lambda e, ch=ch, bank=bank, nb=nb: e.activation(
                        mixC[:, 2 + ch, nb * 512:(nb + 1) * 512], ps[bank][:], AF.Copy),
                        writes=[pk(bank), "mixC%d_%d" % (2 + ch, nb)])
            P.add("sp", lambda e: e.dma_start(out=rope, in_=rope_d.rearrange("p (a b) -> p a b", a=2)),
                  reads=convdone, writes=["rope"], dma_slot="rope")
            P.add("pool", lambda e: e.dma_start(out=V[:, 16:20, :], in_=cvv_d[l].rearrange("(t p) e -> p t e", p=128)),
                  reads=convdone, writes=["Vctx"], dma_slot="Vctx")
            P.add("sp", lambda e: e.dma_start(out=ckst, in_=ck_d[l].rearrange("(t p) e -> p t e", p=128)),
                  reads=convdone, writes=["ckst"], dma_slot="ckst")
            pre_qk = [load_win(0), load_win(1)]
            for h in range(4):
                bank = nbank()
                for pt in range(4):
                    P.add("pe", lambda e, h=h, pt=pt, bank=bank: e.transpose(
                        ps[bank][:, pt * 128:(pt + 1) * 128], ckst[:, pt, h * 128:(h + 1) * 128], idf[:]),
                        reads=["ckst", "idf"], writes=[pk(bank)])
                P.add("act", lambda e, h=h, bank=bank: e.activation(kT[:, h, T:T + PAST], ps[bank][:], AF.Copy),
                      reads=convdone, writes=[pk(bank), "kTctx%d" % h])
            P.barrier()

            qT = abf(62, [128, 4, T])
            qb16 = [abf(114 + k, [128, 512]) for k in range(2)]
            stg = [af32(116, [128, 512]), af32(118, [128, 512])]
            stgi = {"i": 0}
            for half in range(2):
                if half == 1:
                    norm_site(l, s, half)
                for which in range(2):
                    bq = pre_qk[which] if half == 0 else load_win(which)
                    dst = qT if which == 0 else kT
                    prev = None
                    for h in range(4):
                        for bl in range(2):
                            t0 = half * 1024 + bl * 512
                            bank = nbank()
                            proj_fm(bq, h * 128, bl, bank)
                            qb = qb16[(h * 2 + bl) % 2]
                            qk_ = "qb16_%d" % ((h * 2 + bl) % 2)
                            P.add("act", lambda e, qb=qb, bank=bank: e.activation(qb, ps[bank][:], AF.Copy),
                                  writes=[pk(bank), qk_])

                            def rope_part(qb=qb, qk_=qk_, bank=bank, t0=t0, h=h, dst=dst, which=which):
                                bank2 = nbank()
                                P.add("pe", lambda e, qb=qb, bank2=bank2: e.matmul(ps[bank2][:], lhsT=cbm(CB_PERM), rhs=qb,
                                                                                   start=True, stop=True),
                                      reads=["cb", qk_], writes=[pk(bank2)])
                                t1 = NT_T[0]
                                t2 = NT_T[1]
                                P.add("dve", lambda e, bank=bank, t0=t0: e.tensor_tensor(
                                    t1, ps[bank][:], rope[:, 0, t0:t0 + 512], ALU.mult),
                                    reads=["rope"], writes=[pk(bank), "ntt0"])
                                P.add("dve", lambda e, bank2=bank2, t0=t0: e.tensor_tensor(
                                    t2, ps[bank2][:], rope[:, 1, t0:t0 + 512], ALU.mult),
                                    reads=["rope"], writes=[pk(bank2), "ntt1"])
                                P.add("dve", lambda e, dst=dst, h=h, t0=t0: e.tensor_tensor(
                                    dst[:, h, t0:t0 + 512], t1, t2, ALU.add),
                                    reads=["ntt0", "ntt1"],
                                    writes=["%s%d_%d" % ("qT" if which == 0 else "kT", h, t0 // 512)])

                            if prev is not None:
                                prev()
                            prev = rope_part
                    prev()
                    if which == 1:
                        for ttl in range(8):
                            tt = half * 8 + ttl
                            bank = nbank()
                            proj_tm(bq, 0, 512, ttl, bank)
                            si = stgi["i"] % 2
                            stgi["i"] += 1
                            P.add("act", lambda e, bank=bank, si=si: e.activation(stg[si], ps[bank][:], AF.Copy),
                                  writes=[pk(bank), "stg%d" % si])
                            P.add("sp", lambda e, tt=tt, si=si: e.dma_start(out=nk_d[l, tt * 128:(tt + 1) * 128, :], in_=stg[si]),
                                  reads=["stg%d" % si], dma_slot="ostg%d" % si, final=True)
                bv = load_win(2)
                for ttl in range(8):
                    tt = half * 8 + ttl
                    bank = nbank()
                    proj_tm(bv, 0, 512, ttl, bank)
                    si = stgi["i"] % 2
                    stgi["i"] += 1
                    P.add("act", lambda e, bank=bank, si=si: e.activation(stg[si], ps[bank][:], AF.Copy),
                          writes=[pk(bank), "stg%d" % si])
                    P.add("dve", lambda e, tt=tt, si=si: e.tensor_copy(V[:, tt, :], stg[si]),
                          reads=["stg%d" % si], writes=["V%d" % tt])
                    P.add("sp", lambda e, tt=tt, si=si: e.dma_start(out=nv_d[l, tt * 128:(tt + 1) * 128, :], in_=stg[si]),
                          reads=["stg%d" % si], dma_slot="ostg%d" % si, final=True)
            P.barrier()

            mixA = abf(0, [128, 4, T])
            NPT = 6
            pTall = abf(98, [128, NPT, 512])
            pT = [pTall[:, k, :] for k in range(NPT)]
            rcp = [af32(104 + 2 * k, [128, 512]) for k in range(2)]
            tO = [af32(108 + 2 * k, [128, 512]) for k in range(2)]
            oF = af32(112, [128, 512])
            osq = abf(114, [128, 512])
            accD = [af32(115, [128, 512]), af32(117, [128, 512])]
            negLam = drv[:, 80 * l + 72: 80 * l + 73]
            sgl = drv[:, 80 * l + 73: 80 * l + 74]
            items = [(h, qb_, kt) for h in range(4) for qb_ in range(4) for kt in range(NKT)]
            state = {}
            acc_o = [5, 6]
            SPARE = 4

            def stage_A(j):
                h, qb_, kt = items[j]
                q0 = qb_ * 512
                diag = (kt < 16 and kt // 4 == qb_)
                banks = [(2 * j) % 4, (2 * j + 1) % 4]
                kkey = ("kT%d_%d" % (h, kt // 4)) if kt < 16 else ("kTctx%d" % h)
                for m in range(2):
                    bank = banks[m]
                    P.add("pe", lambda e, bank=bank, h=h, kt=kt, m=m, q0=q0: e.matmul(
                        ps[bank][:], lhsT=kT[m * 64:(m + 1) * 64, h, kt * 128:(kt + 1) * 128],
                        rhs=qT[m * 64:(m + 1) * 64, h, q0:q0 + 512], start=True, stop=True),
                        reads=[kkey, "qT%d_%d" % (h, qb_)], writes=[pk(bank)])
                b0 = banks[0]
                pi0 = (2 * j) % NPT
                if diag:
                    for qh in range(2):
                        mcol = SM_MASK + kt * 8 + 4 + qh
                        P.add("act", lambda e, b0=b0, pi0=pi0, mcol=mcol, qh=qh: e.activation(
                            pTall[:, pi0:pi0 + 2, qh * 256:(qh + 1) * 256], psall[:, b0:b0 + 2, qh * 256:(qh + 1) * 256],
                            AF.Exp, bias=sm[:, mcol:mcol + 1], scale=0.125),
                            reads=["sm"], writes=[pk(b0), pk(b0 + 1), "pT%d" % pi0, "pT%d" % (pi0 + 1)])
                else:
                    mcol = SM_MASK + kt * 8 + qb_
                    P.add("act", lambda e, b0=b0, pi0=pi0, mcol=mcol: e.activation(
                        pTall[:, pi0:pi0 + 2, :], psall[:, b0:b0 + 2, :], AF.Exp, bias=sm[:, mcol:mcol + 1], scale=0.125),
                        reads=["sm"], writes=[pk(b0), pk(b0 + 1), "pT%d" % pi0, "pT%d" % (pi0 + 1)])
                state[j] = pi0

            def stage_C(j):
                h, qb_, kt = items[j]
                par = (h * 4 + qb_) % 2
                pi0 = state.pop(j)
                vkey = ("V%d" % kt) if kt < 16 else "Vctx"
                for m in range(2):
                    pi = pi0 + m
                    P.add("pe", lambda e, pi=pi, m=m, kt=kt, h=h: e.matmul(
                        ps[acc_o[m]][:], lhsT=V[:, kt, h * 128:(h + 1) * 128], rhs=pT[pi],
                        start=(kt == 0), stop=(kt == NKT - 1)),
                        reads=[vkey, "pT%d" % pi], writes=[pk(acc_o[m])])
                P.add("pe", lambda e, pi0=pi0, kt=kt: e.matmul(
                    ps[7][:], lhsT=cbm(CB_ONE), rhs=pT[pi0], start=(kt == 0), stop=(kt == NKT - 1)),
                    reads=["cb", "pT%d" % pi0], writes=[pk(7)])
                ad = accD[par]
                if kt == 0:
                    P.add("dve", lambda e, ad=ad, pi0=pi0: e.tensor_copy(ad, pT[pi0 + 1]),
                          reads=["pT%d" % (pi0 + 1)], writes=["accD%d" % par])
                else:
                    P.add("dve", lambda e, ad=ad, pi0=pi0: e.tensor_tensor(ad, ad, pT[pi0 + 1], ALU.add),
                          reads=["pT%d" % (pi0 + 1)], writes=["accD%d" % par])
                if kt == NKT - 1:
                    P.add("dve", lambda e: e.tensor_copy(rcp[0], ps[7][:]), writes=[pk(7), "rcp0"])
                    P.add("dve", lambda e: e.tensor_copy(tO[0], ps[acc_o[0]][:]), writes=[pk(acc_o[0]), "tO0"])
                    P.add("dve", lambda e: e.tensor_copy(tO[1], ps[acc_o[1]][:]), writes=[pk(acc_o[1]), "tO1"])
                    P.add("dve", lambda e: e.reciprocal(rcp[0], rcp[0]), writes=["rcp0"])
                    state["epi"] = (h, qb_, par)

            def epi_den1():
                h, qb_, par = state["epi"]
                P.add("pe", lambda e, par=par: e.matmul(ps[SPARE][:], lhsT=onesf[:], rhs=accD[par], start=True, stop=True),
                      reads=["onesf", "accD%d" % par], writes=[pk(SPARE)])
                P.add("act", lambda e: e.activation(rcp[1], ps[SPARE][:], AF.Ln), writes=[pk(SPARE), "rcp1"])
                P.add("act", lambda e: e.activation(rcp[1], rcp[1], AF.Exp, scale=-1.0), writes=["rcp1"])

            def epi_comb():
                for m in range(2):
                    P.add("dve", lambda e, m=m: e.tensor_tensor(tO[m], tO[m], rcp[m], ALU.mult),
                          reads=["rcp%d" % m], writes=["tO%d" % m])
                P.add("dve", lambda e: e.scalar_tensor_tensor(oF, tO[1], negLam, tO[0], ALU.mult, ALU.add),
                      reads=["tO0", "tO1", "drv"], writes=["oF"])
                P.add("dve", lambda e: e.tensor_tensor(osq, oF, oF, ALU.mult), reads=["oF"], writes=["osq"])

            def epi_final():
                h, qb_, par = state.pop("epi")
                q0 = qb_ * 512
                P.add("pe", lambda e: e.matmul(ps[SPARE][:], lhsT=cbm(CB_M128), rhs=osq, start=True, stop=True),
                      reads=["cb", "osq"], writes=[pk(SPARE)])
                P.add("act", lambda e: e.activation(NT_L, ps[SPARE][:], AF.Ln, bias=epsc[:, 0:1]),
                      reads=["epsc"], writes=[pk(SPARE), "ntl"])
                P.add("act", lambda e: e.activation(NT_R, NT_L, AF.Exp, scale=-0.5), reads=["ntl"], writes=["ntr"])
                P.add("dve", lambda e, h=h, q0=q0: e.scalar_tensor_tensor(
                    mixA[:, h, q0:q0 + 512], oF, sgl, NT_R, ALU.mult, ALU.mult),
                    reads=["oF", "ntr", "drv"], writes=["mixA%d_%d" % (h, qb_)])

            NI = len(items)
            LA = 2
            modt = {"i": 0}
            j0 = None
            sched = {2: epi_den1, 5: epi_comb, 10: epi_final}
            for j in range(NI + LA):
                if j < NI:
                    stage_A(j)
                if j - LA >= 0:
                    stage_C(j - LA)
                    if items[j - LA][2] == NKT - 1:
                        j0 = j
                if j0 is not None and (j - j0) in sched:
                    sched[j - j0]()
                    if j - j0 == 10:
                        j0 = None
                if l == 0 and j % 17 == 8 and modt["i"] < 18:
                    mod_tile_evac(1, modt["i"], SPARE)
                    modt["i"] += 1
            if l == 0:
                while modt["i"] < 18:
                    mod_tile_evac(1, modt["i"], SPARE)
                    modt["i"] += 1
            wv = wout_d[l].rearrange("(c p) d -> p c d", p=128)
            bo = []
            for hf in range(2):
                b = next_wg()
                bo.append(b)
                P.add("pool", lambda e, b=b, hf=hf: e.dma_start(out=wg[b][:], in_=wv[:, :, hf * 512:(hf + 1) * 512]),
                      writes=["wg%d" % b], dma_slot="wg%d" % b)
            if j0 is not None:
                for d in (2, 5, 10):
                    if d > (NI + LA - 1 - j0):
                        sched[d]()
            if l == 0:
                mod_finish(1, evacuated=True)
            P.barrier()

            yTs = [af32(42, [128, 8, 512]), af32(58, [128, 8, 512])]
            def emit_ssb(sq, sqk, dc, sbank):
                P.add("pe", lambda e: e.matmul(ps[sbank][:], lhsT=cbm(CB_M1024), rhs=sq,
                                               start=(dc == 0), stop=(dc == 7)),
                      reads=[sqk, "cb"], writes=[pk(sbank)])

            for blk in range(4):
                t0 = blk * 512
                par = blk % 2
                yT = yTs[par]
                sbank = 7 - par
                pend = None
                for dc in range(8):
                    b = bo[dc // 4]
                    bank = nbank(0, 6)
                    for cc in range(8):
                        if cc < 4:
                            rhs = mixA[:, cc, t0:t0 + 512]
                            rk = "mixA%d_%d" % (cc, blk)
                        else:
                            rhs = mixC[:, cc - 4, t0:t0 + 512]
                            rk = "mixC%d_%d" % (cc - 4, blk)
                        P.add("pe", lambda e, b=b, bank=bank, cc=cc, dc=dc, rhs=rhs: e.matmul(
                            ps[bank][:], lhsT=wg[b][:, cc, (dc % 4) * 128:(dc % 4 + 1) * 128], rhs=rhs,
                            start=(cc == 0), stop=(cc == 7)),
                            reads=["wg%d" % b, rk], writes=[pk(bank)])
                    P.add("act", lambda e, bank=bank, dc=dc, yT=yT: e.activation(yT[:, dc, :], ps[bank][:], AF.Copy),
                          writes=[pk(bank), "yT%d_%d" % (dc, par)])
                    sq = NT_SQ[dc % 2]
                    sqk = "ntsq%d" % (dc % 2)
                    P.add("dve", lambda e, sq=sq, dc=dc, yT=yT: e.tensor_tensor(sq, yT[:, dc, :], yT[:, dc, :], ALU.mult),
                          reads=["yT%d_%d" % (dc, par)], writes=[sqk])
                    if pend is not None:
                        emit_ssb(*pend)
                    pend = (sq, sqk, dc, sbank)
                    drain(2)
                emit_ssb(*pend)
                assert not pending
                pending.extend(post_closures(l, s, yT, 1, t0, [sbank], lambda c, bl, par=par: "yT%d_%d" % (c, par)))
            pre_ffn2["b"] = ffn_first_tile(l, 1)
            flush()
            P.barrier()

        pre_ffn2 = {"b": None}
        for l in range(DEPTH):
            ffn(l, 0, first_norm_done=(l > 0), tail=(lambda d, l=l: norm_site(l, 1, 0, alt=True, defer=d)),
                down_hook=(mod0_down_hook if l == 0 else None), pre_tile=(pre_ffn1 if l == 0 else None))
            mixer(l, first_norm_done=True, pre_drain=True)
            if l + 1 < DEPTH:
                ffn(l, 1, tail=(lambda d, l=l: norm_site(l + 1, 0, 0, alt=True, defer=d)), pre_tile=pre_ffn2["b"])
            else:
                ffn(l, 1, pre_tile=pre_ffn2["b"])

        ost = [af32(0 + 4 * i, [128, D]) for i in range(2)]
        for tt in range(16):
            if tt == 8:
                flush()
            st = ost[tt % 2]
            sk = "ost%d" % (tt % 2)
            for hf in range(2):
                b = nbank()
                for c4 in range(4):
                    c = hf * 4 + c4
                    P.add("pe", lambda e, b=b, c=c, c4=c4, tt=tt: e.transpose(
                        ps[b][:, c4 * 128:(c4 + 1) * 128], xT[:, c, tt * 128:(tt + 1) * 128], idf[:]),
                        reads=["xT%d_%d" % (c, tt // 4), "idf"], writes=[pk(b)])
                if hf == 0:
                    P.add("act", lambda e, b=b, st=st: e.activation(st[:, 0:512], ps[b][:], AF.Copy),
                          writes=[pk(b), sk + "a"])
                else:
                    P.add("dve", lambda e, b=b, st=st: e.tensor_copy(st[:, 512:1024], ps[b][:]),
                          writes=[pk(b), sk + "b"])
            P.add("sp", lambda e, st=st, tt=tt: e.dma_start(out=y_d[tt * 128:(tt + 1) * 128, :], in_=st),
                  reads=[sk + "a", sk + "b"], writes=[sk + "a", sk + "b"], dma_slot=sk, final=True)
            if tt < 8:
                drain(3)
        flush()
        P.emit(nc, ctx)
    return nc


def _const_tables():
    bf = ml_dtypes.bfloat16
    cbt = np.zeros((128, 6, 128), np.float32)
    cbt[:, CB_M1024] = 1.0 / 1024.0
    cbt[:, CB_M128] = 1.0 / 128.0
    cbt[:, CB_ONE] = 1.0
    perm = np.zeros((128, 128), np.float32)
    for dst in range(128):
        j = dst % 32
        if j < 16:
            perm[dst + 16, dst] = -1.0
        else:
            perm[dst - 16, dst] = 1.0
    cbt[:, CB_PERM] = perm
    c = np.arange(64)
    ang = 2.0 * np.pi * ((c[:, None] * c[None, :]) % 64) / 64.0
    cc = np.zeros((128, 128))
    cs = np.zeros((128, 128))
    for g in range(2):
        cc[g * 64:(g + 1) * 64, g * 64:(g + 1) * 64] = np.cos(ang) / 8.0
        cs[g * 64:(g + 1) * 64, g * 64:(g + 1) * 64] = -np.sin(ang) / 8.0
    cbt[:, CB_CC] = cc
    cbt[:, CB_CS] = cs
    cb = cbt.reshape(128, 6 * 128).astype(bf)

    def dft(n_seq):
        n = np.arange(T)
        pos = n % n_seq
        blk = n // n_seq
        prod = (pos[:, None].astype(np.int64) * pos[None, :].astype(np.int64)) % n_seq
        a = 2.0 * np.pi * prod / n_seq
        same = (blk[:, None] == blk[None, :])
        sc = 1.0 / math.sqrt(n_seq)
        out = np.stack([np.where(same, np.cos(a) * sc, 0.0), np.where(same, np.sin(a) * sc, 0.0)])
        return out.astype(bf)

    p = np.arange(128)
    j = p % 64
    axis = j // 32
    fi = j % 16
    inv = 1.0 / (10000.0 ** (fi.astype(np.float64) / 16.0))
    t = np.arange(T)
    row = (t // 64).astype(np.float64)
    col = (t % 64).astype(np.float64)
    posn = np.where(axis[:, None] == 0, row[None, :], col[None, :])
    ang = posn * inv[:, None]
    rope_s = np.concatenate([np.cos(ang), np.sin(ang)], axis=1).astype(np.float32)
    rope_p = np.concatenate([np.ones((128, T)), np.zeros((128, T))], axis=1).astype(np.float32)

    def cmask(L):
        mL = (t % L != 0).astype(np.float32)
        mR = (t % L != L - 1).astype(np.float32)
        return np.broadcast_to(np.concatenate([mL, mR])[None, :], (128, 2 * T)).astype(bf)

    amask_s = np.zeros((NKT, 8), np.float32)
    amask_p = np.full((NKT, 8), NEG, np.float32)
    for kt in range(16):
        amask_p[kt, kt // 4] = 0.0
        amask_p[kt, 4 + (kt // 2) % 2] = 0.0
    amask_s[:, 4:6] = 0.0
    NEGQ = -240000.0
    mq_s = np.zeros((2, 768), np.float32)
    mq_s[0, 512:640] = 1.0
    mq_s[1, 640:768] = 1.0
    mq_p = mq_s.copy()
    mq_p[0, 256:512] = NEGQ
    mq_p[1, 0:256] = NEGQ
    mq_s = mq_s.astype(bf)
    mq_p = mq_p.astype(bf)
    return dict(cb=cb, dft_s=dft(T), dft_p=dft(256), rope_s=rope_s, rope_p=rope_p,
                cmask_s=cmask(T), cmask_p=cmask(256), amask_s=amask_s, amask_p=amask_p, mq_s=mq_s, mq_p=mq_p)


_CACHE = {}


def kernel(x_prompt, x_sample, cache_k, cache_v, c, c_ctx, w_mod, b_mod, norm_g,
           w_ffn_gu, w_ffn_down, w_in, w_out, conv_w, lam_qk, subln_g):
    f32 = np.float32
    x_prompt = np.asarray(x_prompt, f32)
    x_sample = np.asarray(x_sample, f32)
    cache_k = np.asarray(cache_k, f32)
    cache_v = np.asarray(cache_v, f32)
    c = np.asarray(c, f32)
    c_ctx = np.asarray(c_ctx, f32)
    w_mod = np.ascontiguousarray(np.asarray(w_mod, f32))
    b_mod = np.asarray(b_mod, f32)
    norm_g = np.asarray(norm_g, f32)
    w_ffn_gu = np.ascontiguousarray(np.asarray(w_ffn_gu, f32))
    w_ffn_down = np.ascontiguousarray(np.asarray(w_ffn_down, f32))
    w_in = np.ascontiguousarray(np.asarray(w_in, f32))
    w_out = np.ascontiguousarray(np.asarray(w_out, f32))
    conv_w = np.asarray(conv_w, f32)
    lam_qk = np.asarray(lam_qk, f32)
    subln_g = np.asarray(subln_g, f32)

    if "nc" not in _CACHE:
        _CACHE["nc"] = build_program()
        _CACHE["tab"] = _const_tables()
    nc = _CACHE["nc"]
    tab = _CACHE["tab"]
    idf = np.eye(128, dtype=f32)

    def fm(v):
        return np.ascontiguousarray(v.reshape(8, 128).T)

    in_maps = []
    for core in range(8):
        is_s = core < 4
        sm = np.zeros((128, NSM), f32)
        cvec = c[core] if is_s else c_ctx
        sm[:, SM_CV:SM_CV + 8] = fm(cvec)
        for l in range(DEPTH):
            o = SM_L + l * SM_LSZ
            sm[:, o:o + 72] = b_mod[l].reshape(72, 128).T
            sm[:, o + 72:o + 120] = norm_g[l].reshape(48, 128).T
            sm[:, o + 120:o + 126] = conv_w[l].reshape(3, 2, 128).transpose(2, 1, 0).reshape(128, 6)
            sm[:, o + 126] = subln_g[l]
            sm[0:64, SM_LAM + l * 4:SM_LAM + l * 4 + 4] = lam_qk[l].T
        am = tab["amask_s"] if is_s else tab["amask_p"]
        sm[:, SM_MASK:SM_MASK + 160] = am.reshape(1, 160)
        if is_s:
            xx = x_sample[core]
            ck = cache_k[core].reshape(DEPTH, PAST, 512)
            cvv = cache_v[core].reshape(DEPTH, PAST, 512)
        else:
            xx = x_prompt[(core - 4) * 8:(core - 4) * 8 + 8].reshape(T, D)
            ck = np.zeros((DEPTH, PAST, 512), f32)
            cvv = np.zeros((DEPTH, PAST, 512), f32)
        in_maps.append({
            "x": np.ascontiguousarray(xx), "sm": sm, "cb": tab["cb"], "idf": idf,
            "rope": tab["rope_s"] if is_s else tab["rope_p"],
            "mq": tab["mq_s"] if is_s else tab["mq_p"],
            "cmask": tab["cmask_s"] if is_s else tab["cmask_p"],
            "dft": tab["dft_s"] if is_s else tab["dft_p"],
            "ck": np.ascontiguousarray(ck), "cvv": np.ascontiguousarray(cvv),
            "w_mod": w_mod, "w_ffn_gu": w_ffn_gu, "w_ffn_down": w_ffn_down, "w_in": w_in, "w_out": w_out,
        })
    res = run_bass_kernel_spmd(nc, in_maps, core_ids=list(range(8)))
    r = res.results
    y_sample = np.stack([r[b]["y"] for b in range(4)]).astype(f32)
    y_prompt = np.concatenate([r[4 + i]["y"].reshape(8, 256, D) for i in range(4)]).astype(f32)
    nk = np.concatenate([r[4 + i]["nk"].reshape(DEPTH, 8, 256, 4, 2, 64).transpose(1, 0, 2, 3, 4, 5)
                         for i in range(4)]).astype(f32)
    nv = np.concatenate([r[4 + i]["nv"].reshape(DEPTH, 8, 256, 4, 128).transpose(1, 0, 2, 3, 4)
                         for i in range(4)]).astype(f32)
    return (y_prompt, y_sample, nk, nv)
```
